# Optimizing a Trainium2 kernel written in Bass

```python
import math
import jax
import jax.numpy as jnp
from jax import lax
import numpy as np

D_MODEL = 1024
BATCH = 8
SEQ = 2048
DEPTH = 1
DEC_BATCH = 128
DEC_SEQ = 8
PAST_LEN = 16384
PAGE_SIZE = 128

S5_WIDTH = D_MODEL // 2
S5_GROUP = 16
S5_GROUPS = S5_WIDTH // S5_GROUP
S5_STATE = 64
GM_WIDTH = D_MODEL // 2
GM_HEADS = 8
GM_HEAD_DIM = GM_WIDTH // GM_HEADS
CHUNK = 128
D_FF = ((8 * D_MODEL // 3 + 127) // 128) * 128
CONV_W = 3
IN_COLS = S5_WIDTH + 2 * GM_WIDTH + 2 * D_MODEL
EPS = 1e-6

kernel_name = 'hybrid_s5_gmlp_convffn_adaln_step'


def rms_norm(x, g):
    xf = x.astype(jnp.float32)
    y = xf * lax.rsqrt(jnp.mean(xf * xf, axis=-1, keepdims=True) + EPS)
    return (y * g.astype(jnp.float32)).astype(x.dtype)


def layer_norm(x, g, b):
    xf = x.astype(jnp.float32)
    xc = xf - jnp.mean(xf, axis=-1, keepdims=True)
    y = xc * lax.rsqrt(jnp.mean(xc * xc, axis=-1, keepdims=True) + EPS)
    return (y * g.astype(jnp.float32) + b.astype(jnp.float32)).astype(x.dtype)


def s5_scan(u, h0_re, h0_im, a_re, a_im, log_dt, b_re, b_im, c_re, c_im, d_skip):
    n, t, _ = u.shape
    f32 = jnp.float32
    uf = u.astype(f32).reshape(n, t, S5_GROUPS, S5_GROUP)
    lam = lax.complex(a_re.astype(f32), a_im.astype(f32))
    dt = jnp.exp(log_dt.astype(f32))[:, None]
    a_bar = jnp.exp(lam * dt)
    b_bar = ((a_bar - 1.0) / lam)[..., None] * lax.complex(b_re.astype(f32), b_im.astype(f32))
    bu = jnp.einsum('gph,ntgh->ntgp', b_bar, uf.astype(jnp.complex64))
    h0 = lax.complex(h0_re.astype(f32), h0_im.astype(f32))
    bu = bu.at[:, 0].add(a_bar * h0)
    a_seq = jnp.broadcast_to(a_bar, bu.shape)

    def combine(left, right):
        a_l, b_l = left
        a_r, b_r = right
        return a_l * a_r, a_r * b_l + b_r

    _, h = lax.associative_scan(combine, (a_seq, bu), axis=1)
    c_mat = lax.complex(c_re.astype(f32), c_im.astype(f32))
    y = jnp.real(jnp.einsum('ghp,ntgp->ntgh', c_mat, h)) + d_skip.astype(f32) * uf
    h_last = h[:, -1]
    return y.reshape(n, t, S5_WIDTH), jnp.real(h_last), jnp.imag(h_last)


def gmlp_gate(u, v, ln_g, ln_b, w_sp, b_sp):
    v = layer_norm(v, ln_g, ln_b)
    n, t, _ = v.shape
    length = min(t, CHUNK)
    n_chunks = t // length
    causal = jnp.tril(jnp.ones((length, length), dtype=bool))
    w = jnp.where(causal, w_sp[:, :length, :length], 0)
    vc = v.reshape(n, n_chunks, length, GM_HEADS, GM_HEAD_DIM)
    s = jnp.einsum('hts,ncshd->ncthd', w, vc) + jnp.transpose(b_sp[:, :length])[None, None, :, :, None]
    return u * s.reshape(n, t, GM_WIDTH), v


def conv_ffn(h, past, w_up, w_conv, b_conv, w_down):
    up = h @ w_up
    t = up.shape[1]
    full = jnp.concatenate([past.astype(up.dtype), up], axis=1)
    conv = b_conv + full[:, 0:t] * w_conv[0]
    for k in range(1, CONV_W):
        conv = conv + full[:, k:k + t] * w_conv[k]
    gate, val = jnp.split(conv, 2, axis=-1)
    return (jax.nn.gelu(gate) * val) @ w_down, full[:, t:]


def decoder_layer(x, c, h0_re, h0_im, conv_past, p):
    mod = jax.nn.silu(c) @ p['w_ada'] + p['b_ada']
    shift_m, scale_m, gate_m, shift_f, scale_f, gate_f = jnp.split(mod[:, None, :], 6, axis=-1)
    h = rms_norm(x, p['norm1_g']) * (1 + scale_m) + shift_m
    z = h @ p['w_in']
    o1 = S5_WIDTH
    o2 = o1 + GM_WIDTH
    o3 = o2 + GM_WIDTH
    o4 = o3 + D_MODEL
    s5_u, gm_u, gm_v, g_a, g_b = z[..., :o1], z[..., o1:o2], z[..., o2:o3], z[..., o3:o4], z[..., o4:]
    y5, h_re, h_im = s5_scan(s5_u, h0_re, h0_im, p['s5_a_re'], p['s5_a_im'], p['s5_log_dt'],
                             p['s5_b_re'], p['s5_b_im'], p['s5_c_re'], p['s5_c_im'], p['s5_d'])
    za = jax.nn.gelu(y5.astype(x.dtype)) @ p['w_s5_glu'] + p['b_s5_glu']
    y_a = za[..., :D_MODEL] * jax.nn.sigmoid(za[..., D_MODEL:])
    yg, v_rows = gmlp_gate(jax.nn.gelu(gm_u), jax.nn.gelu(gm_v), p['gm_ln_g'], p['gm_ln_b'],
                           p['gm_w_sp'], p['gm_b_sp'])
    y_b = yg @ p['w_gm_out']
    merged = jax.nn.sigmoid(g_a) * y_a + jax.nn.sigmoid(g_b) * y_b
    x = x + gate_m * (merged @ p['w_out'])
    h2 = rms_norm(x, p['norm2_g']) * (1 + scale_f) + shift_f
    f, conv_new = conv_ffn(h2, conv_past, p['w_up'], p['w_conv'], p['b_conv'], p['w_down'])
    x = x + gate_f * f
    return x, h_re, h_im, conv_new, v_rows


def setup_inputs(seed: int = 0) -> dict:
    key = jax.random.key(seed)
    ks = jax.random.split(key, 40)
    f32 = jnp.float32

    def nrm(k, shape, scale):
        return jax.random.normal(k, shape, f32) * scale

    n_idx = jnp.arange(S5_STATE, dtype=f32)
    return {
        'x_prompt': nrm(ks[0], (BATCH, SEQ, D_MODEL), 1.0),
        'x_sample': nrm(ks[1], (DEC_BATCH, DEC_SEQ, D_MODEL), 1.0),
        'c_prompt': nrm(ks[2], (BATCH, D_MODEL), 1.0),
        'c_sample': nrm(ks[3], (DEC_BATCH, D_MODEL), 1.0),
        'state_ssm_re': nrm(ks[4], (DEPTH, DEC_BATCH, S5_GROUPS, S5_STATE), 0.5),
        'state_ssm_im': nrm(ks[5], (DEPTH, DEC_BATCH, S5_GROUPS, S5_STATE), 0.5),
        'state_ffn_conv': nrm(ks[6], (DEPTH, DEC_BATCH, CONV_W - 1, 2 * D_FF), 1.0),
        'norm1_g': 1.0 + nrm(ks[7], (DEPTH, D_MODEL), 0.02),
        'norm2_g': 1.0 + nrm(ks[8], (DEPTH, D_MODEL), 0.02),
        'w_ada': nrm(ks[9], (DEPTH, D_MODEL, 6 * D_MODEL), 0.3 * D_MODEL ** -0.5),
        'b_ada': nrm(ks[10], (DEPTH, 6 * D_MODEL), 0.02),
        'w_in': nrm(ks[11], (DEPTH, D_MODEL, IN_COLS), D_MODEL ** -0.5),
        's5_a_re': -0.5 + nrm(ks[12], (DEPTH, S5_GROUPS, S5_STATE), 0.01),
        's5_a_im': jnp.pi * n_idx + nrm(ks[13], (DEPTH, S5_GROUPS, S5_STATE), 0.01),
        's5_log_dt': jax.random.uniform(ks[14], (DEPTH, S5_GROUPS), f32, math.log(1e-3), math.log(1e-1)),
        's5_b_re': nrm(ks[15], (DEPTH, S5_GROUPS, S5_STATE, S5_GROUP), (2.0 * S5_GROUP) ** -0.5),
        's5_b_im': nrm(ks[16], (DEPTH, S5_GROUPS, S5_STATE, S5_GROUP), (2.0 * S5_GROUP) ** -0.5),
        's5_c_re': nrm(ks[17], (DEPTH, S5_GROUPS, S5_GROUP, S5_STATE), (2.0 * S5_STATE) ** -0.5),
        's5_c_im': nrm(ks[18], (DEPTH, S5_GROUPS, S5_GROUP, S5_STATE), (2.0 * S5_STATE) ** -0.5),
        's5_d': nrm(ks[19], (DEPTH, S5_GROUPS, S5_GROUP), 0.5),
        'w_s5_glu': nrm(ks[20], (DEPTH, S5_WIDTH, 2 * D_MODEL), S5_WIDTH ** -0.5),
        'b_s5_glu': nrm(ks[21], (DEPTH, 2 * D_MODEL), 0.02),
        'gm_ln_g': 1.0 + nrm(ks[22], (DEPTH, GM_WIDTH), 0.02),
        'gm_ln_b': nrm(ks[23], (DEPTH, GM_WIDTH), 0.02),
        'gm_w_sp': nrm(ks[24], (DEPTH, GM_HEADS, CHUNK, CHUNK), CHUNK ** -0.5),
        'gm_b_sp': 1.0 + nrm(ks[25], (DEPTH, GM_HEADS, CHUNK), 0.02),
        'w_gm_out': nrm(ks[26], (DEPTH, GM_WIDTH, D_MODEL), GM_WIDTH ** -0.5),
        'w_out': nrm(ks[27], (DEPTH, D_MODEL, D_MODEL), D_MODEL ** -0.5),
        'w_up': nrm(ks[28], (DEPTH, D_MODEL, 2 * D_FF), D_MODEL ** -0.5),
        'w_conv': nrm(ks[29], (DEPTH, CONV_W, 2 * D_FF), CONV_W ** -0.5),
        'b_conv': nrm(ks[30], (DEPTH, 2 * D_FF), 0.02),
        'w_down': nrm(ks[31], (DEPTH, D_FF, D_MODEL), D_FF ** -0.5),
        'final_g': 1.0 + nrm(ks[32], (D_MODEL,), 0.02),
    }


def reference(x_prompt, x_sample, c_prompt, c_sample, state_ssm_re, state_ssm_im, state_ffn_conv,
              norm1_g, norm2_g, w_ada, b_ada, w_in, s5_a_re, s5_a_im, s5_log_dt, s5_b_re, s5_b_im,
              s5_c_re, s5_c_im, s5_d, w_s5_glu, b_s5_glu, gm_ln_g, gm_ln_b, gm_w_sp, gm_b_sp,
              w_gm_out, w_out, w_up, w_conv, b_conv, w_down, final_g):
    xp, xs = x_prompt, x_sample
    p_re, p_im, p_conv = [], [], []
    s_re, s_im, s_conv, s_v = [], [], [], []
    n_prompt = xp.shape[0]
    for l in range(DEPTH):
        p = dict(norm1_g=norm1_g[l], norm2_g=norm2_g[l], w_ada=w_ada[l], b_ada=b_ada[l], w_in=w_in[l],
                 s5_a_re=s5_a_re[l], s5_a_im=s5_a_im[l], s5_log_dt=s5_log_dt[l],
                 s5_b_re=s5_b_re[l], s5_b_im=s5_b_im[l], s5_c_re=s5_c_re[l], s5_c_im=s5_c_im[l],
                 s5_d=s5_d[l], w_s5_glu=w_s5_glu[l], b_s5_glu=b_s5_glu[l], gm_ln_g=gm_ln_g[l],
                 gm_ln_b=gm_ln_b[l], gm_w_sp=gm_w_sp[l], gm_b_sp=gm_b_sp[l], w_gm_out=w_gm_out[l],
                 w_out=w_out[l], w_up=w_up[l], w_conv=w_conv[l], b_conv=b_conv[l], w_down=w_down[l])
        zero_state = jnp.zeros((n_prompt, S5_GROUPS, S5_STATE), jnp.float32)
        zero_conv = jnp.zeros((n_prompt, CONV_W - 1, 2 * D_FF), xp.dtype)
        xp, hr, hi, cv, _ = decoder_layer(xp, c_prompt, zero_state, zero_state, zero_conv, p)
        p_re.append(hr)
        p_im.append(hi)
        p_conv.append(cv)
        xs, hr_s, hi_s, cv_s, v_s = decoder_layer(xs, c_sample, state_ssm_re[l], state_ssm_im[l],
                                                  state_ffn_conv[l], p)
        s_re.append(hr_s)
        s_im.append(hi_s)
        s_conv.append(cv_s)
        s_v.append(v_s)
    y_prompt = rms_norm(xp, final_g)
    y_sample = rms_norm(xs, final_g)
    return (y_prompt, y_sample, jnp.stack(p_re), jnp.stack(p_im), jnp.stack(p_conv),
            jnp.stack(s_re), jnp.stack(s_im), jnp.stack(s_conv), jnp.stack(s_v))
```

```python
import contextlib
import math
import numpy as np
import concourse.bass as bass
import concourse.mybir as mybir
from concourse.bass_utils import run_bass_kernel_spmd

F32 = mybir.dt.float32
BF16 = mybir.dt.bfloat16
I32 = mybir.dt.int32
AF = mybir.ActivationFunctionType
ALU = mybir.AluOpType

NCORES = 8
D = 1024
T = 2176
NT = 17
NCH = 272
DFF = 2816
EPS = 1e-6
TBS = [(0, 512), (512, 512), (1024, 512), (1536, 512), (2048, 128)]

EPOCH = 6000
H1 = True
DMA_RING = 12
SAME_ENGINE_SYNC = True


class Prog:
    ENG = ("pe", "act", "dve", "pool", "sp")

    def __init__(self, nc, stack):
        self.nc = nc
        self.stack = stack
        self.streams = {e: [] for e in self.ENG}
        self.count = {e: 0 for e in self.ENG}
        self.sems = {}
        self.waited = {e: {} for e in self.ENG}
        self.last_w = {}
        self.readers = {}
        self.dma_n = {e: 0 for e in self.ENG}
        self.dma_val = {}

    def _sem(self, key):
        if key not in self.sems:
            name = "s_" + "_".join(str(k) for k in key)
            self.sems[key] = self.stack.enter_context(self.nc.semaphore(name))
        return self.sems[key]

    def _add_wait(self, eng, tok, waits):
        key, val = tok
        if key[0] == "eng" and key[1] == eng:
            if eng == "pe" or not SAME_ENGINE_SYNC:
                return
        if self.waited[eng].get(key, 0) >= val:
            return
        self.waited[eng][key] = val
        waits.append((key, val))

    def _deps(self, eng, reads, writes, waits):
        best = {}

        def need(t):
            if t is not None and best.get(t[0], 0) < t[1]:
                best[t[0]] = t[1]
        for k in reads:
            need(self.last_w.get(k))
        for k in writes:
            need(self.last_w.get(k))
            for t in self.readers.get(k, ()):
                need(t)
        for key, val in best.items():
            self._add_wait(eng, (key, val), waits)

    def _record(self, tok, reads, writes):
        for k in reads:
            self.readers.setdefault(k, []).append(tok)
        for k in writes:
            self.last_w[k] = tok
            self.readers[k] = []

    def op(self, eng, fn, reads=(), writes=()):
        waits = []
        self._deps(eng, reads, writes, waits)
        self.count[eng] += 1
        n = self.count[eng]
        key = ("eng", eng, (n - 1) // EPOCH)
        val = (n - 1) % EPOCH + 1
        self._sem(key)
        self.streams[eng].append((waits, fn, key, 1))
        self._record((key, val), reads, writes)

    def dma(self, eng, out, in_, reads=(), writes=(), **kw):
        waits = []
        self._deps(eng, reads, writes, waits)
        slot = self.dma_n[eng] % DMA_RING
        self.dma_n[eng] += 1
        key = ("dma", eng, slot)
        prev = self.dma_val.get(key, 0)
        if prev:
            self._add_wait(eng, (key, prev), waits)
        val = prev + 16
        self.dma_val[key] = val
        self._sem(key)

        kw.setdefault("allow_slow_non_contiguous", True)

        def fn(e, out=out, in_=in_, kw=kw):
            return e.dma_start(out=out, in_=in_, **kw)
        self.streams[eng].append((waits, fn, key, 16))
        self._record((key, val), reads, writes)

    def _all_tokens(self):
        toks = [(k, v) for k, v in self.dma_val.items()]
        for e in self.ENG:
            n = self.count[e]
            if n:
                toks.append((("eng", e, (n - 1) // EPOCH), (n - 1) % EPOCH + 1))
        return toks

    def barrier(self):
        toks = self._all_tokens()
        for e in self.ENG:
            waits = []
            for t in toks:
                if t[0][0] == "eng" and t[0][1] == e:
                    continue
                self._add_wait(e, t, waits)
            if waits:
                self.streams[e].append((waits, None, None, 0))

    def finish(self):
        waits = []
        for t in self._all_tokens():
            self._add_wait("sp", t, waits)
        self.streams["sp"].append((waits, None, None, 0))

    def replay(self):
        nc = self.nc
        sems = self.sems
        streams = self.streams

        def run(e, items):
            for waits, fn, key, inc in items:
                for wkey, wval in waits:
                    e.wait_ge(sems[wkey], wval)
                if fn is not None:
                    fn(e).then_inc(sems[key], inc)

        with nc.Block() as block:
            @block.tensor
            def _(e):
                run(e, streams["pe"])

            @block.scalar
            def _(e):
                run(e, streams["act"])

            @block.vector
            def _(e):
                run(e, streams["dve"])

            @block.gpsimd
            def _(e):
                run(e, streams["pool"])

            @block.sync
            def _(e):
                run(e, streams["sp"])


class Rec:
    def __init__(self):
        self.items = []

    def op(self, eng, fn, reads=(), writes=()):
        self.items.append(("op", eng, fn, list(reads), list(writes)))

    def dma(self, eng, out, in_, reads=(), writes=(), **kw):
        self.items.append(("dma", eng, out, in_, list(reads), list(writes), kw))

    def barrier(self):
        pass

    def gen(self, P, every=1):
        for n, it in enumerate(self.items):
            if it[0] == "op":
                P.op(it[1], it[2], reads=it[3], writes=it[4])
            else:
                P.dma(it[1], it[2], it[3], reads=it[4], writes=it[5], **it[6])
            if n % every == every - 1:
                yield


def build_program(stop_after=None, dumps=()):
    nc = bass.Bass("TRN2", target_bir_lowering=False)

    def din(name, shape):
        return nc.dram_tensor(name, list(shape), F32, kind="ExternalInput").ap()

    def dout(name, shape):
        return nc.dram_tensor(name, list(shape), F32, kind="ExternalOutput").ap()

    x_d = din("x", [T, D])
    c_d = din("c", [17, D])
    sre_d = din("sre", [16, 2048])
    sim_d = din("sim", [16, 2048])
    scv_d = din("scv", [32, 2 * DFF])
    g1_d = din("norm1_g", [D])
    g2_d = din("norm2_g", [D])
    wada_d = din("w_ada", [D, 6 * D])
    bada_d = din("b_ada", [6 * D])
    win_d = din("w_in", [D, 3584])
    are_d = din("s5_a_re", [32, 64])
    aim_d = din("s5_a_im", [32, 64])
    ldt_d = din("s5_log_dt", [32])
    bre_d = din("s5_b_re", [32, 64, 16])
    bim_d = din("s5_b_im", [32, 64, 16])
    cre_d = din("s5_c_re", [32, 16, 64])
    cim_d = din("s5_c_im", [32, 16, 64])
    sd_d = din("s5_d", [32, 16])
    wglu_d = din("w_s5_glu", [512, 2048])
    bglu_d = din("b_s5_glu", [2048])
    lng_d = din("gm_ln_g", [512])
    lnb_d = din("gm_ln_b", [512])
    wsp_d = din("gm_w_sp", [8, 128, 128])
    bsp_d = din("gm_b_sp", [8, 128])
    wgo_d = din("w_gm_out", [512, D])
    wout_d = din("w_out", [D, D])
    wup_d = din("w_up", [D, 2 * DFF])
    wcv_d = din("w_conv", [3, 2 * DFF])
    bcv_d = din("b_conv", [2 * DFF])
    wdn_d = din("w_down", [DFF, D])
    fg_d = din("final_g", [D])

    y_d = dout("y", [T, D])
    pre_d = dout("p_re", [32, 64])
    pim_d = dout("p_im", [32, 64])
    pcv_d = dout("p_conv", [2, 2 * DFF])
    sreo_d = dout("s_re", [16, 2048])
    simo_d = dout("s_im", [16, 2048])
    scvo_d = dout("s_conv", [32, 2 * DFF])
    sv_d = dout("s_v", [128, 512])
    x1_d = nc.dram_tensor("x1_scratch", [T, D], F32, kind="Internal").ap()

    dump_aps = {}

    with contextlib.ExitStack() as st:
        P = Prog(nc, st)

        def mk(stack, name, shape, dt=F32):
            return stack.enter_context(nc.sbuf_tensor(name, list(shape), dt))

        def sb(name, shape, dt=F32):
            return mk(st, name, shape, dt)

        def dump(name, ap, shape, reads, dt=F32):
            if name in dumps:
                d = nc.dram_tensor("dbg_" + name, list(shape), dt, kind="ExternalOutput").ap()
                P.dma("sp", d, ap, reads=reads)

        pb = [st.enter_context(nc.psum_tensor("pb%d" % i, [128, 512], F32)) for i in range(8)]

        def cw(wdr):
            return wdr.rearrange("(k p) n -> p k n", p=128)

        ident = sb("ident", [128, 128])
        io_i = sb("io_i", [128, 128], I32)
        P.op("pool", lambda e: e.iota(io_i[:], [[1, 128]], base=0, channel_multiplier=-1), writes=["io_i"])
        P.op("dve", lambda e: e.tensor_single_scalar(ident[:], io_i[:], 0, ALU.is_equal), reads=["io_i"], writes=["ident"])

        scr = dict(
            LA=nc.dram_tensor("scr_LA", [128, 8192], BF16, kind="Internal").ap(),
            LC=nc.dram_tensor("scr_LC", [128, 8192], BF16, kind="Internal").ap(),
            KC=nc.dram_tensor("scr_KC", [128, 4096], BF16, kind="Internal").ap(),
            tab=nc.dram_tensor("scr_tab", [128, 2, 16, 256], F32, kind="Internal").ap(),
            small=nc.dram_tensor("scr_small", [128, 4, 16], F32, kind="Internal").ap())
        s5dr = dict(are=are_d, aim=aim_d, ldt=ldt_d, bre=bre_d, bim=bim_d, cre=cre_d, cim=cim_d, sd=sd_d,
                    sre=sre_d, sim=sim_d, sreo=sreo_d, simo=simo_d, pre=pre_d, pim=pim_d)
        TH8 = sb("TH8", [128, 16])
        pastT = sb("pastT", [128, 44, 32])
        gf_p = sb("gf_p", [128, D])
        gf_s = sb("gf_s", [128, D])
        wc = sb("wc", [128, 44, 3])
        bcv = sb("bcv", [128, 44])
        modT = sb("modT", [128, 48, 17])
        gs1 = sb("gs1", [128, 8, 17])
        gs2 = sb("gs2", [128, 8, 17])
        g1T = sb("g1T", [128, 8])
        g2T = sb("g2T", [128, 8])
        bT = sb("bT", [128, 48])
        fg_bc = sb("fg_bc", [128, D])
        with contextlib.ExitStack() as ph:
            P_real = P
            P = Rec()
            ct = mk(ph, "ct", [17, D])
            cT = mk(ph, "cT", [128, 8, 17], BF16)
            wad = [mk(ph, "wad%d" % i, [128, 8, 128], BF16) for i in range(4)]
            P.dma("sp", ct[:], c_d, writes=["ct"])
            P.dma("act", bT[:], bada_d.rearrange("(b p) -> p b", p=128), writes=["bT"], allow_slow_non_contiguous=True)
            P.dma("act", g1T[:], g1_d.rearrange("(k p) -> p k", p=128), writes=["g1T"], allow_slow_non_contiguous=True)
            P.dma("act", g2T[:], g2_d.rearrange("(k p) -> p k", p=128), writes=["g2T"], allow_slow_non_contiguous=True)
            P.dma("sp", fg_bc[:], fg_d.rearrange("(o n) -> o n", o=1).to_broadcast([128, D]), writes=["fg_bc"])
            P.op("act", lambda e: e.activation(ct[:], ct[:], AF.Silu), reads=["ct"], writes=["ct"])
            for k in range(8):
                P.op("pe", lambda e, k=k: e.transpose(pb[0][:, k * 32:k * 32 + 17], ct[:, k * 128:(k + 1) * 128], ident[0:17, 0:17]),
                     reads=["ct", "ident"], writes=["pb0"])
            P.op("act", lambda e: e.copy(cT[:], pb[0][:, 0:256].rearrange("p (k c) -> p k c", c=32)[:, :, 0:17]),
                 reads=["pb0"], writes=["cT"])
            for blk in range(48):
                w = wad[blk % 4]
                wk = "wad%d" % (blk % 4)
                P.dma("pool", w[:], cw(wada_d[:, blk * 128:(blk + 1) * 128]), writes=[wk])
                bank = 1 + blk % 2
                for k in range(8):
                    P.op("pe", lambda e, k=k, w=w, bank=bank: e.matmul(pb[bank][:, 0:17], w[:, k, :], cT[:, k, :], start=(k == 0), stop=(k == 7)),
                         reads=[wk, "cT"], writes=["pb%d" % bank])
                P.op("act", lambda e, blk=blk, bank=bank: e.activation(modT[:, blk, :], pb[bank][:, 0:17], AF.Identity, bias=bT[:, blk:blk + 1]),
                     reads=["pb%d" % bank, "bT"], writes=["modT"])
            for k in range(8):
                P.op("dve", lambda e, k=k: e.tensor_scalar(gs1[:, k, :], modT[:, 8 + k, :], 1.0, g1T[:, k:k + 1], ALU.add, ALU.mult),
                     reads=["modT", "g1T"], writes=["gs1"])
                P.op("dve", lambda e, k=k: e.tensor_scalar(gs2[:, k, :], modT[:, 32 + k, :], 1.0, g2T[:, k:k + 1], ALU.add, ALU.mult),
                     reads=["modT", "g2T"], writes=["gs2"])
            rec_mod = P
            P = Rec()
            s5_prep(nc, P, mk, ph, pb, ident, s5dr, scr, TH8)
            rec_prep = P
            P = P_real
            gens = [rec_mod.gen(P), rec_prep.gen(P)]
            while gens:
                for g_ in list(gens):
                    try:
                        next(g_)
                    except StopIteration:
                        gens.remove(g_)
            past_in = mk(ph, "past_in", [32, 2 * DFF])
            P.dma("sp", past_in[:], scv_d, writes=["past_in"])
            for c in range(44):
                bank = c // 16
                P.op("pe", lambda e, c=c, bank=bank: e.transpose(pb[bank][:, (c % 16) * 32:(c % 16 + 1) * 32], past_in[:, c * 128:(c + 1) * 128], ident[0:32, 0:32]),
                     reads=["past_in", "ident"], writes=["pb%d" % bank])
            for bank in range(3):
                n_ = 16 if bank < 2 else 12
                P.op("dve", lambda e, bank=bank, n_=n_: e.tensor_copy(pastT[:, bank * 16:bank * 16 + n_, :], pb[bank][:, 0:n_ * 32].rearrange("p (c m) -> p c m", m=32)),
                     reads=["pb%d" % bank], writes=["pastT"])
            P.barrier()
        dump("modT", modT[:], [128, 48, 17], ["modT"])

        def tm_gate(dst_p, dst_s, blk0, tag):
            for k in range(8):
                P.op("dve", lambda e, k=k: e.tensor_copy(bc_p[:], modT[:, blk0 + k, 0:1].to_broadcast([128, 128])),
                     reads=["modT"], writes=["bc_p"])
                P.op("dve", lambda e, k=k: e.tensor_copy(bc_s[:].rearrange("p (n j) -> p n j", j=8),
                                                         modT[:, blk0 + k, 1:17].unsqueeze(2).to_broadcast([128, 16, 8])),
                     reads=["modT"], writes=["bc_s"])
                P.op("pe", lambda e, k=k: e.matmul(pb[0][:, k * 128:(k + 1) * 128] if k < 4 else pb[1][:, (k - 4) * 128:(k - 3) * 128],
                                                   bc_p[:], ident[:], start=True, stop=True),
                     reads=["bc_p", "ident"], writes=["pb0" if k < 4 else "pb1"])
                P.op("pe", lambda e, k=k: e.matmul(pb[2][:, k * 128:(k + 1) * 128] if k < 4 else pb[3][:, (k - 4) * 128:(k - 3) * 128],
                                                   bc_s[:], ident[:], start=True, stop=True),
                     reads=["bc_s", "ident"], writes=["pb2" if k < 4 else "pb3"])
            for h in range(2):
                P.op("act", lambda e, h=h: e.copy(dst_p[:, h * 512:(h + 1) * 512], pb[h][:]), reads=["pb%d" % h], writes=[tag + "_p"])
                P.op("act", lambda e, h=h: e.copy(dst_s[:, h * 512:(h + 1) * 512], pb[2 + h][:]), reads=["pb%d" % (2 + h)], writes=[tag + "_s"])

        bc_p = sb("bc_p", [128, 128])
        bc_s = sb("bc_s", [128, 128])
        tm_gate(gf_p, gf_s, 40, "gf")

        hT = sb("hT", [128, 8, T], BF16)
        rs_all = sb("rs_all", [128, 4 * NT])

        def run_pipeline(stages, n):
            for _ in pipeline_gen(stages, n):
                pass

        def pipeline_gen(stages, n):
            nst = len(stages)
            for step in range(n + nst - 1):
                gens = []
                for s_, f in enumerate(stages):
                    idx = step - s_
                    if 0 <= idx < n:
                        gens.append(f(idx))
                while gens:
                    for g_ in list(gens):
                        try:
                            next(g_)
                        except StopIteration:
                            gens.remove(g_)
                    yield

        def norm_stages(srcf, dstT, gs, shblk, ssk, tagp):
            def n1(t):
                src, srck = srcf(t)
                ss = rs_all[:, ssk * NT + t: ssk * NT + t + 1]
                sk = tagp + "ss%d" % t
                P.op("act", lambda e: e.activation(junk[:], src, AF.Square, accum_out=ss), reads=[srck], writes=["junk", sk]); yield
                P.op("dve", lambda e: e.tensor_scalar(ss, ss, 1.0 / D, EPS, ALU.mult, ALU.add), reads=[sk], writes=[sk]); yield
                P.op("act", lambda e: e.activation(ss, ss, AF.Sqrt), reads=[sk], writes=[sk]); yield
                P.op("dve", lambda e: e.reciprocal(ss, ss), reads=[sk], writes=[sk]); yield

            def n2(t):
                src, srck = srcf(t)
                ss = rs_all[:, ssk * NT + t: ssk * NT + t + 1]
                sk = tagp + "ss%d" % t
                xn = rings["xn"][t % 2]
                xnk = "xn%d" % (t % 2)
                P.op("pool", lambda e: e.tensor_scalar(xn[:], src, ss, 0.0, ALU.mult, ALU.add), reads=[srck, sk], writes=[xnk]); yield
                b0 = 4 + 2 * (t % 2)
                for k in range(8):
                    bank = b0 + k // 4
                    P.op("pe", lambda e, k=k, bank=bank: e.transpose(pb[bank][:, (k % 4) * 128:(k % 4 + 1) * 128], xn[:, k * 128:(k + 1) * 128], ident[:]),
                         reads=[xnk, "ident"], writes=["pb%d" % bank])
                    if k % 4 == 3:
                        yield

            def n3(t):
                b0 = 4 + 2 * (t % 2)
                for k in range(8):
                    bank = b0 + k // 4
                    src_ps = pb[bank][:, (k % 4) * 128:(k % 4 + 1) * 128]
                    dst = dstT[:, k, t * 128:(t + 1) * 128]
                    if t < 16:
                        if k % 2 == 0:
                            P.op("act", lambda e, k=k, src_ps=src_ps, dst=dst: e.activation(dst, src_ps, AF.Identity, bias=modT[:, shblk + k, 0:1], scale=gs[:, k, 0:1]),
                                 reads=["pb%d" % bank, "modT", "gs1", "gs2"], writes=[tagp + "dstT"])
                        else:
                            P.op("dve", lambda e, k=k, src_ps=src_ps, dst=dst: e.tensor_scalar(dst, src_ps, gs[:, k, 0:1], modT[:, shblk + k, 0:1], ALU.mult, ALU.add),
                                 reads=["pb%d" % bank, "modT", "gs1", "gs2"], writes=[tagp + "dstT"])
                    else:
                        tm = tmp128[k % 2]; tk = "tmp128_%d" % (k % 2)
                        P.op("dve", lambda e, k=k, src_ps=src_ps, tm=tm: e.tensor_tensor(tm[:].rearrange("p (n j) -> p n j", j=8),
                                                                                          src_ps.rearrange("p (n j) -> p n j", j=8),
                                                                                          gs[:, k, 1:17].unsqueeze(2).to_broadcast([128, 16, 8]), ALU.mult),
                             reads=["pb%d" % bank, "gs1", "gs2"], writes=[tk])
                        P.op("dve", lambda e, k=k, dst=dst, tm=tm: e.tensor_tensor(dst.rearrange("p (n j) -> p n j", j=8),
                                                                                    tm[:].rearrange("p (n j) -> p n j", j=8),
                                                                                    modT[:, shblk + k, 1:17].unsqueeze(2).to_broadcast([128, 16, 8]), ALU.add),
                             reads=[tk, "modT"], writes=[tagp + "dstT"])
                    yield
            return [n1, n2, n3]

        junk = sb("junk", [128, D], BF16)
        tmp128 = [sb("tmp128_%d" % i, [128, 128]) for i in range(2)]
        rings = {}

        with contextlib.ExitStack() as phT, contextlib.ExitStack() as ph:
            rec_tab = Rec()
            s5_tables(nc, rec_tab, mk, phT, TH8, scr)
            rings["xn"] = [mk(ph, "xn%d" % i, [128, D]) for i in range(2)]
            xt_ring = [mk(ph, "xt%d" % i, [128, D]) for i in range(4)]
            def a0(t):
                P.dma("sp", xt_ring[t % 4][:], x_d[t * 128:(t + 1) * 128, :], writes=["xt%d" % (t % 4)]); yield
            gens = [(pipeline_gen([a0] + norm_stages(lambda t: (xt_ring[t % 4][:], "xt%d" % (t % 4)), hT, gs1, 0, 0, "A_"), NT), 2), (rec_tab.gen(P), 1)]
            while gens:
                for g_ in list(gens):
                    try:
                        for _ in range(g_[1]):
                            next(g_[0])
                    except StopIteration:
                        gens.remove(g_)
            P.barrier()
        dump("hT", hT[:], [128, 8, T], ["A_dstT"], BF16)
        if stop_after == "A":
            P.finish(); P.replay(); return nc

        mix = contextlib.ExitStack()
        uT = mk(mix, "uT", [128, 4, 8, NCH], BF16)
        gy5T = mk(mix, "gy5T", [128, 4, NCH, 8], BF16)
        gy5Tf = gy5T[:].rearrange("p q c s -> p q (c s)")
        with contextlib.ExitStack() as ph:
            wU = mk(ph, "wU", [128, 8, 512], BF16)
            P.dma("pool", wU[:], cw(win_d[:, 0:512]), writes=["wU"])
            n = 0
            for q in range(4):
                for (t0, tn) in TBS:
                    bank = n % 4
                    n += 1
                    for k in range(8):
                        P.op("pe", lambda e, k=k, q=q, t0=t0, tn=tn, bank=bank: e.matmul(pb[bank][:, 0:tn], wU[:, k, q * 128:(q + 1) * 128], hT[:, k, t0:t0 + tn], start=(k == 0), stop=(k == 7)),
                             reads=["wU", "A_dstT"], writes=["pb%d" % bank])
                    eng = "act" if n % 2 else "dve"
                    if eng == "act":
                        P.op("act", lambda e, q=q, t0=t0, tn=tn, bank=bank: e.copy(uT[:, q, :, t0 // 8:(t0 + tn) // 8], pb[bank][:, 0:tn].rearrange("p (c s) -> p s c", s=8)), reads=["pb%d" % bank], writes=["uT"])
                    else:
                        P.op("dve", lambda e, q=q, t0=t0, tn=tn, bank=bank: e.tensor_copy(uT[:, q, :, t0 // 8:(t0 + tn) // 8], pb[bank][:, 0:tn].rearrange("p (c s) -> p s c", s=8)), reads=["pb%d" % bank], writes=["uT"])
            P.barrier()

        fin_p = mk(mix, "fin_p", [128, 2, 16])
        s5_phase(nc, P, mk, sb, pb, ident, uT, gy5T, fin_p, s5dr, scr, dump, dumps)
        dump("gy5T", gy5Tf, [128, 4, T], ["gy5T"], BF16)
        if stop_after == "S5":
            P.finish(); P.replay(); mix.close(); return nc

        ygT = mk(mix, "ygT", [128, 4, T], BF16)
        with contextlib.ExitStack() as ph:
            wGM = mk(ph, "wGM", [128, 8, 1024], BF16)
            P.dma("pool", wGM[:, :, 0:512], cw(win_d[:, 512:1024]), writes=["wGMu"])
            P.dma("pool", wGM[:, :, 512:1024], cw(win_d[:, 1024:1536]), writes=["wGMv"])
            lng_bc = mk(ph, "lng_bc", [128, 512])
            lnb_bc = mk(ph, "lnb_bc", [128, 512])
            P.dma("sp", lng_bc[:], lng_d.rearrange("(o n) -> o n", o=1).to_broadcast([128, 512]), writes=["lng"])
            P.dma("sp", lnb_bc[:], lnb_d.rearrange("(o n) -> o n", o=1).to_broadcast([128, 512]), writes=["lnb"])
            wsp_n = mk(ph, "wsp_n", [128, 8, 128])
            P.dma("sp", wsp_n[:], wsp_d.rearrange("h t s -> t h s"), writes=["wsp_n"])
            WmT = mk(ph, "WmT", [128, 8, 128], BF16)
            WmT32 = mk(ph, "WmT32", [128, 8, 128])
            for h in range(8):
                bank = h // 4
                P.op("pe", lambda e, h=h, bank=bank: e.transpose(pb[bank][:, (h % 4) * 128:(h % 4 + 1) * 128], wsp_n[:, h, :], ident[:]),
                     reads=["wsp_n", "ident"], writes=["pb%d" % bank])
            for bank in range(2):
                P.op("act", lambda e, bank=bank: e.copy(WmT32[:, bank * 4:(bank + 1) * 4, :], pb[bank][:].rearrange("p (h t) -> p h t", t=128)),
                     reads=["pb%d" % bank], writes=["WmT32"])
            P.op("pool", lambda e: e.affine_select(WmT32[:], WmT32[:], [[0, 8], [1, 128]], ALU.is_ge, 0.0, base=0, channel_multiplier=-1),
                 reads=["WmT32"], writes=["WmT32"])
            P.op("act", lambda e: e.copy(WmT[:], WmT32[:]), reads=["WmT32"], writes=["WmT"])
            WmS = mk(ph, "WmS", [128, 8, 128], BF16)
            bm8 = mk(ph, "bm8", [128, 128])
            ti1 = mk(ph, "ti1", [128, 128], I32)
            ti2 = mk(ph, "ti2", [128, 128], I32)
            E8 = mk(ph, "E8", [8, 128])
            A1 = mk(ph, "A1", [8, 8, 128])
            P.op("pool", lambda e: e.iota(ti1[:], [[1, 16], [0, 8]], base=0, channel_multiplier=0), writes=["ti1"])
            P.op("pool", lambda e: e.iota(ti2[:], [[0, 128]], base=0, channel_multiplier=1), writes=["ti2"])
            P.op("dve", lambda e: e.tensor_single_scalar(ti2[:], ti2[:], 3, ALU.arith_shift_right), reads=["ti2"], writes=["ti2"])
            P.op("dve", lambda e: e.tensor_tensor(bm8[:], ti1[:], ti2[:], ALU.is_equal), reads=["ti1", "ti2"], writes=["bm8"])
            P.op("pool", lambda e: e.iota(ti1[0:8, :], [[0, 16], [1, 8]], base=0, channel_multiplier=-1), reads=["bm8"], writes=["ti1"])
            P.op("dve", lambda e: e.tensor_single_scalar(E8[:], ti1[0:8, :], 0, ALU.is_equal), reads=["ti1"], writes=["E8"])
            for h in range(8):
                P.op("dve", lambda e, h=h: e.tensor_copy(A1[:, h, :].rearrange("p (n j) -> p n j", j=8),
                                                         WmT32[0:8, h, 0:8].unsqueeze(1).to_broadcast([8, 16, 8])),
                     reads=["WmT32"], writes=["A1"])
            for h in range(8):
                bank = h // 4
                P.op("pe", lambda e, h=h, bank=bank: e.matmul(pb[bank][:, (h % 4) * 128:(h % 4 + 1) * 128], E8[:], A1[:, h, :], start=True, stop=True),
                     reads=["E8", "A1"], writes=["pb%d" % bank])
            for h in range(8):
                bank = h // 4
                P.op("dve", lambda e, h=h, bank=bank: e.tensor_tensor(WmS[:, h, :], pb[bank][:, (h % 4) * 128:(h % 4 + 1) * 128], bm8[:], ALU.mult),
                     reads=["pb%d" % bank, "bm8"], writes=["WmS"])
            bsp = mk(ph, "bsp", [128, 4, 128])
            for h in range(8):
                P.dma("sp", bsp[(h % 2) * 64:(h % 2 + 1) * 64, h // 2, :], bsp_d[h:h + 1, :].to_broadcast([64, 128]), writes=["bsp"])

            gu_blk = [mk(ph, "gu_blk%d" % i, [128, 4, 512]) for i in range(3)]
            vg = [mk(ph, "vg%d" % i, [128, 512]) for i in range(4)]
            vl = [mk(ph, "vl%d" % i, [128, 512]) for i in range(2)]
            vb = [mk(ph, "vb%d" % i, [128, 512], BF16) for i in range(2)]
            st6 = [mk(ph, "st6_%d" % i, [128, 6]) for i in range(3)]
            mv = [mk(ph, "mv%d" % i, [128, 2]) for i in range(3)]
            stmp = [mk(ph, "stmp%d" % i, [128, 4, 128]) for i in range(2)]

            def g0(t):
                bi = t // 4
                t0, tn = TBS[bi]
                todo = [(0, q) for q in range(4)] if t == 0 else []
                if bi + 1 < len(TBS) and t // 4 == bi and t < 16:
                    todo.append((bi + 1, t % 4))
                for (b2, q) in todo:
                    t0b, tnb = TBS[b2]
                    gb2 = gu_blk[b2 % 3]; gk_ = "gu_blk%d" % (b2 % 3)
                    bank = q % 2
                    for k in range(8):
                        P.op("pe", lambda e, k=k, q=q, bank=bank, t0b=t0b, tnb=tnb: e.matmul(pb[bank][:, 0:tnb], wGM[:, k, q * 128:(q + 1) * 128], hT[:, k, t0b:t0b + tnb], start=(k == 0), stop=(k == 7)),
                             reads=["wGMu", "A_dstT"], writes=["pb%d" % bank])
                    yield
                    P.op("act", lambda e, q=q, bank=bank, gb2=gb2, tnb=tnb: e.activation(gb2[:, q, 0:tnb], pb[bank][:, 0:tnb], AF.Gelu_apprx_tanh),
                         reads=["pb%d" % bank], writes=[gk_]); yield
                vbank = 2 if t % 2 == 0 else 4
                vk = "pb%d" % vbank
                for k in range(8):
                    P.op("pe", lambda e, k=k: e.matmul(pb[vbank][:], hT[:, k, t * 128:(t + 1) * 128], wGM[:, k, 512:1024], start=(k == 0), stop=(k == 7)),
                         reads=["wGMv", "A_dstT"], writes=[vk])
                yield
                g = vg[t % 4]; gk2 = "vg%d" % (t % 4)
                P.op("act", lambda e: e.activation(g[:], pb[vbank][:], AF.Gelu_apprx_tanh), reads=[vk], writes=[gk2]); yield

            def g0b(t):
                g = vg[t % 4]; gk2 = "vg%d" % (t % 4); s6 = st6[t % 3]; m = mv[t % 3]; mk_ = "mv%d" % (t % 3)
                P.op("dve", lambda e: e.bn_stats(s6[:], g[:]), reads=[gk2], writes=[mk_ + "s"]); yield
                P.op("dve", lambda e: e.bn_aggr(m[:], s6[:]), reads=[mk_ + "s"], writes=[mk_]); yield
                P.op("dve", lambda e: e.tensor_scalar(m[:, 1:2], m[:, 1:2], EPS, None, ALU.add), reads=[mk_], writes=[mk_]); yield
                P.op("act", lambda e: e.activation(m[:, 1:2], m[:, 1:2], AF.Sqrt), reads=[mk_], writes=[mk_]); yield
                P.op("dve", lambda e: e.reciprocal(m[:, 1:2], m[:, 1:2]), reads=[mk_], writes=[mk_]); yield

            def g1(t):
                g = vg[t % 4]; gk2 = "vg%d" % (t % 4); m = mv[t % 3]; mk_ = "mv%d" % (t % 3)
                l = vl[t % 2]; lk = "vl%d" % (t % 2); b = vb[t % 2]; bk = "vb%d" % (t % 2)
                P.op("dve", lambda e: e.tensor_scalar(g[:], g[:], m[:, 0:1], m[:, 1:2], ALU.subtract, ALU.mult), reads=[gk2, mk_], writes=[gk2]); yield
                P.op("pool", lambda e: e.tensor_tensor(l[:], g[:], lng_bc[:], ALU.mult), reads=[gk2, "lng"], writes=[lk]); yield
                P.op("pool", lambda e: e.tensor_tensor(l[:], l[:], lnb_bc[:], ALU.add), reads=[lk, "lnb"], writes=[lk]); yield
                if t == 16:
                    P.dma("sp", sv_d, l[:], reads=[lk])
                P.op("act", lambda e: e.copy(b[:], l[:]), reads=[lk], writes=[bk]); yield

            def g2(t):
                bi = t // 4
                tt = t % 4
                b = vb[t % 2]; bk = "vb%d" % (t % 2)
                sbank = 3 if t % 2 == 0 else 5
                sk = "pb%d" % sbank
                stm = stmp[t % 2]; stk = "stmp%d" % (t % 2)
                gb = gu_blk[bi % 3]; gk = "gu_blk%d" % (bi % 3)
                Wm = WmT if t < 16 else WmS
                for h in range(8):
                    P.op("pe", lambda e, h=h: e.matmul(pb[sbank][(h % 2) * 64:(h % 2 + 1) * 64, (h // 2) * 128:(h // 2 + 1) * 128],
                                                       b[:, h * 64:(h + 1) * 64], Wm[:, h, :], start=True, stop=True),
                         reads=[bk, "WmT", "WmS"], writes=[sk])
                yield
                if t < 16:
                    P.op("dve", lambda e: e.tensor_tensor(stm[:], pb[sbank][:].rearrange("p (a t) -> p a t", t=128), bsp[:], ALU.add),
                         reads=[sk, "bsp"], writes=[stk])
                else:
                    P.op("dve", lambda e: e.tensor_tensor(stm[:].rearrange("p a (n j) -> p a n j", j=8),
                                                           pb[sbank][:].rearrange("p (a n j) -> p a n j", n=16, j=8),
                                                           bsp[:, :, 0:8].unsqueeze(2).to_broadcast([128, 4, 16, 8]), ALU.add),
                         reads=[sk, "bsp"], writes=[stk])
                yield
                P.op("dve", lambda e: e.tensor_tensor(ygT[:, :, t * 128:(t + 1) * 128], stm[:], gb[:, :, tt * 128:(tt + 1) * 128], ALU.mult),
                     reads=[stk, gk], writes=["ygT"]); yield
            run_pipeline([g0, g0b, g1, g2], NT)
            P.barrier()
        dump("ygT", ygT[:], [128, 4, T], ["ygT"], BF16)
        if stop_after == "GM":
            P.finish(); P.replay(); mix.close(); return nc

        for k in range(3):
            P.dma("act", wc[:, :, k], wcv_d[k].rearrange("(c p) -> p c", p=128), writes=["wc"])
        P.dma("act", bcv[:], bcv_d.rearrange("(c p) -> p c", p=128), writes=["bcv"])
        mergedT = mk(mix, "mergedT", [128, 8, T], BF16)
        bgl = mk(mix, "bgl", [128, 16])
        P.dma("sp", bgl[:], bglu_d.rearrange("(b p) -> p b", p=128), writes=["bgl"], allow_slow_non_contiguous=True)
        with contextlib.ExitStack() as ph:
            NR = 2
            wga = [mk(ph, "wga%d" % i, [128, 8, 128], BF16) for i in range(NR)]
            wgb = [mk(ph, "wgb%d" % i, [128, 8, 128], BF16) for i in range(NR)]
            wz1 = [mk(ph, "wz1%d" % i, [128, 4, 128], BF16) for i in range(NR)]
            wz2 = [mk(ph, "wz2%d" % i, [128, 4, 128], BF16) for i in range(NR)]
            wyb = [mk(ph, "wyb%d" % i, [128, 4, 128], BF16) for i in range(NR)]
            tA = [mk(ph, "tA%d" % i, [128, 512]) for i in range(2)]
            tB = [mk(ph, "tB%d" % i, [128, 512]) for i in range(2)]
            tC = [mk(ph, "tC%d" % i, [128, 512]) for i in range(2)]

            def load_m(m):
                r = m % NR
                P.dma("pool", wga[r][:], cw(win_d[:, 1536 + m * 128:1536 + (m + 1) * 128]), writes=["wga%d" % r])
                P.dma("pool", wgb[r][:], cw(win_d[:, 2560 + m * 128:2560 + (m + 1) * 128]), writes=["wgb%d" % r])
                P.dma("pool", wz1[r][:], cw(wglu_d[:, m * 128:(m + 1) * 128]), writes=["wz1%d" % r])
                P.dma("pool", wz2[r][:], cw(wglu_d[:, 1024 + m * 128:1024 + (m + 1) * 128]), writes=["wz2%d" % r])
                P.dma("pool", wyb[r][:], cw(wgo_d[:, m * 128:(m + 1) * 128]), writes=["wyb%d" % r])
            load_m(0)
            it = 0
            for m in range(8):
                if m + 1 < 8:
                    load_m(m + 1)
                r = m % NR
                for (t0, tn) in TBS:
                    par = it % 2
                    it += 1
                    bga, bgb, bz1, bz2, byb = 0, 1, 2, 3, 4
                    for k in range(8):
                        P.op("pe", lambda e, k=k, r=r, t0=t0, tn=tn: e.matmul(pb[0][:, 0:tn], wga[r][:, k, :], hT[:, k, t0:t0 + tn], start=(k == 0), stop=(k == 7)),
                             reads=["wga%d" % r, "A_dstT"], writes=["pb0"])
                    for k in range(4):
                        P.op("pe", lambda e, k=k, r=r, t0=t0, tn=tn: e.matmul(pb[3][:, 0:tn], wz2[r][:, k, :], gy5Tf[:, k, t0:t0 + tn], start=(k == 0), stop=(k == 3)),
                             reads=["wz2%d" % r, "gy5T"], writes=["pb3"])
                    for k in range(4):
                        P.op("pe", lambda e, k=k, r=r, t0=t0, tn=tn: e.matmul(pb[2][:, 0:tn], wz1[r][:, k, :], gy5Tf[:, k, t0:t0 + tn], start=(k == 0), stop=(k == 3)),
                             reads=["wz1%d" % r, "gy5T"], writes=["pb2"])
                    for k in range(8):
                        P.op("pe", lambda e, k=k, r=r, t0=t0, tn=tn: e.matmul(pb[1][:, 0:tn], wgb[r][:, k, :], hT[:, k, t0:t0 + tn], start=(k == 0), stop=(k == 7)),
                             reads=["wgb%d" % r, "A_dstT"], writes=["pb1"])
                    for k in range(4):
                        P.op("pe", lambda e, k=k, r=r, t0=t0, tn=tn: e.matmul(pb[4][:, 0:tn], wyb[r][:, k, :], ygT[:, k, t0:t0 + tn], start=(k == 0), stop=(k == 3)),
                             reads=["wyb%d" % r, "ygT"], writes=["pb4"])
                    a, b, c3 = tA[par], tB[par], tC[par]
                    ak, bk, ck = "tA%d" % par, "tB%d" % par, "tC%d" % par
                    P.op("act", lambda e, a=a, tn=tn: e.activation(a[:, 0:tn], pb[0][:, 0:tn], AF.Sigmoid), reads=["pb0"], writes=[ak])
                    P.op("act", lambda e, b=b, tn=tn, m=m: e.activation(b[:, 0:tn], pb[3][:, 0:tn], AF.Sigmoid, bias=bgl[:, 8 + m:9 + m]), reads=["pb3", "bgl"], writes=[bk])
                    P.op("act", lambda e, c3=c3, tn=tn: e.activation(c3[:, 0:tn], pb[1][:, 0:tn], AF.Sigmoid), reads=["pb1"], writes=[ck])
                    P.op("dve", lambda e, b=b, tn=tn, m=m: e.scalar_tensor_tensor(b[:, 0:tn], pb[2][:, 0:tn], bgl[:, m:m + 1], b[:, 0:tn], ALU.add, ALU.mult),
                         reads=["pb2", "bgl", bk], writes=[bk])
                    P.op("dve", lambda e, a=a, b=b, tn=tn: e.tensor_tensor(a[:, 0:tn], a[:, 0:tn], b[:, 0:tn], ALU.mult), reads=[ak, bk], writes=[ak])
                    P.op("dve", lambda e, c3=c3, tn=tn: e.tensor_tensor(c3[:, 0:tn], pb[4][:, 0:tn], c3[:, 0:tn], ALU.mult), reads=["pb4", ck], writes=[ck])
                    P.op("dve", lambda e, a=a, c3=c3, m=m, t0=t0, tn=tn: e.tensor_tensor(mergedT[:, m, t0:t0 + tn], a[:, 0:tn], c3[:, 0:tn], ALU.add),
                         reads=[ak, ck], writes=["mergedT"])
            P.barrier()
        dump("mergedT", mergedT[:], [128, 8, T], ["mergedT"], BF16)
        if stop_after == "MERGE":
            P.finish(); P.replay(); mix.close(); return nc

        h2T = hT
        with contextlib.ExitStack() as ph:
            wo = mk(ph, "wo", [128, 8, D], BF16)
            P.dma("pool", wo[:, :, 0:512], cw(wout_d[:, 0:512]), writes=["wo0"])
            P.dma("pool", wo[:, :, 512:1024], cw(wout_d[:, 512:1024]), writes=["wo1"])
            gm_p = mk(ph, "gm_p", [128, D])
            gm_s = mk(ph, "gm_s", [128, D])
            tm_gate(gm_p, gm_s, 16, "gm")
            x1r = [mk(ph, "x1r%d" % i, [128, D]) for i in range(3)]
            rings["xn"] = [mk(ph, "xnB%d" % i, [128, D]) for i in range(2)]
            xt_ring = [mk(ph, "xtB%d" % i, [128, D]) for i in range(3)]

            def p0(t):
                xt = xt_ring[t % 3]
                xk = "xt%d" % (t % 3)
                P.dma("sp", xt[:], x_d[t * 128:(t + 1) * 128, :], writes=[xk]); yield
                banks = (0, 1) if t % 2 == 0 else (2, 3)
                for h in range(2):
                    for k in range(8):
                        P.op("pe", lambda e, k=k, h=h: e.matmul(pb[banks[h]][:], mergedT[:, k, t * 128:(t + 1) * 128], wo[:, k, h * 512:(h + 1) * 512], start=(k == 0), stop=(k == 7)),
                             reads=["mergedT", "wo%d" % h], writes=["pb%d" % banks[h]])
                    yield

            def p0b(t):
                xt = xt_ring[t % 3]
                xk = "xt%d" % (t % 3)
                banks = (0, 1) if t % 2 == 0 else (2, 3)
                x1 = x1r[t % 3]
                x1k = "x1r%d" % (t % 3)
                g = gm_p if t < 16 else gm_s
                for h in range(2):
                    P.op("dve", lambda e, h=h: e.tensor_tensor(x1[:, h * 512:(h + 1) * 512], pb[banks[h]][:], g[:, h * 512:(h + 1) * 512], ALU.mult),
                         reads=["pb%d" % banks[h], "gm_p", "gm_s"], writes=[x1k]); yield
                P.op("pool", lambda e: e.tensor_tensor(x1[:], x1[:], xt[:], ALU.add), reads=[x1k, xk], writes=[x1k]); yield
                P.dma("sp", x1_d[t * 128:(t + 1) * 128, :], x1[:], reads=[x1k], writes=["x1_d%d" % t]); yield
            run_pipeline([p0, p0b] + norm_stages(lambda t: (x1r[t % 3][:], "x1r%d" % (t % 3)), h2T, gs2, 24, 1, "B_"), NT)
            P.barrier()
        dump("h2T", h2T[:], [128, 8, T], ["B_dstT"], BF16)
        if "h2T" in dumps or "mergedT" in dumps:
            P.barrier()
        if stop_after == "P3":
            P.finish(); P.replay(); mix.close(); return nc

        mix.close()
        ffn_phase(nc, P, mk, sb, pb, ident, h2T, modT, tm_gate, fg_bc, (gf_p, gf_s, pastT, wc, bcv), rs_all, junk, dict(
            wup=wup_d, wcv=wcv_d, bcv=bcv_d, wdn=wdn_d, scv=scv_d, x1=x1_d, y=y_d, pcv=pcv_d, scvo=scvo_d), cw, dump, stop_after)

        P.finish()
        P.replay()
    return nc


def s5_prep(nc, P, mk, ph, pb, ident, dr, scr, TH8):
    TWO_PI = 2.0 * math.pi
    if True:
        are = mk(ph, "are", [128, 16]); aim = mk(ph, "aim", [128, 16]); ldt = mk(ph, "ldt", [128, 16])
        Bre = mk(ph, "Bre", [128, 16, 16]); Bim = mk(ph, "Bim", [128, 16, 16])
        CTr = mk(ph, "CTr", [128, 16, 16]); CTi = mk(ph, "CTi", [128, 16, 16])
        dcol = mk(ph, "dcol", [128, 4])
        ph0 = ph
        Cn_re = mk(ph0, "Cn_re", [16, 32, 64]); Cn_im = mk(ph0, "Cn_im", [16, 32, 64])
        for gi in range(2):
            sl = slice(gi * 64, (gi + 1) * 64)
            P.dma("sp", are[sl, :], dr["are"].rearrange("(pr gi) p -> gi p pr", gi=2)[gi], writes=["are"], allow_slow_non_contiguous=True)
            P.dma("sp", aim[sl, :], dr["aim"].rearrange("(pr gi) p -> gi p pr", gi=2)[gi], writes=["aim"], allow_slow_non_contiguous=True)
            P.dma("sp", ldt[sl, :], dr["ldt"].rearrange("(pr gi) -> gi pr", gi=2)[gi:gi + 1, :].to_broadcast([64, 16]), writes=["ldt"])
            P.dma("sp", Bre[sl], dr["bre"].rearrange("(pr gi) p h -> gi p pr h", gi=2)[gi], writes=["Bre"])
            P.dma("sp", Bim[sl], dr["bim"].rearrange("(pr gi) p h -> gi p pr h", gi=2)[gi], writes=["Bim"])
        P.dma("sp", Cn_re[:], dr["cre"].rearrange("g h p -> h g p"), writes=["Cn_re"])
        P.dma("sp", Cn_im[:], dr["cim"].rearrange("g h p -> h g p"), writes=["Cn_im"])
        P.dma("sp", dcol[:], dr["sd"].rearrange("(q g) h -> (g h) q", q=4), writes=["dcol"], allow_slow_non_contiguous=True)
        for ri, (Cn, CT, nm) in enumerate(((Cn_re, CTr, "CTr"), (Cn_im, CTi, "CTi"))):
            for pr in range(16):
                P.op("pe", lambda e, pr=pr, Cn=Cn, ri=ri: e.transpose(pb[4 + ri][:, pr * 16:(pr + 1) * 16],
                                                                      Cn[:, 2 * pr:2 * pr + 2, :].rearrange("h g p -> h (g p)"), ident[0:16, 0:16]),
                     reads=["Cn_re", "Cn_im", "ident"], writes=["pb%d" % (4 + ri)])
            P.op("act", lambda e, CT=CT, ri=ri: e.copy(CT[:], pb[4 + ri][:, 0:256].rearrange("p (a b) -> p a b", b=16)), reads=["pb%d" % (4 + ri)], writes=[nm])

        tA = mk(ph, "s5tA", [128, 256]); tB = mk(ph, "s5tB", [128, 256]); tI = mk(ph, "s5tI", [128, 256], I32)

        def sincos(ang, n, sn_out, cs_out, rk, wk):
            a = tA[:, 0:n]; b = tB[:, 0:n]; ii = tI[:, 0:n]
            for off, out in ((0.0, sn_out), (0.25, cs_out)):
                P.op("dve", lambda e, off=off: e.tensor_scalar(a, ang, 1.0 / TWO_PI, off, ALU.mult, ALU.add), reads=rk, writes=["s5tA"])
                P.op("dve", lambda e: e.tensor_copy(ii, a), reads=["s5tA"], writes=["s5tI"])
                P.op("dve", lambda e: e.tensor_copy(b, ii), reads=["s5tI"], writes=["s5tB"])
                P.op("dve", lambda e: e.tensor_tensor(a, a, b, ALU.subtract), reads=["s5tA", "s5tB"], writes=["s5tA"])
                P.op("dve", lambda e: e.tensor_scalar(b, a, 0.5, -1.0, ALU.is_gt, ALU.mult), reads=["s5tA"], writes=["s5tB"])
                P.op("dve", lambda e: e.tensor_tensor(a, a, b, ALU.add), reads=["s5tA", "s5tB"], writes=["s5tA"])
                P.op("dve", lambda e: e.tensor_scalar(b, a, -0.5, 1.0, ALU.is_lt, ALU.mult), reads=["s5tA"], writes=["s5tB"])
                P.op("dve", lambda e: e.tensor_tensor(a, a, b, ALU.add), reads=["s5tA", "s5tB"], writes=["s5tA"])
                P.op("act", lambda e, out=out: e.activation(out, a, AF.Sin, scale=6.28318), reads=["s5tA"], writes=wk)

        def tt(out, a, b, op, reads, writes, eng="dve"):
            P.op(eng, lambda e: e.tensor_tensor(out, a, b, op), reads=reads, writes=writes)

        c1 = mk(ph, "s5c1", [128, 256]); c2 = mk(ph, "s5c2", [128, 256])

        def cmul(o_re, o_im, a_re, a_im, b_re, b_im, shape, reads, writes):
            n = int(np.prod(shape))
            v1 = c1[:, 0:n]; v2 = c2[:, 0:n]
            if len(shape) == 2:
                v1 = v1.rearrange("p (a b) -> p a b", b=shape[1]); v2 = v2.rearrange("p (a b) -> p a b", b=shape[1])
            tt(v1, a_re, b_re, ALU.mult, reads, ["s5c1"])
            tt(v2, a_im, b_im, ALU.mult, reads, ["s5c2"])
            tt(o_re, v1, v2, ALU.subtract, ["s5c1", "s5c2"], writes + ["s5c1", "s5c2"])
            tt(v1, a_re, b_im, ALU.mult, reads, ["s5c1"])
            tt(v2, a_im, b_re, ALU.mult, reads, ["s5c2"])
            tt(o_im, v1, v2, ALU.add, ["s5c1", "s5c2"], writes + ["s5c1", "s5c2"])

        dt = mk(ph, "s5dt", [128, 16]); mag = mk(ph, "s5mag", [128, 16]); th = mk(ph, "s5th", [128, 16])
        sn = mk(ph, "s5sn", [128, 16]); cs = mk(ph, "s5cs", [128, 16])
        APr = mk(ph, "APr", [128, 9, 16]); APi = mk(ph, "APi", [128, 9, 16])
        P.op("act", lambda e: e.activation(dt[:], ldt[:], AF.Exp), reads=["ldt"], writes=["dt"])
        tt(mag[:], are[:], dt[:], ALU.mult, ["are", "dt"], ["mag"])
        P.op("act", lambda e: e.activation(mag[:], mag[:], AF.Exp), reads=["mag"], writes=["mag"])
        tt(th[:], aim[:], dt[:], ALU.mult, ["aim", "dt"], ["th"])
        sincos(th[:], 16, sn[:], cs[:], ["th"], ["sncs"])
        P.op("dve", lambda e: e.memset(APr[:, 0, :], 1.0), writes=["AP"])
        P.op("dve", lambda e: e.memset(APi[:, 0, :], 0.0), writes=["AP"])
        tt(APr[:, 1, :], mag[:], cs[:], ALU.mult, ["mag", "sncs"], ["AP"])
        tt(APi[:, 1, :], mag[:], sn[:], ALU.mult, ["mag", "sncs"], ["AP"])
        def bck(ap16, n):
            return ap16.unsqueeze(1).to_broadcast([128, n, 16])
        cmul(APr[:, 2, :], APi[:, 2, :], APr[:, 1, :], APi[:, 1, :], APr[:, 1, :], APi[:, 1, :], [16], ["AP"], ["AP"])
        cmul(APr[:, 3:5, :], APi[:, 3:5, :], APr[:, 1:3, :], APi[:, 1:3, :], bck(APr[:, 2, :], 2), bck(APi[:, 2, :], 2), [2, 16], ["AP"], ["AP"])
        cmul(APr[:, 5:9, :], APi[:, 5:9, :], APr[:, 1:5, :], APi[:, 1:5, :], bck(APr[:, 4, :], 4), bck(APi[:, 4, :], 4), [4, 16], ["AP"], ["AP"])
        R8 = mk(ph, "R8", [128, 16]); nA8i = mk(ph, "nA8i", [128, 16])
        tt(R8[:], are[:], dt[:], ALU.mult, ["are", "dt"], ["R8"])
        P.op("act", lambda e: e.activation(R8[:], R8[:], AF.Exp, scale=8.0), reads=["R8"], writes=["R8"])
        P.op("dve", lambda e: e.tensor_scalar(TH8[:], th[:], 8.0, None, ALU.mult), reads=["th"], writes=["TH8"])
        P.op("dve", lambda e: e.tensor_scalar(nA8i[:], APi[:, 8, :], -1.0, None, ALU.mult), reads=["AP"], writes=["nA8i"])

        nr = mk(ph, "s5nr", [128, 16]); den = mk(ph, "s5den", [128, 16]); t16 = mk(ph, "s5t16", [128, 16])
        cr = mk(ph, "s5cr", [128, 16]); ci = mk(ph, "s5ci", [128, 16])
        P.op("dve", lambda e: e.tensor_scalar(nr[:], APr[:, 1, :], -1.0, None, ALU.add), reads=["AP"], writes=["nr"])
        tt(den[:], are[:], are[:], ALU.mult, ["are"], ["den"])
        tt(t16[:], aim[:], aim[:], ALU.mult, ["aim"], ["t16"])
        tt(den[:], den[:], t16[:], ALU.add, ["den", "t16"], ["den"])
        P.op("dve", lambda e: e.reciprocal(den[:], den[:]), reads=["den"], writes=["den"])
        tt(cr[:], nr[:], are[:], ALU.mult, ["nr", "are"], ["cr"])
        tt(t16[:], APi[:, 1, :], aim[:], ALU.mult, ["AP", "aim"], ["t16"])
        tt(cr[:], cr[:], t16[:], ALU.add, ["cr", "t16"], ["cr"])
        tt(cr[:], cr[:], den[:], ALU.mult, ["cr", "den"], ["cr"])
        tt(ci[:], APi[:, 1, :], are[:], ALU.mult, ["AP", "are"], ["ci"])
        tt(t16[:], nr[:], aim[:], ALU.mult, ["nr", "aim"], ["t16"])
        tt(ci[:], ci[:], t16[:], ALU.subtract, ["ci", "t16"], ["ci"])
        tt(ci[:], ci[:], den[:], ALU.mult, ["ci", "den"], ["ci"])
        bbr = mk(ph, "bbr", [128, 16, 16]); bbi = mk(ph, "bbi", [128, 16, 16])

        def bc16(ap16):
            return ap16.unsqueeze(2).to_broadcast([128, 16, 16])
        cmul(bbr[:], bbi[:], bc16(cr[:]), bc16(ci[:]), Bre[:], Bim[:], [16, 16], ["cr", "ci", "Bre", "Bim"], ["bb"])

        LA = mk(ph, "LA", [128, 4, 8, 2, 128], BF16)
        LC = mk(ph, "LC", [128, 8, 2, 16, 32], BF16)
        KC = mk(ph, "KC", [128, 4, 8, 128], BF16)
        P.op("pool", lambda e: e.memset(LC[:], 0.0), writes=["LC"])
        if True:
            ph2 = ph
            WP = mk(ph2, "WP", [128, 8, 2, 16, 2, 16])
            CP = mk(ph2, "CP", [128, 2, 16, 2, 16])
            Wr = mk(ph2, "s5Wr", [128, 16, 16]); Wi = mk(ph2, "s5Wi", [128, 16, 16])
            bm16 = mk(ph2, "bm16", [128, 128]); ti1 = mk(ph2, "s5ti1", [128, 128], I32); ti2 = mk(ph2, "s5ti2", [128, 128], I32)
            ktmp = mk(ph2, "ktmp", [128, 128])
            P.op("pool", lambda e: e.memset(WP[:], 0.0), writes=["WP"])
            P.op("pool", lambda e: e.memset(CP[:], 0.0), writes=["CP"])
            P.op("pool", lambda e: e.iota(ti1[:], [[1, 8], [0, 16]], base=0, channel_multiplier=0), writes=["s5ti1"])
            P.op("pool", lambda e: e.iota(ti2[:], [[0, 128]], base=0, channel_multiplier=1), writes=["s5ti2"])
            P.op("dve", lambda e: e.tensor_single_scalar(ti2[:], ti2[:], 4, ALU.arith_shift_right), reads=["s5ti2"], writes=["s5ti2"])
            P.op("dve", lambda e: e.tensor_tensor(bm16[:], ti1[:], ti2[:], ALU.is_equal), reads=["s5ti1", "s5ti2"], writes=["bm16"])
            for gi in range(2):
                sl = slice(gi * 64, (gi + 1) * 64)
                P.op("dve", lambda e, sl=sl, gi=gi: e.tensor_copy(CP[sl, 0, :, gi, :], CTr[sl]), reads=["CTr", "CP"], writes=["CP"])
                P.op("dve", lambda e, sl=sl, gi=gi: e.tensor_scalar(CP[sl, 1, :, gi, :], CTi[sl], -1.0, None, ALU.mult), reads=["CTi", "CP"], writes=["CP"])
            Wt = [[mk(ph2, "s5Wt%d_%d" % (i, j), [128, 16, 16]) for j in range(4)] for i in range(8)]

            def cmul_multi(insts, eng="dve"):
                for step in range(6):
                    for (o_re, o_im, a_re, a_im, b_re, b_im, v1, v2, rk, okey, vkey) in insts:
                        if step == 0:
                            tt(v1, a_re, b_re, ALU.mult, rk, [vkey + "a"], eng)
                        elif step == 1:
                            tt(v2, a_im, b_im, ALU.mult, rk, [vkey + "b"], eng)
                        elif step == 2:
                            tt(o_re, v1, v2, ALU.subtract, [vkey + "a", vkey + "b"], [okey + "r"], eng)
                        elif step == 3:
                            tt(v1, a_re, b_im, ALU.mult, rk, [vkey + "a"], eng)
                        elif step == 4:
                            tt(v2, a_im, b_re, ALU.mult, rk, [vkey + "b"], eng)
                        else:
                            tt(o_im, v1, v2, ALU.add, [vkey + "a", vkey + "b"], [okey + "i"], eng)
            insts = []
            for s in range(8):
                k = 7 - s
                w = Wt[s]
                insts.append((w[0][:], w[1][:], bc16(APr[:, k, :]), bc16(APi[:, k, :]), bbr[:], bbi[:], w[2][:], w[3][:], ["AP", "bb"], "Wt%d" % s, "Wv%d" % s))
            cmul_multi(insts)
            for s in range(8):
                w = Wt[s]
                for gi in range(2):
                    sl = slice(gi * 64, (gi + 1) * 64)
                    P.op("dve", lambda e, sl=sl, gi=gi, s=s, w=w: e.tensor_copy(WP[sl, s, 0, :, gi, :], w[0][sl]), reads=["Wt%dr" % s, "WP"], writes=["WP"])
                    P.op("dve", lambda e, sl=sl, gi=gi, s=s, w=w: e.tensor_copy(WP[sl, s, 1, :, gi, :], w[1][sl]), reads=["Wt%di" % s, "WP"], writes=["WP"])
            insts = []
            for j in range(8):
                w = Wt[j]
                insts.append((w[0][:], w[1][:], bc16(APr[:, j + 1, :]), bc16(APi[:, j + 1, :]), CTr[:], CTi[:], w[2][:], w[3][:], ["AP", "CTr", "CTi"], "Wt%d" % j, "Wv%d" % j))
            cmul_multi(insts, "pool")
            for j in range(8):
                w = Wt[j]
                for gi in range(2):
                    sl = slice(gi * 64, (gi + 1) * 64)
                    P.op("dve", lambda e, sl=sl, gi=gi, j=j, w=w: e.tensor_copy(LC[sl, j, 0, :, gi * 16:(gi + 1) * 16], w[0][sl]), reads=["Wt%dr" % j, "LC"], writes=["LC"])
                    P.op("dve", lambda e, sl=sl, gi=gi, j=j, w=w: e.tensor_scalar(LC[sl, j, 1, :, gi * 16:(gi + 1) * 16], w[1][sl], -1.0, None, ALU.mult), reads=["Wt%di" % j, "LC"], writes=["LC"])
            WPf = WP[:].rearrange("p s r a b c -> p s r (a b c)")
            CPf = CP[:].rearrange("p r a b c -> p r (a b c)")
            n = 0
            for q in range(4):
                qs = slice(q * 128, (q + 1) * 128)
                for s0 in range(0, 8, 2):
                    bank = 6 + n % 2
                    n += 1
                    for ds in range(2):
                        for ri in range(2):
                            col = (ds * 2 + ri) * 128
                            P.op("pe", lambda e, s_=s0 + ds, ri=ri, qs=qs, bank=bank, col=col: e.transpose(pb[bank][:, col:col + 128], WPf[:, s_, ri, qs], ident[:]),
                                 reads=["WP", "ident"], writes=["pb%d" % bank])
                    P.op("dve", lambda e, q=q, s0=s0, bank=bank: e.tensor_copy(LA[:, q, s0:s0 + 2, :, :].rearrange("p a b c -> p (a b c)"), pb[bank][:, 0:512]),
                         reads=["pb%d" % bank], writes=["LA"])
                for (t0, nt) in ((0, 1), (1, 4), (5, 3)):
                    bank = 4 + n % 2
                    n += 1
                    for dt_ in range(nt):
                        s_ = 7 - (t0 + dt_)
                        col = dt_ * 128
                        P.op("pe", lambda e, s_=s_, qs=qs, bank=bank, col=col: e.matmul(pb[bank][:, col:col + 128], WPf[:, s_, 0, qs], CPf[:, 0, qs], start=True, stop=False),
                             reads=["WP", "CP"], writes=["pb%d" % bank])
                        P.op("pe", lambda e, s_=s_, qs=qs, bank=bank, col=col: e.matmul(pb[bank][:, col:col + 128], WPf[:, s_, 1, qs], CPf[:, 1, qs], start=False, stop=True),
                             reads=["WP", "CP"], writes=["pb%d" % bank])
                    if t0 == 0:
                        tt(ktmp[:], pb[bank][:, 0:128], bm16[:], ALU.mult, ["pb%d" % bank, "bm16"], ["ktmp"])
                        P.op("dve", lambda e, q=q: e.scalar_tensor_tensor(KC[:, q, 0, :], ident[:], dcol[:, q:q + 1], ktmp[:], ALU.mult, ALU.add),
                             reads=["ident", "dcol", "ktmp"], writes=["KC"])
                    else:
                        P.op("dve", lambda e, q=q, t0=t0, nt=nt, bank=bank: e.tensor_tensor(KC[:, q, t0:t0 + nt, :], pb[bank][:, 0:nt * 128].rearrange("p (a b) -> p a b", b=128),
                                                                                             bm16[:].unsqueeze(1).to_broadcast([128, nt, 128]), ALU.mult),
                             reads=["pb%d" % bank, "bm16"], writes=["KC"])

        small = mk(ph, "s5small", [128, 4, 16])
        P.op("dve", lambda e: e.tensor_copy(small[:, 0, :], R8[:]), reads=["R8"], writes=["s5small"])
        P.op("dve", lambda e: e.tensor_copy(small[:, 1, :], nA8i[:]), reads=["nA8i"], writes=["s5small"])
        P.op("dve", lambda e: e.tensor_copy(small[:, 2, :], APr[:, 8, :]), reads=["AP"], writes=["s5small"])
        P.op("dve", lambda e: e.tensor_copy(small[:, 3, :], APi[:, 8, :]), reads=["AP"], writes=["s5small"])
        P.dma("sp", scr["small"], small[:], reads=["s5small"], writes=["scr_small"])
        P.dma("sp", scr["LA"], LA[:].rearrange("p a b c d -> p (a b c d)"), reads=["LA"], writes=["scr_LA"])
        P.dma("sp", scr["LC"], LC[:].rearrange("p a b c d -> p (a b c d)"), reads=["LC"], writes=["scr_LC"])
        P.dma("sp", scr["KC"], KC[:].rearrange("p a b c -> p (a b c)"), reads=["KC"], writes=["scr_KC"])


def s5_tables(nc, P, mk, ph, TH8, scr):
    TWO_PI = 2.0 * math.pi
    cio_i = mk(ph, "cio_i", [128, 256], I32); cio = mk(ph, "cio", [128, 256])
    nhalf = mk(ph, "nhalf", [128, 1])
    P.op("dve", lambda e: e.memset(nhalf[:], -3.14159), writes=["nhalf"])
    P.op("pool", lambda e: e.iota(cio_i[:], [[1, 256]], base=0, channel_multiplier=0), writes=["cio_i"])
    P.op("dve", lambda e: e.tensor_copy(cio[:], cio_i[:]), reads=["cio_i"], writes=["cio"])
    NQ = 4
    angq = mk(ph, "angq", [128, NQ * 256])
    tA = [mk(ph, "tAq%d" % i, [128, NQ * 256]) for i in range(2)]
    tB = [mk(ph, "tBq%d" % i, [128, NQ * 256]) for i in range(2)]
    tI = [mk(ph, "tIq%d" % i, [128, NQ * 256], I32) for i in range(2)]
    Eq = [[mk(ph, "Eq%d_%d" % (i, j), [128, NQ * 256]) for j in range(2)] for i in range(2)]
    for qq in range(16 // NQ):
        par = qq % 2
        P.op("dve", lambda e, qq=qq: e.tensor_tensor(angq[:].rearrange("p (a c) -> p a c", c=256),
                                                      cio[:].unsqueeze(1).to_broadcast([128, NQ, 256]),
                                                      TH8[:, qq * NQ:(qq + 1) * NQ].unsqueeze(2).to_broadcast([128, NQ, 256]), ALU.mult),
             reads=["cio", "TH8"], writes=["angq"])
        steps = []
        for ti, off in ((0, 0.0), (1, 0.25)):
            out = Eq[par][ti]; ok = "Eq%d_%d" % (par, ti)
            a = tA[ti][:]; b = tB[ti][:]; ii = tI[ti][:]
            ka, kb, ki = "tAq%d" % ti, "tBq%d" % ti, "tIq%d" % ti
            steps.append([
                ("dve", lambda e, a=a, off=off: e.tensor_scalar(a, angq[:], 1.0 / TWO_PI, off + 0.5, ALU.mult, ALU.add), ["angq"], [ka]),
                ("dve", lambda e, a=a, ii=ii: e.tensor_copy(ii, a), [ka], [ki]),
                ("dve", lambda e, b=b, ii=ii: e.tensor_copy(b, ii), [ki], [kb]),
                ("dve", lambda e, a=a, b=b: e.tensor_tensor(a, a, b, ALU.subtract), [ka, kb], [ka]),
                ("dve", lambda e, a=a, b=b: e.tensor_scalar(b, a, 0.0, 1.0, ALU.is_lt, ALU.mult), [ka], [kb]),
                ("dve", lambda e, a=a, b=b: e.tensor_tensor(a, a, b, ALU.add), [ka, kb], [ka]),
                ("act", lambda e, a=a, out=out: e.activation(out[:], a, AF.Sin, scale=6.28318, bias=nhalf[:, 0:1]), [ka, "nhalf"], [ok]),
            ])
        for k in range(7):
            for ti in range(2):
                eng, fn, rk, wk = steps[ti][k]
                P.op(eng, fn, reads=rk, writes=wk)
        for ti in range(2):
            out = Eq[par][ti]; ok = "Eq%d_%d" % (par, ti)
            P.dma("sp", scr["tab"][:, ti, qq * NQ:(qq + 1) * NQ, :], out[:].rearrange("p (a c) -> p a c", c=256), reads=[ok], writes=["scr_tab"])


def s5_phase(nc, P, mk, sb, pb, ident, uT, gy5T, fin_p, dr, scr, dump, dumps):
    with contextlib.ExitStack() as ph:
        LA = mk(ph, "LA_m", [128, 4, 8, 2, 128], BF16)
        LC = mk(ph, "LC_m", [128, 8, 2, 16, 32], BF16)
        KC = mk(ph, "KC_m", [128, 4, 8, 128], BF16)
        small = mk(ph, "s5small_m", [128, 4, 16])
        P.dma("sp", LA[:].rearrange("p a b c d -> p (a b c d)"), scr["LA"], reads=["scr_LA"], writes=["LA"])
        P.dma("sp", LC[:].rearrange("p a b c d -> p (a b c d)"), scr["LC"], reads=["scr_LC"], writes=["LC"])
        P.dma("sp", KC[:].rearrange("p a b c -> p (a b c)"), scr["KC"], reads=["scr_KC"], writes=["KC"])
        P.dma("sp", small[:], scr["small"], reads=["scr_small"], writes=["s5small"])
        sT = mk(ph, "sT", [128, 2, 16, 16])
        HS = mk(ph, "HS", [128, 16, 2, NCH], BF16)
        fin_s = mk(ph, "fin_s", [128, 2, 16, 16])
        st_one = mk(ph, "st_in0", [16, 2048])
        st_in = [st_one, st_one]
        for ri in range(2):
            P.dma("sp", st_in[ri][:], dr["sre" if ri == 0 else "sim"], writes=["st_in0"])
            for pr in range(16):
                P.op("pe", lambda e, ri=ri, pr=pr: e.transpose(pb[2 + ri][:, pr * 16:(pr + 1) * 16], st_in[ri][:, pr * 128:(pr + 1) * 128], ident[0:16, 0:16]),
                     reads=["st_in0", "ident"], writes=["pb%d" % (2 + ri)])
            P.op("act", lambda e, ri=ri: e.copy(sT[:, ri, :, :], pb[2 + ri][:, 0:256].rearrange("p (a b) -> p a b", b=16)), reads=["pb%d" % (2 + ri)], writes=["sT"])
            P.op("dve", lambda e, ri=ri: e.tensor_copy(HS[:, :, ri, 256:272], sT[:, ri, :, :]), reads=["sT"], writes=["HSs"])
            P.op("dve", lambda e, ri=ri: e.memset(HS[:, :, ri, 0:1], 0.0), writes=["HS0"])

        def rr(gens):
            gens = list(gens)
            while gens:
                for g_ in list(gens):
                    try:
                        next(g_)
                    except StopIteration:
                        gens.remove(g_)

        tmps = []
        for par in range(3):
            d = {}
            for nm in ("Ec", "Es", "Mr", "Mi", "m1", "m2", "Gr", "Gi"):
                d[nm] = mk(ph, "%s_%d" % (nm, par), [128, 256])
            d["Sr"] = mk(ph, "Sr_%d" % par, [128, NCH]); d["Si"] = mk(ph, "Si_%d" % par, [128, NCH])
            d["He"] = mk(ph, "He_%d" % par, [128, 2, 256])
            tmps.append(d)

        def pair_gen(pr, par):
            q, i = divmod(pr, 4)
            rows = slice(32 * i, 32 * i + 32)
            d = tmps[par]
            K = lambda nm: "%s_%d" % (nm, par)
            bnk = ((0, 1), (4, 5), (6, 7))[par]
            Ec, Es, Mr, Mi, m1, m2, Gr, Gi, Sr, Si, He = (d[x] for x in ("Ec", "Es", "Mr", "Mi", "m1", "m2", "Gr", "Gi", "Sr", "Si", "He"))

            def t2(out, a, b, op, reads, writes):
                P.op("dve", lambda e: e.tensor_tensor(out, a, b, op), reads=reads, writes=writes)
            for ri in range(2):
                bank = bnk[ri]
                for s_ in range(8):
                    P.op("pe", lambda e, s_=s_, ri=ri, bank=bank: e.matmul(pb[bank][:, 0:NCH], LA[rows, q, s_, ri, :], uT[rows, q, s_, :],
                                                                        start=(s_ == 0), stop=(s_ == 7), tile_position=(32 * i, 0)),
                         reads=["LA", "uT"], writes=["pb%d" % bank])
                yield
            P.op("act", lambda e: e.copy(Sr[:], pb[bnk[0]][:, 0:NCH]), reads=["pb%d" % bnk[0]], writes=[K("Sr")]); yield
            P.op("act", lambda e: e.copy(Si[:], pb[bnk[1]][:, 0:NCH]), reads=["pb%d" % bnk[1]], writes=[K("Si")]); yield
            P.dma("sp", Es[:], scr["tab"][:, 0, pr, :], reads=["scr_tab"], writes=[K("EcEs")]); yield
            P.dma("sp", Ec[:], scr["tab"][:, 1, pr, :], reads=["scr_tab"], writes=[K("EcEs")]); yield
            t2(m1[:], Sr[:, 0:256], Ec[:], ALU.mult, [K("Sr"), K("EcEs")], [K("m1")]); yield
            t2(m2[:], Si[:, 0:256], Es[:], ALU.mult, [K("Si"), K("EcEs")], [K("m2")]); yield
            t2(Mr[:], m1[:], m2[:], ALU.add, [K("m1"), K("m2")], [K("Mr"), K("m1"), K("m2")]); yield
            t2(m1[:], Si[:, 0:256], Ec[:], ALU.mult, [K("Si"), K("EcEs")], [K("m1")]); yield
            t2(m2[:], Sr[:, 0:256], Es[:], ALU.mult, [K("Sr"), K("EcEs")], [K("m2")]); yield
            t2(Mi[:], m1[:], m2[:], ALU.subtract, [K("m1"), K("m2")], [K("Mi"), K("m1"), K("m2")]); yield
            P.op("dve", lambda e: e.tensor_tensor_scan(Gr[:], small[:, 0, pr:pr + 1].to_broadcast([128, 256]), Mr[:], 0.0, ALU.mult, ALU.add),
                 reads=["s5small", K("Mr")], writes=[K("Gr")]); yield
            P.op("dve", lambda e: e.tensor_tensor_scan(Gi[:], small[:, 0, pr:pr + 1].to_broadcast([128, 256]), Mi[:], 0.0, ALU.mult, ALU.add),
                 reads=["s5small", K("Mi")], writes=[K("Gi")]); yield
            t2(m1[:], Gr[:], Ec[:], ALU.mult, [K("Gr"), K("EcEs")], [K("m1")]); yield
            t2(m2[:], Gi[:], Es[:], ALU.mult, [K("Gi"), K("EcEs")], [K("m2")]); yield
            t2(He[:, 0, :], m1[:], m2[:], ALU.subtract, [K("m1"), K("m2")], [K("He"), K("m1"), K("m2")]); yield
            t2(m1[:], Gi[:], Ec[:], ALU.mult, [K("Gi"), K("EcEs")], [K("m1")]); yield
            t2(m2[:], Gr[:], Es[:], ALU.mult, [K("Gr"), K("EcEs")], [K("m2")]); yield
            t2(He[:, 1, :], m1[:], m2[:], ALU.add, [K("m1"), K("m2")], [K("He"), K("m1"), K("m2")]); yield
            P.op("act", lambda e: e.copy(HS[:, pr, :, 1:256], He[:, :, 0:255]), reads=[K("He")], writes=["HSp%d" % q]); yield
            P.op("act", lambda e: e.copy(fin_p[:, :, pr:pr + 1], He[:, :, 255:256]), reads=[K("He")], writes=["fin_p"]); yield
            a8r = small[:, 2, pr:pr + 1]; a8i = small[:, 3, pr:pr + 1]; na8i = small[:, 1, pr:pr + 1]
            P.op("dve", lambda e: e.scalar_tensor_tensor(m1[:, 0:16], sT[:, 0, pr, :], a8r, Sr[:, 256:272], ALU.mult, ALU.add),
                 reads=["sT", "s5small", K("Sr")], writes=[K("m1")]); yield
            P.op("dve", lambda e: e.scalar_tensor_tensor(fin_s[:, 0, pr, :], sT[:, 1, pr, :], na8i, m1[:, 0:16], ALU.mult, ALU.add),
                 reads=["sT", "s5small", K("m1")], writes=["fin_s", K("m1")]); yield
            P.op("dve", lambda e: e.scalar_tensor_tensor(m2[:, 0:16], sT[:, 1, pr, :], a8r, Si[:, 256:272], ALU.mult, ALU.add),
                 reads=["sT", "s5small", K("Si")], writes=[K("m2")]); yield
            P.op("dve", lambda e: e.scalar_tensor_tensor(fin_s[:, 1, pr, :], sT[:, 0, pr, :], a8i, m2[:, 0:16], ALU.mult, ALU.add),
                 reads=["sT", "s5small", K("m2")], writes=["fin_s", K("m2")]); yield

        def c_gen(q):
            for j in range(8):
                bank = 2 + j % 2
                first = True
                for tau in range(j + 1):
                    P.op("pe", lambda e, q=q, j=j, tau=tau, bank=bank, first=first: e.matmul(pb[bank][:, 0:NCH], KC[:, q, tau, :], uT[:, q, j - tau, :], start=first, stop=False),
                         reads=["KC", "uT"], writes=["pb%d" % bank])
                    first = False
                yield
                for i in range(4):
                    pr = 4 * q + i
                    for ri in range(2):
                        last = (ri == 1)
                        P.op("pe", lambda e, j=j, ri=ri, pr=pr, i=i, bank=bank, last=last: e.matmul(pb[bank][32 * i:32 * i + 32, 0:NCH], LC[:, j, ri, pr, :], HS[:, pr, ri, :],
                                                                                                    start=False, stop=last, tile_position=(0, 32 * i)),
                             reads=["LC", "HSs", "HSp%d" % q, "HS0"], writes=["pb%d" % bank])
                P.op("act", lambda e, q=q, j=j, bank=bank: e.activation(gy5T[:, q, :, j], pb[bank][:, 0:NCH], AF.Gelu_apprx_tanh), reads=["pb%d" % bank], writes=["gy5T"]); yield


        done_q = 0
        pend_c = []
        for g_ in ([0, 1, 2], [3, 4, 5], [6, 7, 8], [9, 10, 11], [12, 13, 14], [15]):
            gens_ = [pair_gen(pr_, k_) for k_, pr_ in enumerate(g_)] + [c_gen(q_) for q_ in pend_c]
            pend_c = []
            rr(gens_)
            while done_q < 4 and 4 * done_q + 3 <= g_[-1]:
                pend_c.append(done_q)
                done_q += 1
        rr([c_gen(q_) for q_ in pend_c])

        st_out = st_in
        for ri, (dst_s, dst_p) in enumerate(((dr["sreo"], dr["pre"]), (dr["simo"], dr["pim"]))):
            so = st_out[ri]
            for pr in range(16):
                bank = 4 + pr // 4
                P.op("pe", lambda e, ri=ri, pr=pr, bank=bank: e.transpose(pb[bank][0:16, (pr % 4) * 128:(pr % 4 + 1) * 128], fin_s[:, ri, pr, :], ident[:]),
                     reads=["fin_s", "ident"], writes=["pb%d" % bank])
            for b4 in range(4):
                P.op("act", lambda e, so=so, b4=b4: e.copy(so[:, b4 * 512:(b4 + 1) * 512], pb[4 + b4][0:16, :]), reads=["pb%d" % (4 + b4)], writes=["st_in0"])
            P.dma("sp", dst_s, so[:], reads=["st_in0"], writes=["st_in0"])
            for gi in range(2):
                P.dma("sp", dst_p.rearrange("(pr gi) p -> gi p pr", gi=2)[gi], fin_p[gi * 64:(gi + 1) * 64, ri, :], reads=["fin_p"], allow_slow_non_contiguous=True)
        P.barrier()


def ffn_phase(nc, P, mk, sb, pb, ident, h2T, modT, tm_gate, fg_bc, pre, rs_all, junk, dr, cw, dump, stop_after=None):
    with contextlib.ExitStack() as ph:
        wdn = mk(ph, "wdn", [128, 22, D], BF16)

        def load_wdn(piece):
            h, kk = divmod(piece, 2)
            P.dma("pool", wdn[:, kk * 11:(kk + 1) * 11, h * 256:(h + 1) * 256], cw(dr["wdn"][kk * 1408:(kk + 1) * 1408, h * 256:(h + 1) * 256]), writes=["wdn"])

        NW = 2
        wug = [mk(ph, "wug%d" % i, [128, 8, 128], BF16) for i in range(NW)]
        wuv = [mk(ph, "wuv%d" % i, [128, 8, 128], BF16) for i in range(NW)]

        def load_up(i, slot):
            P.dma("pool", wug[slot][:], cw(dr["wup"][:, i * 128:(i + 1) * 128]), writes=["wug%d" % slot])
            P.dma("pool", wuv[slot][:], cw(dr["wup"][:, DFF + i * 128:DFF + (i + 1) * 128]), writes=["wuv%d" % slot])
        load_up(0, 0)
        nload = 1
        gf_p, gf_s, pastT, wc, bcv = pre
        cvo_s = mk(ph, "cvo_s", [128, 44, 32])
        cvo_p = mk(ph, "cvo_p", [128, 44, 2])
        carry = mk(ph, "carry", [128, 44, 2])
        carry2 = mk(ph, "carry2", [128, 44, 2])

        aT = mk(ph, "aT", [128, 22, 1152], BF16)
        Cg = [mk(ph, "Cg%d" % i, [128, 512]) for i in range(2)]
        Cv = [mk(ph, "Cv%d" % i, [128, 512]) for i in range(2)]
        Gg = [mk(ph, "Gg%d" % i, [128, 512]) for i in range(2)]
        Ux = [mk(ph, "Ux%d" % i, [128, 16, 10]) for i in range(2)]
        x1t = [mk(ph, "x1t%d" % i, [128, D]) for i in range(2)]
        yt = [mk(ph, "yt%d" % i, [128, D]) for i in range(2)]

        groups = [[0, 1], [2, 3, 4]]
        it = 0


        def up_s1(item):
            i, bi, slot, first_load, a0, n = item
            par = n % 2
            t0, tn = TBS[bi]
            if first_load is not None:
                load_up(*first_load)
                if n < 2 * 22 and 2 <= i < 10 and bi == 0:
                    load_wdn(i - 2)
            banks = (0, 1) if par == 0 else (2, 3)
            for half, wsl in ((0, wug[slot]), (1, wuv[slot])):
                wk = ("wug%d" if half == 0 else "wuv%d") % slot
                bank = banks[half]
                for k in range(8):
                    P.op("pe", lambda e, k=k, bank=bank, wsl=wsl: e.matmul(pb[bank][:, 0:tn], wsl[:, k, :], h2T[:, k, t0:t0 + tn], start=(k == 0), stop=(k == 7)),
                         reads=[wk, "B_dstT"], writes=["pb%d" % bank])
            info = []
            for half in range(2):
                c = i + 22 * half
                bank = banks[half]
                Ct = (Cg if half == 0 else Cv)[par]
                Ck = ("Cg%d" if half == 0 else "Cv%d") % par
                info.append((c, bank, "pb%d" % bank, Ct, Ck, wc[:, c, 0:1], wc[:, c, 1:2], wc[:, c, 2:3], bcv[:, c:c + 1]))
            if bi < 4:
                for (c, bank, bk, Ct, Ck, w0, w1, w2, bb) in info:
                    P.op("act", lambda e, bank=bank, Ct=Ct, w2=w2, bb=bb: e.activation(Ct[:, 0:tn], pb[bank][:, 0:tn], AF.Identity, bias=bb, scale=w2), reads=[bk, "wc", "bcv"], writes=[Ck])
                for (c, bank, bk, Ct, Ck, w0, w1, w2, bb) in info:
                    P.op("dve", lambda e, bank=bank, Ct=Ct, w1=w1: e.scalar_tensor_tensor(Ct[:, 1:tn], pb[bank][:, 0:tn - 1], w1, Ct[:, 1:tn], ALU.mult, ALU.add), reads=[bk, "wc", Ck], writes=[Ck])
                    P.op("dve", lambda e, bank=bank, Ct=Ct, w0=w0: e.scalar_tensor_tensor(Ct[:, 2:tn], pb[bank][:, 0:tn - 2], w0, Ct[:, 2:tn], ALU.mult, ALU.add), reads=[bk, "wc", Ck], writes=[Ck])
                    if bi < 3:
                        cw_ = carry if bi % 2 == 0 else carry2
                        P.op("dve", lambda e, c=c, bank=bank, cw_=cw_: e.tensor_copy(cw_[:, c, :], pb[bank][:, tn - 2:tn]), reads=[bk], writes=["carry%d_%d" % (bi % 2, c)])
                    else:
                        P.op("dve", lambda e, c=c, bank=bank: e.tensor_copy(cvo_p[:, c, :], pb[bank][:, tn - 2:tn]), reads=[bk], writes=["cvo_p"])
                if bi > 0:
                    cr_ = carry if (bi - 1) % 2 == 0 else carry2
                    for (c, bank, bk, Ct, Ck, w0, w1, w2, bb) in info:
                        ck = "carry%d_%d" % ((bi - 1) % 2, c)
                        P.op("dve", lambda e, c=c, Ct=Ct, w1=w1: e.scalar_tensor_tensor(Ct[:, 0:1], cr_[:, c, 1:2], w1, Ct[:, 0:1], ALU.mult, ALU.add), reads=[ck, "wc", Ck], writes=[Ck])
                    for (c, bank, bk, Ct, Ck, w0, w1, w2, bb) in info:
                        ck = "carry%d_%d" % ((bi - 1) % 2, c)
                        P.op("dve", lambda e, c=c, Ct=Ct, w0=w0: e.scalar_tensor_tensor(Ct[:, 0:2], cr_[:, c, 0:2], w0, Ct[:, 0:2], ALU.mult, ALU.add), reads=[ck, "wc", Ck], writes=[Ck])

            else:
                for hh, (c, bank, bk, Ct, Ck, w0, w1, w2, bb) in enumerate(info):
                    U = Ux[hh]; Uk = "Ux%d" % hh
                    C3 = Ct[:, 0:128].rearrange("p (n j) -> p n j", j=8)
                    P.op("act", lambda e, bank=bank, U=U: e.copy(U[:, :, 2:10], pb[bank][:, 0:128].rearrange("p (n j) -> p n j", j=8)), reads=[bk], writes=[Uk])
                    P.op("dve", lambda e, c=c, U=U: e.tensor_copy(U[:, :, 0:2], pastT[:, c, :].rearrange("p (n k) -> p n k", k=2)), reads=["pastT"], writes=[Uk])
                    P.op("dve", lambda e, U=U, C3=C3, w2=w2, bb=bb: e.tensor_scalar(C3, U[:, :, 2:10], w2, bb, ALU.mult, ALU.add), reads=[Uk, "wc", "bcv"], writes=[Ck])
                    P.op("dve", lambda e, U=U, C3=C3, w1=w1: e.scalar_tensor_tensor(C3, U[:, :, 1:9], w1, C3, ALU.mult, ALU.add), reads=[Uk, "wc", Ck], writes=[Ck])
                    P.op("dve", lambda e, U=U, C3=C3, w0=w0: e.scalar_tensor_tensor(C3, U[:, :, 0:8], w0, C3, ALU.mult, ALU.add), reads=[Uk, "wc", Ck], writes=[Ck])
                    P.op("act", lambda e, c=c, U=U: e.copy(cvo_s[:, c, :].rearrange("p (n k) -> p n k", k=2), U[:, :, 8:10]), reads=[Uk], writes=["cvo_s"])

        def up_s2(item):
            i, bi, slot, first_load, a0, n = item
            par = n % 2
            t0, tn = TBS[bi]
            G = Gg[par]; Gk = "Gg%d" % par
            Cgt = Cg[par]; Cvt = Cv[par]
            P.op("act", lambda e: e.activation(G[:, 0:tn], Cgt[:, 0:tn], AF.Gelu_apprx_tanh), reads=["Cg%d" % par], writes=[Gk])
            P.op("pool", lambda e: e.tensor_tensor(aT[:, i, t0 - a0:t0 - a0 + tn], G[:, 0:tn], Cvt[:, 0:tn], ALU.mult),
                 reads=[Gk, "Cv%d" % par], writes=["aT"])

        nblk = 0
        ntile = 0
        if stop_after == "F0":
            return
        for gidx, grp in enumerate(groups):
            a0 = TBS[grp[0]][0]
            items = []
            for i in range(22):
                slot = (nload - 1) % NW
                fl = None
                if i + 1 < 22:
                    fl = (i + 1, nload % NW); nload += 1
                for bi in grp:
                    items.append((i, bi, slot, fl, a0, nblk))
                    fl = None
                    nblk += 1
            LAG = 1
            for step in range(len(items) + LAG):
                if step < len(items):
                    up_s1(items[step])
                if step >= LAG:
                    up_s2(items[step - LAG])
            if stop_after == "F1":
                return
            if gidx + 1 < len(groups):
                load_up(0, nload % NW); nload += 1
            else:
                for k in range(2):
                    P.dma("sp", dr["pcv"][k].rearrange("(c p) -> p c", p=128), cvo_p[:, :, k], reads=["cvo_p"])
            for bi in grp:
                t0, tn = TBS[bi]
                for tt_ in range(tn // 128):
                    t = t0 // 128 + tt_
                    loc = t0 - a0 + tt_ * 128
                    par = ntile % 2
                    ntile += 1
                    x1 = x1t[par]; x1k = "x1t%d" % par
                    P.dma("sp", x1[:], dr["x1"][t * 128:(t + 1) * 128, :], reads=["x1_d%d" % t], writes=[x1k])
                    dbanks = (4, 5) if par == 0 else (6, 7)
                    for h in range(2):
                        bank = dbanks[h]
                        for kc in range(22):
                            P.op("pe", lambda e, kc=kc, h=h, loc=loc, bank=bank: e.matmul(pb[bank][:], aT[:, kc, loc:loc + 128], wdn[:, kc, h * 512:(h + 1) * 512], start=(kc == 0), stop=(kc == 21)),
                                 reads=["aT", "wdn"], writes=["pb%d" % bank])
                    y = yt[par]; yk = "yt%d" % par
                    g = gf_p if t < 16 else gf_s
                    for h in range(2):
                        P.op("dve", lambda e, h=h, y=y, g=g, dbanks=dbanks: e.tensor_tensor(y[:, h * 512:(h + 1) * 512], pb[dbanks[h]][:], g[:, h * 512:(h + 1) * 512], ALU.mult),
                             reads=["pb%d" % dbanks[h], "gf_p", "gf_s"], writes=[yk])
                    P.op("pool", lambda e, y=y, x1=x1: e.tensor_tensor(y[:], y[:], x1[:], ALU.add), reads=[yk, x1k], writes=[yk])
                    ss = rs_all[:, 2 * NT + t: 2 * NT + t + 1]
                    ssk = "F_ss%d" % t
                    P.op("act", lambda e, y=y, ss=ss: e.activation(junk[:], y[:], AF.Square, accum_out=ss), reads=[yk], writes=["junk", ssk])
                    P.op("dve", lambda e, ss=ss: e.tensor_scalar(ss, ss, 1.0 / D, EPS, ALU.mult, ALU.add), reads=[ssk], writes=[ssk])
                    P.op("act", lambda e, ss=ss: e.activation(ss, ss, AF.Sqrt), reads=[ssk], writes=[ssk])
                    P.op("dve", lambda e, ss=ss: e.reciprocal(ss, ss), reads=[ssk], writes=[ssk])
                    P.op("dve", lambda e, y=y, ss=ss: e.scalar_tensor_tensor(y[:], y[:], ss, fg_bc[:], ALU.mult, ALU.mult), reads=[yk, ssk, "fg_bc"], writes=[yk])
                    P.dma("sp", dr["y"][t * 128:(t + 1) * 128, :], y[:], reads=[yk])

        so = [mk(ph, "cv_so%d" % i, [32, 512]) for i in range(2)]
        for c in range(44):
            bank = 4 + c // 4 % 4
            P.op("pe", lambda e, c=c, bank=bank: e.transpose(pb[bank][0:32, (c % 4) * 128:(c % 4 + 1) * 128], cvo_s[:, c, :], ident[:]),
                 reads=["cvo_s", "ident"], writes=["pb%d" % bank])
            if c % 4 == 3:
                c0 = c - 3
                sx = so[(c // 4) % 2]; sk = "cv_so%d" % ((c // 4) % 2)
                P.op("act", lambda e, sx=sx, bank=bank: e.copy(sx[:], pb[bank][0:32, :]), reads=["pb%d" % bank], writes=[sk])
                P.dma("sp", dr["scvo"][:, c0 * 128:(c0 + 4) * 128], sx[:], reads=[sk])


_NC_CACHE = {}


def _get_nc():
    if "nc" not in _NC_CACHE:
        _NC_CACHE["nc"] = build_program()
    return _NC_CACHE["nc"]


def make_in_maps(inputs):
    f = lambda a: np.ascontiguousarray(np.asarray(a, dtype=np.float32))
    shared = {}
    for k in ("norm1_g", "norm2_g", "w_ada", "b_ada", "w_in", "s5_a_re", "s5_a_im", "s5_log_dt", "s5_b_re", "s5_b_im",
              "s5_c_re", "s5_c_im", "s5_d", "w_s5_glu", "b_s5_glu", "gm_ln_g", "gm_ln_b", "gm_w_sp", "gm_b_sp",
              "w_gm_out", "w_out", "w_up", "w_conv", "b_conv", "w_down"):
        shared[k] = f(np.asarray(inputs[k])[0])
    shared["final_g"] = f(inputs["final_g"])
    xp = np.asarray(inputs["x_prompt"]); xs = np.asarray(inputs["x_sample"])
    cp = np.asarray(inputs["c_prompt"]); cs = np.asarray(inputs["c_sample"])
    sre = np.asarray(inputs["state_ssm_re"])[0]; sim = np.asarray(inputs["state_ssm_im"])[0]
    scv = np.asarray(inputs["state_ffn_conv"])[0]
    maps = []
    for i in range(NCORES):
        sl = slice(16 * i, 16 * i + 16)
        m = dict(shared)
        m["x"] = f(np.concatenate([xp[i], xs[sl].reshape(128, D)], axis=0))
        m["c"] = f(np.concatenate([cp[i:i + 1], cs[sl]], axis=0))
        m["sre"] = f(sre[sl].reshape(16, 2048))
        m["sim"] = f(sim[sl].reshape(16, 2048))
        m["scv"] = f(scv[sl].reshape(32, 2 * DFF))
        maps.append(m)
    return maps


def kernel(**inputs):
    nc = _get_nc()
    maps = make_in_maps(inputs)
    res = run_bass_kernel_spmd(nc, maps, core_ids=list(range(NCORES)))
    r = res.results
    y_p = np.stack([r[i]["y"][:2048] for i in range(NCORES)], axis=0)
    y_s = np.concatenate([r[i]["y"][2048:].reshape(16, 8, D) for i in range(NCORES)], axis=0)
    p_re = np.stack([r[i]["p_re"] for i in range(NCORES)], axis=0)[None]
    p_im = np.stack([r[i]["p_im"] for i in range(NCORES)], axis=0)[None]
    p_cv = np.stack([r[i]["p_conv"] for i in range(NCORES)], axis=0)[None]
    s_re = np.concatenate([r[i]["s_re"].reshape(16, 32, 64) for i in range(NCORES)], axis=0)[None]
    s_im = np.concatenate([r[i]["s_im"].reshape(16, 32, 64) for i in range(NCORES)], axis=0)[None]
    s_cv = np.concatenate([r[i]["s_conv"].reshape(16, 2, 2 * DFF) for i in range(NCORES)], axis=0)[None]
    s_v = np.concatenate([r[i]["s_v"].reshape(16, 8, 512) for i in range(NCORES)], axis=0)[None]
    outs = (y_p, y_s, p_re, p_im, p_cv, s_re, s_im, s_cv, s_v)
    return tuple(np.ascontiguousarray(o, dtype=np.float32) for o in outs)
```

```python
import contextlib
import math
import numpy as np
import concourse.bass as bass
import concourse.mybir as mybir
from concourse.bass_utils import run_bass_kernel_spmd

F32 = mybir.dt.float32
BF16 = mybir.dt.bfloat16
I32 = mybir.dt.int32
AF = mybir.ActivationFunctionType
ALU = mybir.AluOpType

NCORES = 8
D = 1024
T = 2176
NT = 17
NCH = 272
DFF = 2816
EPS = 1e-6
TBS = [(0, 512), (512, 512), (1024, 512), (1536, 512), (2048, 128)]

EPOCH = 6000
H1 = True
DMA_RING = 12
SAME_ENGINE_SYNC = True


class Prog:
    ENG = ("pe", "act", "dve", "pool", "sp")

    def __init__(self, nc, stack):
        self.nc = nc
        self.stack = stack
        self.streams = {e: [] for e in self.ENG}
        self.count = {e: 0 for e in self.ENG}
        self.sems = {}
        self.waited = {e: {} for e in self.ENG}
        self.last_w = {}
        self.readers = {}
        self.dma_n = {e: 0 for e in self.ENG}
        self.dma_val = {}

    def _sem(self, key):
        if key not in self.sems:
            name = "s_" + "_".join(str(k) for k in key)
            self.sems[key] = self.stack.enter_context(self.nc.semaphore(name))
        return self.sems[key]

    def _add_wait(self, eng, tok, waits):
        key, val = tok
        if key[0] == "eng" and key[1] == eng:
            if eng == "pe" or not SAME_ENGINE_SYNC:
                return
        if self.waited[eng].get(key, 0) >= val:
            return
        self.waited[eng][key] = val
        waits.append((key, val))

    def _deps(self, eng, reads, writes, waits):
        best = {}

        def need(t):
            if t is not None and best.get(t[0], 0) < t[1]:
                best[t[0]] = t[1]
        for k in reads:
            need(self.last_w.get(k))
        for k in writes:
            need(self.last_w.get(k))
            for t in self.readers.get(k, ()):
                need(t)
        for key, val in best.items():
            self._add_wait(eng, (key, val), waits)

    def _record(self, tok, reads, writes):
        for k in reads:
            self.readers.setdefault(k, []).append(tok)
        for k in writes:
            self.last_w[k] = tok
            self.readers[k] = []

    def op(self, eng, fn, reads=(), writes=()):
        waits = []
        self._deps(eng, reads, writes, waits)
        self.count[eng] += 1
        n = self.count[eng]
        key = ("eng", eng, (n - 1) // EPOCH)
        val = (n - 1) % EPOCH + 1
        self._sem(key)
        self.streams[eng].append((waits, fn, key, 1))
        self._record((key, val), reads, writes)

    def dma(self, eng, out, in_, reads=(), writes=(), **kw):
        waits = []
        self._deps(eng, reads, writes, waits)
        slot = self.dma_n[eng] % DMA_RING
        self.dma_n[eng] += 1
        key = ("dma", eng, slot)
        prev = self.dma_val.get(key, 0)
        if prev:
            self._add_wait(eng, (key, prev), waits)
        val = prev + 16
        self.dma_val[key] = val
        self._sem(key)

        kw.setdefault("allow_slow_non_contiguous", True)

        def fn(e, out=out, in_=in_, kw=kw):
            return e.dma_start(out=out, in_=in_, **kw)
        self.streams[eng].append((waits, fn, key, 16))
        self._record((key, val), reads, writes)

    def _all_tokens(self):
        toks = [(k, v) for k, v in self.dma_val.items()]
        for e in self.ENG:
            n = self.count[e]
            if n:
                toks.append((("eng", e, (n - 1) // EPOCH), (n - 1) % EPOCH + 1))
        return toks

    def barrier(self):
        toks = self._all_tokens()
        for e in self.ENG:
            waits = []
            for t in toks:
                if t[0][0] == "eng" and t[0][1] == e:
                    continue
                self._add_wait(e, t, waits)
            if waits:
                self.streams[e].append((waits, None, None, 0))

    def finish(self):
        waits = []
        for t in self._all_tokens():
            self._add_wait("sp", t, waits)
        self.streams["sp"].append((waits, None, None, 0))

    def replay(self):
        nc = self.nc
        sems = self.sems
        streams = self.streams

        def run(e, items):
            for waits, fn, key, inc in items:
                for wkey, wval in waits:
                    e.wait_ge(sems[wkey], wval)
                if fn is not None:
                    fn(e).then_inc(sems[key], inc)

        with nc.Block() as block:
            @block.tensor
            def _(e):
                run(e, streams["pe"])

            @block.scalar
            def _(e):
                run(e, streams["act"])

            @block.vector
            def _(e):
                run(e, streams["dve"])

            @block.gpsimd
            def _(e):
                run(e, streams["pool"])

            @block.sync
            def _(e):
                run(e, streams["sp"])


class Rec:
    def __init__(self):
        self.items = []

    def op(self, eng, fn, reads=(), writes=()):
        self.items.append(("op", eng, fn, list(reads), list(writes)))

    def dma(self, eng, out, in_, reads=(), writes=(), **kw):
        self.items.append(("dma", eng, out, in_, list(reads), list(writes), kw))

    def barrier(self):
        pass

    def gen(self, P, every=1):
        for n, it in enumerate(self.items):
            if it[0] == "op":
                P.op(it[1], it[2], reads=it[3], writes=it[4])
            else:
                P.dma(it[1], it[2], it[3], reads=it[4], writes=it[5], **it[6])
            if n % every == every - 1:
                yield


def build_program(stop_after=None, dumps=()):
    nc = bass.Bass("TRN2", target_bir_lowering=False)

    def din(name, shape):
        return nc.dram_tensor(name, list(shape), F32, kind="ExternalInput").ap()

    def dout(name, shape):
        return nc.dram_tensor(name, list(shape), F32, kind="ExternalOutput").ap()

    x_d = din("x", [T, D])
    c_d = din("c", [17, D])
    sre_d = din("sre", [16, 2048])
    sim_d = din("sim", [16, 2048])
    scv_d = din("scv", [32, 2 * DFF])
    g1_d = din("norm1_g", [D])
    g2_d = din("norm2_g", [D])
    wada_d = din("w_ada", [D, 6 * D])
    bada_d = din("b_ada", [6 * D])
    win_d = din("w_in", [D, 3584])
    are_d = din("s5_a_re", [32, 64])
    aim_d = din("s5_a_im", [32, 64])
    ldt_d = din("s5_log_dt", [32])
    bre_d = din("s5_b_re", [32, 64, 16])
    bim_d = din("s5_b_im", [32, 64, 16])
    cre_d = din("s5_c_re", [32, 16, 64])
    cim_d = din("s5_c_im", [32, 16, 64])
    sd_d = din("s5_d", [32, 16])
    wglu_d = din("w_s5_glu", [512, 2048])
    bglu_d = din("b_s5_glu", [2048])
    lng_d = din("gm_ln_g", [512])
    lnb_d = din("gm_ln_b", [512])
    wsp_d = din("gm_w_sp", [8, 128, 128])
    bsp_d = din("gm_b_sp", [8, 128])
    wgo_d = din("w_gm_out", [512, D])
    wout_d = din("w_out", [D, D])
    wup_d = din("w_up", [D, 2 * DFF])
    wcv_d = din("w_conv", [3, 2 * DFF])
    bcv_d = din("b_conv", [2 * DFF])
    wdn_d = din("w_down", [DFF, D])
    fg_d = din("final_g", [D])

    y_d = dout("y", [T, D])
    pre_d = dout("p_re", [32, 64])
    pim_d = dout("p_im", [32, 64])
    pcv_d = dout("p_conv", [2, 2 * DFF])
    sreo_d = dout("s_re", [16, 2048])
    simo_d = dout("s_im", [16, 2048])
    scvo_d = dout("s_conv", [32, 2 * DFF])
    sv_d = dout("s_v", [128, 512])
    x1_d = nc.dram_tensor("x1_scratch", [T, D], F32, kind="Internal").ap()

    dump_aps = {}

    with contextlib.ExitStack() as st:
        P = Prog(nc, st)

        def mk(stack, name, shape, dt=F32):
            return stack.enter_context(nc.sbuf_tensor(name, list(shape), dt))

        def sb(name, shape, dt=F32):
            return mk(st, name, shape, dt)

        def dump(name, ap, shape, reads, dt=F32):
            if name in dumps:
                d = nc.dram_tensor("dbg_" + name, list(shape), dt, kind="ExternalOutput").ap()
                P.dma("sp", d, ap, reads=reads)

        pb = [st.enter_context(nc.psum_tensor("pb%d" % i, [128, 512], F32)) for i in range(8)]

        def cw(wdr):
            return wdr.rearrange("(k p) n -> p k n", p=128)

        ident = sb("ident", [128, 128])
        io_i = sb("io_i", [128, 128], I32)
        P.op("pool", lambda e: e.iota(io_i[:], [[1, 128]], base=0, channel_multiplier=-1), writes=["io_i"])
        P.op("dve", lambda e: e.tensor_single_scalar(ident[:], io_i[:], 0, ALU.is_equal), reads=["io_i"], writes=["ident"])

        scr = dict(
            LA=nc.dram_tensor("scr_LA", [128, 8192], BF16, kind="Internal").ap(),
            LC=nc.dram_tensor("scr_LC", [128, 8192], BF16, kind="Internal").ap(),
            KC=nc.dram_tensor("scr_KC", [128, 4096], BF16, kind="Internal").ap(),
            tab=nc.dram_tensor("scr_tab", [128, 2, 16, 256], F32, kind="Internal").ap(),
            small=nc.dram_tensor("scr_small", [128, 4, 16], F32, kind="Internal").ap())
        s5dr = dict(are=are_d, aim=aim_d, ldt=ldt_d, bre=bre_d, bim=bim_d, cre=cre_d, cim=cim_d, sd=sd_d,
                    sre=sre_d, sim=sim_d, sreo=sreo_d, simo=simo_d, pre=pre_d, pim=pim_d)
        TH8 = sb("TH8", [128, 16])
        pastT = sb("pastT", [128, 44, 32])
        gf_p = sb("gf_p", [128, D])
        gf_s = sb("gf_s", [128, D])
        wc = sb("wc", [128, 44, 3])
        bcv = sb("bcv", [128, 44])
        modT = sb("modT", [128, 48, 17])
        gs1 = sb("gs1", [128, 8, 17])
        gs2 = sb("gs2", [128, 8, 17])
        g1T = sb("g1T", [128, 8])
        g2T = sb("g2T", [128, 8])
        bT = sb("bT", [128, 48])
        fg_bc = sb("fg_bc", [128, D])
        with contextlib.ExitStack() as ph:
            P_real = P
            P = Rec()
            ct = mk(ph, "ct", [17, D])
            cT = mk(ph, "cT", [128, 8, 17], BF16)
            wad = [mk(ph, "wad%d" % i, [128, 8, 128], BF16) for i in range(4)]
            P.dma("sp", ct[:], c_d, writes=["ct"])
            P.dma("act", bT[:], bada_d.rearrange("(b p) -> p b", p=128), writes=["bT"], allow_slow_non_contiguous=True)
            P.dma("act", g1T[:], g1_d.rearrange("(k p) -> p k", p=128), writes=["g1T"], allow_slow_non_contiguous=True)
            P.dma("act", g2T[:], g2_d.rearrange("(k p) -> p k", p=128), writes=["g2T"], allow_slow_non_contiguous=True)
            P.dma("sp", fg_bc[:], fg_d.rearrange("(o n) -> o n", o=1).to_broadcast([128, D]), writes=["fg_bc"])
            P.op("act", lambda e: e.activation(ct[:], ct[:], AF.Silu), reads=["ct"], writes=["ct"])
            for k in range(8):
                P.op("pe", lambda e, k=k: e.transpose(pb[0][:, k * 32:k * 32 + 17], ct[:, k * 128:(k + 1) * 128], ident[0:17, 0:17]),
                     reads=["ct", "ident"], writes=["pb0"])
            P.op("act", lambda e: e.copy(cT[:], pb[0][:, 0:256].rearrange("p (k c) -> p k c", c=32)[:, :, 0:17]),
                 reads=["pb0"], writes=["cT"])
            for blk in range(48):
                w = wad[blk % 4]
                wk = "wad%d" % (blk % 4)
                P.dma("pool", w[:], cw(wada_d[:, blk * 128:(blk + 1) * 128]), writes=[wk])
                bank = 1 + blk % 2
                for k in range(8):
                    P.op("pe", lambda e, k=k, w=w, bank=bank: e.matmul(pb[bank][:, 0:17], w[:, k, :], cT[:, k, :], start=(k == 0), stop=(k == 7)),
                         reads=[wk, "cT"], writes=["pb%d" % bank])
                P.op("act", lambda e, blk=blk, bank=bank: e.activation(modT[:, blk, :], pb[bank][:, 0:17], AF.Identity, bias=bT[:, blk:blk + 1]),
                     reads=["pb%d" % bank, "bT"], writes=["modT"])
            for k in range(8):
                P.op("dve", lambda e, k=k: e.tensor_scalar(gs1[:, k, :], modT[:, 8 + k, :], 1.0, g1T[:, k:k + 1], ALU.add, ALU.mult),
                     reads=["modT", "g1T"], writes=["gs1"])
                P.op("dve", lambda e, k=k: e.tensor_scalar(gs2[:, k, :], modT[:, 32 + k, :], 1.0, g2T[:, k:k + 1], ALU.add, ALU.mult),
                     reads=["modT", "g2T"], writes=["gs2"])
            rec_mod = P
            P = Rec()
            s5_prep(nc, P, mk, ph, pb, ident, s5dr, scr, TH8)
            rec_prep = P
            P = P_real
            gens = [rec_mod.gen(P), rec_prep.gen(P)]
            while gens:
                for g_ in list(gens):
                    try:
                        next(g_)
                    except StopIteration:
                        gens.remove(g_)
            past_in = mk(ph, "past_in", [32, 2 * DFF])
            P.dma("sp", past_in[:], scv_d, writes=["past_in"])
            for c in range(44):
                bank = c // 16
                P.op("pe", lambda e, c=c, bank=bank: e.transpose(pb[bank][:, (c % 16) * 32:(c % 16 + 1) * 32], past_in[:, c * 128:(c + 1) * 128], ident[0:32, 0:32]),
                     reads=["past_in", "ident"], writes=["pb%d" % bank])
            for bank in range(3):
                n_ = 16 if bank < 2 else 12
                P.op("dve", lambda e, bank=bank, n_=n_: e.tensor_copy(pastT[:, bank * 16:bank * 16 + n_, :], pb[bank][:, 0:n_ * 32].rearrange("p (c m) -> p c m", m=32)),
                     reads=["pb%d" % bank], writes=["pastT"])
            P.barrier()
        dump("modT", modT[:], [128, 48, 17], ["modT"])

        def tm_gate(dst_p, dst_s, blk0, tag):
            for k in range(8):
                P.op("dve", lambda e, k=k: e.tensor_copy(bc_p[:], modT[:, blk0 + k, 0:1].to_broadcast([128, 128])),
                     reads=["modT"], writes=["bc_p"])
                P.op("dve", lambda e, k=k: e.tensor_copy(bc_s[:].rearrange("p (n j) -> p n j", j=8),
                                                         modT[:, blk0 + k, 1:17].unsqueeze(2).to_broadcast([128, 16, 8])),
                     reads=["modT"], writes=["bc_s"])
                P.op("pe", lambda e, k=k: e.matmul(pb[0][:, k * 128:(k + 1) * 128] if k < 4 else pb[1][:, (k - 4) * 128:(k - 3) * 128],
                                                   bc_p[:], ident[:], start=True, stop=True),
                     reads=["bc_p", "ident"], writes=["pb0" if k < 4 else "pb1"])
                P.op("pe", lambda e, k=k: e.matmul(pb[2][:, k * 128:(k + 1) * 128] if k < 4 else pb[3][:, (k - 4) * 128:(k - 3) * 128],
                                                   bc_s[:], ident[:], start=True, stop=True),
                     reads=["bc_s", "ident"], writes=["pb2" if k < 4 else "pb3"])
            for h in range(2):
                P.op("act", lambda e, h=h: e.copy(dst_p[:, h * 512:(h + 1) * 512], pb[h][:]), reads=["pb%d" % h], writes=[tag + "_p"])
                P.op("act", lambda e, h=h: e.copy(dst_s[:, h * 512:(h + 1) * 512], pb[2 + h][:]), reads=["pb%d" % (2 + h)], writes=[tag + "_s"])

        bc_p = sb("bc_p", [128, 128])
        bc_s = sb("bc_s", [128, 128])
        tm_gate(gf_p, gf_s, 40, "gf")

        hT = sb("hT", [128, 8, T], BF16)
        rs_all = sb("rs_all", [128, 4 * NT])

        def run_pipeline(stages, n):
            for _ in pipeline_gen(stages, n):
                pass

        def pipeline_gen(stages, n):
            nst = len(stages)
            for step in range(n + nst - 1):
                gens = []
                for s_, f in enumerate(stages):
                    idx = step - s_
                    if 0 <= idx < n:
                        gens.append(f(idx))
                while gens:
                    for g_ in list(gens):
                        try:
                            next(g_)
                        except StopIteration:
                            gens.remove(g_)
                    yield

        def norm_stages(srcf, dstT, gs, shblk, ssk, tagp):
            def n1(t):
                src, srck = srcf(t)
                ss = rs_all[:, ssk * NT + t: ssk * NT + t + 1]
                sk = tagp + "ss%d" % t
                P.op("act", lambda e: e.activation(junk[:], src, AF.Square, accum_out=ss), reads=[srck], writes=["junk", sk]); yield
                P.op("dve", lambda e: e.tensor_scalar(ss, ss, 1.0 / D, EPS, ALU.mult, ALU.add), reads=[sk], writes=[sk]); yield
                P.op("act", lambda e: e.activation(ss, ss, AF.Sqrt), reads=[sk], writes=[sk]); yield
                P.op("dve", lambda e: e.reciprocal(ss, ss), reads=[sk], writes=[sk]); yield

            def n2(t):
                src, srck = srcf(t)
                ss = rs_all[:, ssk * NT + t: ssk * NT + t + 1]
                sk = tagp + "ss%d" % t
                xn = rings["xn"][t % 2]
                xnk = "xn%d" % (t % 2)
                P.op("pool", lambda e: e.tensor_scalar(xn[:], src, ss, 0.0, ALU.mult, ALU.add), reads=[srck, sk], writes=[xnk]); yield
                b0 = 4 + 2 * (t % 2)
                for k in range(8):
                    bank = b0 + k // 4
                    P.op("pe", lambda e, k=k, bank=bank: e.transpose(pb[bank][:, (k % 4) * 128:(k % 4 + 1) * 128], xn[:, k * 128:(k + 1) * 128], ident[:]),
                         reads=[xnk, "ident"], writes=["pb%d" % bank])
                    if k % 4 == 3:
                        yield

            def n3(t):
                b0 = 4 + 2 * (t % 2)
                for k in range(8):
                    bank = b0 + k // 4
                    src_ps = pb[bank][:, (k % 4) * 128:(k % 4 + 1) * 128]
                    dst = dstT[:, k, t * 128:(t + 1) * 128]
                    if t < 16:
                        if k % 2 == 0:
                            P.op("act", lambda e, k=k, src_ps=src_ps, dst=dst: e.activation(dst, src_ps, AF.Identity, bias=modT[:, shblk + k, 0:1], scale=gs[:, k, 0:1]),
                                 reads=["pb%d" % bank, "modT", "gs1", "gs2"], writes=[tagp + "dstT"])
                        else:
                            P.op("dve", lambda e, k=k, src_ps=src_ps, dst=dst: e.tensor_scalar(dst, src_ps, gs[:, k, 0:1], modT[:, shblk + k, 0:1], ALU.mult, ALU.add),
                                 reads=["pb%d" % bank, "modT", "gs1", "gs2"], writes=[tagp + "dstT"])
                    else:
                        tm = tmp128[k % 2]; tk = "tmp128_%d" % (k % 2)
                        P.op("dve", lambda e, k=k, src_ps=src_ps, tm=tm: e.tensor_tensor(tm[:].rearrange("p (n j) -> p n j", j=8),
                                                                                          src_ps.rearrange("p (n j) -> p n j", j=8),
                                                                                          gs[:, k, 1:17].unsqueeze(2).to_broadcast([128, 16, 8]), ALU.mult),
                             reads=["pb%d" % bank, "gs1", "gs2"], writes=[tk])
                        P.op("dve", lambda e, k=k, dst=dst, tm=tm: e.tensor_tensor(dst.rearrange("p (n j) -> p n j", j=8),
                                                                                    tm[:].rearrange("p (n j) -> p n j", j=8),
                                                                                    modT[:, shblk + k, 1:17].unsqueeze(2).to_broadcast([128, 16, 8]), ALU.add),
                             reads=[tk, "modT"], writes=[tagp + "dstT"])
                    yield
            return [n1, n2, n3]

        junk = sb("junk", [128, D], BF16)
        tmp128 = [sb("tmp128_%d" % i, [128, 128]) for i in range(2)]
        rings = {}

        with contextlib.ExitStack() as phT, contextlib.ExitStack() as ph:
            rec_tab = Rec()
            s5_tables(nc, rec_tab, mk, phT, TH8, scr)
            rings["xn"] = [mk(ph, "xn%d" % i, [128, D]) for i in range(2)]
            xt_ring = [mk(ph, "xt%d" % i, [128, D]) for i in range(4)]
            def a0(t):
                P.dma("sp", xt_ring[t % 4][:], x_d[t * 128:(t + 1) * 128, :], writes=["xt%d" % (t % 4)]); yield
            gens = [(pipeline_gen([a0] + norm_stages(lambda t: (xt_ring[t % 4][:], "xt%d" % (t % 4)), hT, gs1, 0, 0, "A_"), NT), 2), (rec_tab.gen(P), 1)]
            while gens:
                for g_ in list(gens):
                    try:
                        for _ in range(g_[1]):
                            next(g_[0])
                    except StopIteration:
                        gens.remove(g_)
            P.barrier()
        dump("hT", hT[:], [128, 8, T], ["A_dstT"], BF16)
        if stop_after == "A":
            P.finish(); P.replay(); return nc

        mix = contextlib.ExitStack()
        uT = mk(mix, "uT", [128, 4, 8, NCH], BF16)
        gy5T = mk(mix, "gy5T", [128, 4, NCH, 8], BF16)
        gy5Tf = gy5T[:].rearrange("p q c s -> p q (c s)")
        with contextlib.ExitStack() as ph:
            wU = mk(ph, "wU", [128, 8, 512], BF16)
            P.dma("pool", wU[:], cw(win_d[:, 0:512]), writes=["wU"])
            n = 0
            for q in range(4):
                for (t0, tn) in TBS:
                    bank = n % 4
                    n += 1
                    for k in range(8):
                        P.op("pe", lambda e, k=k, q=q, t0=t0, tn=tn, bank=bank: e.matmul(pb[bank][:, 0:tn], wU[:, k, q * 128:(q + 1) * 128], hT[:, k, t0:t0 + tn], start=(k == 0), stop=(k == 7)),
                             reads=["wU", "A_dstT"], writes=["pb%d" % bank])
                    eng = "act" if n % 2 else "dve"
                    if eng == "act":
                        P.op("act", lambda e, q=q, t0=t0, tn=tn, bank=bank: e.copy(uT[:, q, :, t0 // 8:(t0 + tn) // 8], pb[bank][:, 0:tn].rearrange("p (c s) -> p s c", s=8)), reads=["pb%d" % bank], writes=["uT"])
                    else:
                        P.op("dve", lambda e, q=q, t0=t0, tn=tn, bank=bank: e.tensor_copy(uT[:, q, :, t0 // 8:(t0 + tn) // 8], pb[bank][:, 0:tn].rearrange("p (c s) -> p s c", s=8)), reads=["pb%d" % bank], writes=["uT"])
            P.barrier()

        fin_p = mk(mix, "fin_p", [128, 2, 16])
        s5_phase(nc, P, mk, sb, pb, ident, uT, gy5T, fin_p, s5dr, scr, dump, dumps)
        dump("gy5T", gy5Tf, [128, 4, T], ["gy5T"], BF16)
        if stop_after == "S5":
            P.finish(); P.replay(); mix.close(); return nc

        ygT = mk(mix, "ygT", [128, 4, T], BF16)
        with contextlib.ExitStack() as ph:
            wGM = mk(ph, "wGM", [128, 8, 1024], BF16)
            P.dma("pool", wGM[:, :, 0:512], cw(win_d[:, 512:1024]), writes=["wGMu"])
            P.dma("pool", wGM[:, :, 512:1024], cw(win_d[:, 1024:1536]), writes=["wGMv"])
            lng_bc = mk(ph, "lng_bc", [128, 512])
            lnb_bc = mk(ph, "lnb_bc", [128, 512])
            P.dma("sp", lng_bc[:], lng_d.rearrange("(o n) -> o n", o=1).to_broadcast([128, 512]), writes=["lng"])
            P.dma("sp", lnb_bc[:], lnb_d.rearrange("(o n) -> o n", o=1).to_broadcast([128, 512]), writes=["lnb"])
            wsp_n = mk(ph, "wsp_n", [128, 8, 128])
            P.dma("sp", wsp_n[:], wsp_d.rearrange("h t s -> t h s"), writes=["wsp_n"])
            WmT = mk(ph, "WmT", [128, 8, 128], BF16)
            WmT32 = mk(ph, "WmT32", [128, 8, 128])
            for h in range(8):
                bank = h // 4
                P.op("pe", lambda e, h=h, bank=bank: e.transpose(pb[bank][:, (h % 4) * 128:(h % 4 + 1) * 128], wsp_n[:, h, :], ident[:]),
                     reads=["wsp_n", "ident"], writes=["pb%d" % bank])
            for bank in range(2):
                P.op("act", lambda e, bank=bank: e.copy(WmT32[:, bank * 4:(bank + 1) * 4, :], pb[bank][:].rearrange("p (h t) -> p h t", t=128)),
                     reads=["pb%d" % bank], writes=["WmT32"])
            P.op("pool", lambda e: e.affine_select(WmT32[:], WmT32[:], [[0, 8], [1, 128]], ALU.is_ge, 0.0, base=0, channel_multiplier=-1),
                 reads=["WmT32"], writes=["WmT32"])
            P.op("act", lambda e: e.copy(WmT[:], WmT32[:]), reads=["WmT32"], writes=["WmT"])
            WmS = mk(ph, "WmS", [128, 8, 128], BF16)
            bm8 = mk(ph, "bm8", [128, 128])
            ti1 = mk(ph, "ti1", [128, 128], I32)
            ti2 = mk(ph, "ti2", [128, 128], I32)
            E8 = mk(ph, "E8", [8, 128])
            A1 = mk(ph, "A1", [8, 8, 128])
            P.op("pool", lambda e: e.iota(ti1[:], [[1, 16], [0, 8]], base=0, channel_multiplier=0), writes=["ti1"])
            P.op("pool", lambda e: e.iota(ti2[:], [[0, 128]], base=0, channel_multiplier=1), writes=["ti2"])
            P.op("dve", lambda e: e.tensor_single_scalar(ti2[:], ti2[:], 3, ALU.arith_shift_right), reads=["ti2"], writes=["ti2"])
            P.op("dve", lambda e: e.tensor_tensor(bm8[:], ti1[:], ti2[:], ALU.is_equal), reads=["ti1", "ti2"], writes=["bm8"])
            P.op("pool", lambda e: e.iota(ti1[0:8, :], [[0, 16], [1, 8]], base=0, channel_multiplier=-1), reads=["bm8"], writes=["ti1"])
            P.op("dve", lambda e: e.tensor_single_scalar(E8[:], ti1[0:8, :], 0, ALU.is_equal), reads=["ti1"], writes=["E8"])
            for h in range(8):
                P.op("dve", lambda e, h=h: e.tensor_copy(A1[:, h, :].rearrange("p (n j) -> p n j", j=8),
                                                         WmT32[0:8, h, 0:8].unsqueeze(1).to_broadcast([8, 16, 8])),
                     reads=["WmT32"], writes=["A1"])
            for h in range(8):
                bank = h // 4
                P.op("pe", lambda e, h=h, bank=bank: e.matmul(pb[bank][:, (h % 4) * 128:(h % 4 + 1) * 128], E8[:], A1[:, h, :], start=True, stop=True),
                     reads=["E8", "A1"], writes=["pb%d" % bank])
            for h in range(8):
                bank = h // 4
                P.op("dve", lambda e, h=h, bank=bank: e.tensor_tensor(WmS[:, h, :], pb[bank][:, (h % 4) * 128:(h % 4 + 1) * 128], bm8[:], ALU.mult),
                     reads=["pb%d" % bank, "bm8"], writes=["WmS"])
            bsp = mk(ph, "bsp", [128, 4, 128])
            for h in range(8):
                P.dma("sp", bsp[(h % 2) * 64:(h % 2 + 1) * 64, h // 2, :], bsp_d[h:h + 1, :].to_broadcast([64, 128]), writes=["bsp"])

            gu_blk = [mk(ph, "gu_blk%d" % i, [128, 4, 512]) for i in range(3)]
            vg = [mk(ph, "vg%d" % i, [128, 512]) for i in range(4)]
            vl = [mk(ph, "vl%d" % i, [128, 512]) for i in range(2)]
            vb = [mk(ph, "vb%d" % i, [128, 512], BF16) for i in range(2)]
            st6 = [mk(ph, "st6_%d" % i, [128, 6]) for i in range(3)]
            mv = [mk(ph, "mv%d" % i, [128, 2]) for i in range(3)]
            stmp = [mk(ph, "stmp%d" % i, [128, 4, 128]) for i in range(2)]

            def g0(t):
                bi = t // 4
                t0, tn = TBS[bi]
                todo = [(0, q) for q in range(4)] if t == 0 else []
                if bi + 1 < len(TBS) and t // 4 == bi and t < 16:
                    todo.append((bi + 1, t % 4))
                for (b2, q) in todo:
                    t0b, tnb = TBS[b2]
                    gb2 = gu_blk[b2 % 3]; gk_ = "gu_blk%d" % (b2 % 3)
                    bank = q % 2
                    for k in range(8):
                        P.op("pe", lambda e, k=k, q=q, bank=bank, t0b=t0b, tnb=tnb: e.matmul(pb[bank][:, 0:tnb], wGM[:, k, q * 128:(q + 1) * 128], hT[:, k, t0b:t0b + tnb], start=(k == 0), stop=(k == 7)),
                             reads=["wGMu", "A_dstT"], writes=["pb%d" % bank])
                    yield
                    P.op("act", lambda e, q=q, bank=bank, gb2=gb2, tnb=tnb: e.activation(gb2[:, q, 0:tnb], pb[bank][:, 0:tnb], AF.Gelu_apprx_tanh),
                         reads=["pb%d" % bank], writes=[gk_]); yield
                vbank = 2 if t % 2 == 0 else 4
                vk = "pb%d" % vbank
                for k in range(8):
                    P.op("pe", lambda e, k=k: e.matmul(pb[vbank][:], hT[:, k, t * 128:(t + 1) * 128], wGM[:, k, 512:1024], start=(k == 0), stop=(k == 7)),
                         reads=["wGMv", "A_dstT"], writes=[vk])
                yield
                g = vg[t % 4]; gk2 = "vg%d" % (t % 4)
                P.op("act", lambda e: e.activation(g[:], pb[vbank][:], AF.Gelu_apprx_tanh), reads=[vk], writes=[gk2]); yield

            def g0b(t):
                g = vg[t % 4]; gk2 = "vg%d" % (t % 4); s6 = st6[t % 3]; m = mv[t % 3]; mk_ = "mv%d" % (t % 3)
                P.op("dve", lambda e: e.bn_stats(s6[:], g[:]), reads=[gk2], writes=[mk_ + "s"]); yield
                P.op("dve", lambda e: e.bn_aggr(m[:], s6[:]), reads=[mk_ + "s"], writes=[mk_]); yield
                P.op("dve", lambda e: e.tensor_scalar(m[:, 1:2], m[:, 1:2], EPS, None, ALU.add), reads=[mk_], writes=[mk_]); yield
                P.op("act", lambda e: e.activation(m[:, 1:2], m[:, 1:2], AF.Sqrt), reads=[mk_], writes=[mk_]); yield
                P.op("dve", lambda e: e.reciprocal(m[:, 1:2], m[:, 1:2]), reads=[mk_], writes=[mk_]); yield

            def g1(t):
                g = vg[t % 4]; gk2 = "vg%d" % (t % 4); m = mv[t % 3]; mk_ = "mv%d" % (t % 3)
                l = vl[t % 2]; lk = "vl%d" % (t % 2); b = vb[t % 2]; bk = "vb%d" % (t % 2)
                P.op("dve", lambda e: e.tensor_scalar(g[:], g[:], m[:, 0:1], m[:, 1:2], ALU.subtract, ALU.mult), reads=[gk2, mk_], writes=[gk2]); yield
                P.op("pool", lambda e: e.tensor_tensor(l[:], g[:], lng_bc[:], ALU.mult), reads=[gk2, "lng"], writes=[lk]); yield
                P.op("pool", lambda e: e.tensor_tensor(l[:], l[:], lnb_bc[:], ALU.add), reads=[lk, "lnb"], writes=[lk]); yield
                if t == 16:
                    P.dma("sp", sv_d, l[:], reads=[lk])
                P.op("act", lambda e: e.copy(b[:], l[:]), reads=[lk], writes=[bk]); yield

            def g2(t):
                bi = t // 4
                tt = t % 4
                b = vb[t % 2]; bk = "vb%d" % (t % 2)
                sbank = 3 if t % 2 == 0 else 5
                sk = "pb%d" % sbank
                stm = stmp[t % 2]; stk = "stmp%d" % (t % 2)
                gb = gu_blk[bi % 3]; gk = "gu_blk%d" % (bi % 3)
                Wm = WmT if t < 16 else WmS
                for h in range(8):
                    P.op("pe", lambda e, h=h: e.matmul(pb[sbank][(h % 2) * 64:(h % 2 + 1) * 64, (h // 2) * 128:(h // 2 + 1) * 128],
                                                       b[:, h * 64:(h + 1) * 64], Wm[:, h, :], start=True, stop=True),
                         reads=[bk, "WmT", "WmS"], writes=[sk])
                yield
                if t < 16:
                    P.op("dve", lambda e: e.tensor_tensor(stm[:], pb[sbank][:].rearrange("p (a t) -> p a t", t=128), bsp[:], ALU.add),
                         reads=[sk, "bsp"], writes=[stk])
                else:
                    P.op("dve", lambda e: e.tensor_tensor(stm[:].rearrange("p a (n j) -> p a n j", j=8),
                                                           pb[sbank][:].rearrange("p (a n j) -> p a n j", n=16, j=8),
                                                           bsp[:, :, 0:8].unsqueeze(2).to_broadcast([128, 4, 16, 8]), ALU.add),
                         reads=[sk, "bsp"], writes=[stk])
                yield
                P.op("dve", lambda e: e.tensor_tensor(ygT[:, :, t * 128:(t + 1) * 128], stm[:], gb[:, :, tt * 128:(tt + 1) * 128], ALU.mult),
                     reads=[stk, gk], writes=["ygT"]); yield
            run_pipeline([g0, g0b, g1, g2], NT)
            P.barrier()
        dump("ygT", ygT[:], [128, 4, T], ["ygT"], BF16)
        if stop_after == "GM":
            P.finish(); P.replay(); mix.close(); return nc

        for k in range(3):
            P.dma("act", wc[:, :, k], wcv_d[k].rearrange("(c p) -> p c", p=128), writes=["wc"])
        P.dma("act", bcv[:], bcv_d.rearrange("(c p) -> p c", p=128), writes=["bcv"])
        mergedT = mk(mix, "mergedT", [128, 8, T], BF16)
        bgl = mk(mix, "bgl", [128, 16])
        P.dma("sp", bgl[:], bglu_d.rearrange("(b p) -> p b", p=128), writes=["bgl"], allow_slow_non_contiguous=True)
        with contextlib.ExitStack() as ph:
            NR = 2
            wga = [mk(ph, "wga%d" % i, [128, 8, 128], BF16) for i in range(NR)]
            wgb = [mk(ph, "wgb%d" % i, [128, 8, 128], BF16) for i in range(NR)]
            wz1 = [mk(ph, "wz1%d" % i, [128, 4, 128], BF16) for i in range(NR)]
            wz2 = [mk(ph, "wz2%d" % i, [128, 4, 128], BF16) for i in range(NR)]
            wyb = [mk(ph, "wyb%d" % i, [128, 4, 128], BF16) for i in range(NR)]
            tA = [mk(ph, "tA%d" % i, [128, 512]) for i in range(2)]
            tB = [mk(ph, "tB%d" % i, [128, 512]) for i in range(2)]
            tC = [mk(ph, "tC%d" % i, [128, 512]) for i in range(2)]

            def load_m(m):
                r = m % NR
                P.dma("pool", wga[r][:], cw(win_d[:, 1536 + m * 128:1536 + (m + 1) * 128]), writes=["wga%d" % r])
                P.dma("pool", wgb[r][:], cw(win_d[:, 2560 + m * 128:2560 + (m + 1) * 128]), writes=["wgb%d" % r])
                P.dma("pool", wz1[r][:], cw(wglu_d[:, m * 128:(m + 1) * 128]), writes=["wz1%d" % r])
                P.dma("pool", wz2[r][:], cw(wglu_d[:, 1024 + m * 128:1024 + (m + 1) * 128]), writes=["wz2%d" % r])
                P.dma("pool", wyb[r][:], cw(wgo_d[:, m * 128:(m + 1) * 128]), writes=["wyb%d" % r])
            load_m(0)
            it = 0
            for m in range(8):
                if m + 1 < 8:
                    load_m(m + 1)
                r = m % NR
                for (t0, tn) in TBS:
                    par = it % 2
                    it += 1
                    bga, bgb, bz1, bz2, byb = 0, 1, 2, 3, 4
                    for k in range(8):
                        P.op("pe", lambda e, k=k, r=r, t0=t0, tn=tn: e.matmul(pb[0][:, 0:tn], wga[r][:, k, :], hT[:, k, t0:t0 + tn], start=(k == 0), stop=(k == 7)),
                             reads=["wga%d" % r, "A_dstT"], writes=["pb0"])
                    for k in range(4):
                        P.op("pe", lambda e, k=k, r=r, t0=t0, tn=tn: e.matmul(pb[3][:, 0:tn], wz2[r][:, k, :], gy5Tf[:, k, t0:t0 + tn], start=(k == 0), stop=(k == 3)),
                             reads=["wz2%d" % r, "gy5T"], writes=["pb3"])
                    for k in range(4):
                        P.op("pe", lambda e, k=k, r=r, t0=t0, tn=tn: e.matmul(pb[2][:, 0:tn], wz1[r][:, k, :], gy5Tf[:, k, t0:t0 + tn], start=(k == 0), stop=(k == 3)),
                             reads=["wz1%d" % r, "gy5T"], writes=["pb2"])
                    for k in range(8):
                        P.op("pe", lambda e, k=k, r=r, t0=t0, tn=tn: e.matmul(pb[1][:, 0:tn], wgb[r][:, k, :], hT[:, k, t0:t0 + tn], start=(k == 0), stop=(k == 7)),
                             reads=["wgb%d" % r, "A_dstT"], writes=["pb1"])
                    for k in range(4):
                        P.op("pe", lambda e, k=k, r=r, t0=t0, tn=tn: e.matmul(pb[4][:, 0:tn], wyb[r][:, k, :], ygT[:, k, t0:t0 + tn], start=(k == 0), stop=(k == 3)),
                             reads=["wyb%d" % r, "ygT"], writes=["pb4"])
                    a, b, c3 = tA[par], tB[par], tC[par]
                    ak, bk, ck = "tA%d" % par, "tB%d" % par, "tC%d" % par
                    P.op("act", lambda e, a=a, tn=tn: e.activation(a[:, 0:tn], pb[0][:, 0:tn], AF.Sigmoid), reads=["pb0"], writes=[ak])
                    P.op("act", lambda e, b=b, tn=tn, m=m: e.activation(b[:, 0:tn], pb[3][:, 0:tn], AF.Sigmoid, bias=bgl[:, 8 + m:9 + m]), reads=["pb3", "bgl"], writes=[bk])
                    P.op("act", lambda e, c3=c3, tn=tn: e.activation(c3[:, 0:tn], pb[1][:, 0:tn], AF.Sigmoid), reads=["pb1"], writes=[ck])
                    P.op("dve", lambda e, b=b, tn=tn, m=m: e.scalar_tensor_tensor(b[:, 0:tn], pb[2][:, 0:tn], bgl[:, m:m + 1], b[:, 0:tn], ALU.add, ALU.mult),
                         reads=["pb2", "bgl", bk], writes=[bk])
                    P.op("dve", lambda e, a=a, b=b, tn=tn: e.tensor_tensor(a[:, 0:tn], a[:, 0:tn], b[:, 0:tn], ALU.mult), reads=[ak, bk], writes=[ak])
                    P.op("dve", lambda e, c3=c3, tn=tn: e.tensor_tensor(c3[:, 0:tn], pb[4][:, 0:tn], c3[:, 0:tn], ALU.mult), reads=["pb4", ck], writes=[ck])
                    P.op("dve", lambda e, a=a, c3=c3, m=m, t0=t0, tn=tn: e.tensor_tensor(mergedT[:, m, t0:t0 + tn], a[:, 0:tn], c3[:, 0:tn], ALU.add),
                         reads=[ak, ck], writes=["mergedT"])
            P.barrier()
        dump("mergedT", mergedT[:], [128, 8, T], ["mergedT"], BF16)
        if stop_after == "MERGE":
            P.finish(); P.replay(); mix.close(); return nc

        h2T = hT
        with contextlib.ExitStack() as ph:
            wo = mk(ph, "wo", [128, 8, D], BF16)
            P.dma("pool", wo[:, :, 0:512], cw(wout_d[:, 0:512]), writes=["wo0"])
            P.dma("pool", wo[:, :, 512:1024], cw(wout_d[:, 512:1024]), writes=["wo1"])
            gm_p = mk(ph, "gm_p", [128, D])
            gm_s = mk(ph, "gm_s", [128, D])
            tm_gate(gm_p, gm_s, 16, "gm")
            x1r = [mk(ph, "x1r%d" % i, [128, D]) for i in range(3)]
            rings["xn"] = [mk(ph, "xnB%d" % i, [128, D]) for i in range(2)]
            xt_ring = [mk(ph, "xtB%d" % i, [128, D]) for i in range(3)]

            def p0(t):
                xt = xt_ring[t % 3]
                xk = "xt%d" % (t % 3)
                P.dma("sp", xt[:], x_d[t * 128:(t + 1) * 128, :], writes=[xk]); yield
                banks = (0, 1) if t % 2 == 0 else (2, 3)
                for h in range(2):
                    for k in range(8):
                        P.op("pe", lambda e, k=k, h=h: e.matmul(pb[banks[h]][:], mergedT[:, k, t * 128:(t + 1) * 128], wo[:, k, h * 512:(h + 1) * 512], start=(k == 0), stop=(k == 7)),
                             reads=["mergedT", "wo%d" % h], writes=["pb%d" % banks[h]])
                    yield

            def p0b(t):
                xt = xt_ring[t % 3]
                xk = "xt%d" % (t % 3)
                banks = (0, 1) if t % 2 == 0 else (2, 3)
                x1 = x1r[t % 3]
                x1k = "x1r%d" % (t % 3)
                g = gm_p if t < 16 else gm_s
                for h in range(2):
                    P.op("dve", lambda e, h=h: e.tensor_tensor(x1[:, h * 512:(h + 1) * 512], pb[banks[h]][:], g[:, h * 512:(h + 1) * 512], ALU.mult),
                         reads=["pb%d" % banks[h], "gm_p", "gm_s"], writes=[x1k]); yield
                P.op("pool", lambda e: e.tensor_tensor(x1[:], x1[:], xt[:], ALU.add), reads=[x1k, xk], writes=[x1k]); yield
                P.dma("sp", x1_d[t * 128:(t + 1) * 128, :], x1[:], reads=[x1k], writes=["x1_d%d" % t]); yield
            run_pipeline([p0, p0b] + norm_stages(lambda t: (x1r[t % 3][:], "x1r%d" % (t % 3)), h2T, gs2, 24, 1, "B_"), NT)
            P.barrier()
        dump("h2T", h2T[:], [128, 8, T], ["B_dstT"], BF16)
        if "h2T" in dumps or "mergedT" in dumps:
            P.barrier()
        if stop_after == "P3":
            P.finish(); P.replay(); mix.close(); return nc

        mix.close()
        ffn_phase(nc, P, mk, sb, pb, ident, h2T, modT, tm_gate, fg_bc, (gf_p, gf_s, pastT, wc, bcv), rs_all, junk, dict(
            wup=wup_d, wcv=wcv_d, bcv=bcv_d, wdn=wdn_d, scv=scv_d, x1=x1_d, y=y_d, pcv=pcv_d, scvo=scvo_d), cw, dump, stop_after)

        P.finish()
        P.replay()
    return nc


def s5_prep(nc, P, mk, ph, pb, ident, dr, scr, TH8):
    TWO_PI = 2.0 * math.pi
    if True:
        are = mk(ph, "are", [128, 16]); aim = mk(ph, "aim", [128, 16]); ldt = mk(ph, "ldt", [128, 16])
        Bre = mk(ph, "Bre", [128, 16, 16]); Bim = mk(ph, "Bim", [128, 16, 16])
        CTr = mk(ph, "CTr", [128, 16, 16]); CTi = mk(ph, "CTi", [128, 16, 16])
        dcol = mk(ph, "dcol", [128, 4])
        ph0 = ph
        Cn_re = mk(ph0, "Cn_re", [16, 32, 64]); Cn_im = mk(ph0, "Cn_im", [16, 32, 64])
        for gi in range(2):
            sl = slice(gi * 64, (gi + 1) * 64)
            P.dma("sp", are[sl, :], dr["are"].rearrange("(pr gi) p -> gi p pr", gi=2)[gi], writes=["are"], allow_slow_non_contiguous=True)
            P.dma("sp", aim[sl, :], dr["aim"].rearrange("(pr gi) p -> gi p pr", gi=2)[gi], writes=["aim"], allow_slow_non_contiguous=True)
            P.dma("sp", ldt[sl, :], dr["ldt"].rearrange("(pr gi) -> gi pr", gi=2)[gi:gi + 1, :].to_broadcast([64, 16]), writes=["ldt"])
            P.dma("sp", Bre[sl], dr["bre"].rearrange("(pr gi) p h -> gi p pr h", gi=2)[gi], writes=["Bre"])
            P.dma("sp", Bim[sl], dr["bim"].rearrange("(pr gi) p h -> gi p pr h", gi=2)[gi], writes=["Bim"])
        P.dma("sp", Cn_re[:], dr["cre"].rearrange("g h p -> h g p"), writes=["Cn_re"])
        P.dma("sp", Cn_im[:], dr["cim"].rearrange("g h p -> h g p"), writes=["Cn_im"])
        P.dma("sp", dcol[:], dr["sd"].rearrange("(q g) h -> (g h) q", q=4), writes=["dcol"], allow_slow_non_contiguous=True)
        for ri, (Cn, CT, nm) in enumerate(((Cn_re, CTr, "CTr"), (Cn_im, CTi, "CTi"))):
            for pr in range(16):
                P.op("pe", lambda e, pr=pr, Cn=Cn, ri=ri: e.transpose(pb[4 + ri][:, pr * 16:(pr + 1) * 16],
                                                                      Cn[:, 2 * pr:2 * pr + 2, :].rearrange("h g p -> h (g p)"), ident[0:16, 0:16]),
                     reads=["Cn_re", "Cn_im", "ident"], writes=["pb%d" % (4 + ri)])
            P.op("act", lambda e, CT=CT, ri=ri: e.copy(CT[:], pb[4 + ri][:, 0:256].rearrange("p (a b) -> p a b", b=16)), reads=["pb%d" % (4 + ri)], writes=[nm])

        tA = mk(ph, "s5tA", [128, 256]); tB = mk(ph, "s5tB", [128, 256]); tI = mk(ph, "s5tI", [128, 256], I32)

        def sincos(ang, n, sn_out, cs_out, rk, wk):
            a = tA[:, 0:n]; b = tB[:, 0:n]; ii = tI[:, 0:n]
            for off, out in ((0.0, sn_out), (0.25, cs_out)):
                P.op("dve", lambda e, off=off: e.tensor_scalar(a, ang, 1.0 / TWO_PI, off, ALU.mult, ALU.add), reads=rk, writes=["s5tA"])
                P.op("dve", lambda e: e.tensor_copy(ii, a), reads=["s5tA"], writes=["s5tI"])
                P.op("dve", lambda e: e.tensor_copy(b, ii), reads=["s5tI"], writes=["s5tB"])
                P.op("dve", lambda e: e.tensor_tensor(a, a, b, ALU.subtract), reads=["s5tA", "s5tB"], writes=["s5tA"])
                P.op("dve", lambda e: e.tensor_scalar(b, a, 0.5, -1.0, ALU.is_gt, ALU.mult), reads=["s5tA"], writes=["s5tB"])
                P.op("dve", lambda e: e.tensor_tensor(a, a, b, ALU.add), reads=["s5tA", "s5tB"], writes=["s5tA"])
                P.op("dve", lambda e: e.tensor_scalar(b, a, -0.5, 1.0, ALU.is_lt, ALU.mult), reads=["s5tA"], writes=["s5tB"])
                P.op("dve", lambda e: e.tensor_tensor(a, a, b, ALU.add), reads=["s5tA", "s5tB"], writes=["s5tA"])
                P.op("act", lambda e, out=out: e.activation(out, a, AF.Sin, scale=6.28318), reads=["s5tA"], writes=wk)

        def tt(out, a, b, op, reads, writes, eng="dve"):
            P.op(eng, lambda e: e.tensor_tensor(out, a, b, op), reads=reads, writes=writes)

        c1 = mk(ph, "s5c1", [128, 256]); c2 = mk(ph, "s5c2", [128, 256])

        def cmul(o_re, o_im, a_re, a_im, b_re, b_im, shape, reads, writes):
            n = int(np.prod(shape))
            v1 = c1[:, 0:n]; v2 = c2[:, 0:n]
            if len(shape) == 2:
                v1 = v1.rearrange("p (a b) -> p a b", b=shape[1]); v2 = v2.rearrange("p (a b) -> p a b", b=shape[1])
            tt(v1, a_re, b_re, ALU.mult, reads, ["s5c1"])
            tt(v2, a_im, b_im, ALU.mult, reads, ["s5c2"])
            tt(o_re, v1, v2, ALU.subtract, ["s5c1", "s5c2"], writes + ["s5c1", "s5c2"])
            tt(v1, a_re, b_im, ALU.mult, reads, ["s5c1"])
            tt(v2, a_im, b_re, ALU.mult, reads, ["s5c2"])
            tt(o_im, v1, v2, ALU.add, ["s5c1", "s5c2"], writes + ["s5c1", "s5c2"])

        dt = mk(ph, "s5dt", [128, 16]); mag = mk(ph, "s5mag", [128, 16]); th = mk(ph, "s5th", [128, 16])
        sn = mk(ph, "s5sn", [128, 16]); cs = mk(ph, "s5cs", [128, 16])
        APr = mk(ph, "APr", [128, 9, 16]); APi = mk(ph, "APi", [128, 9, 16])
        P.op("act", lambda e: e.activation(dt[:], ldt[:], AF.Exp), reads=["ldt"], writes=["dt"])
        tt(mag[:], are[:], dt[:], ALU.mult, ["are", "dt"], ["mag"])
        P.op("act", lambda e: e.activation(mag[:], mag[:], AF.Exp), reads=["mag"], writes=["mag"])
        tt(th[:], aim[:], dt[:], ALU.mult, ["aim", "dt"], ["th"])
        sincos(th[:], 16, sn[:], cs[:], ["th"], ["sncs"])
        P.op("dve", lambda e: e.memset(APr[:, 0, :], 1.0), writes=["AP"])
        P.op("dve", lambda e: e.memset(APi[:, 0, :], 0.0), writes=["AP"])
        tt(APr[:, 1, :], mag[:], cs[:], ALU.mult, ["mag", "sncs"], ["AP"])
        tt(APi[:, 1, :], mag[:], sn[:], ALU.mult, ["mag", "sncs"], ["AP"])
        def bck(ap16, n):
            return ap16.unsqueeze(1).to_broadcast([128, n, 16])
        cmul(APr[:, 2, :], APi[:, 2, :], APr[:, 1, :], APi[:, 1, :], APr[:, 1, :], APi[:, 1, :], [16], ["AP"], ["AP"])
        cmul(APr[:, 3:5, :], APi[:, 3:5, :], APr[:, 1:3, :], APi[:, 1:3, :], bck(APr[:, 2, :], 2), bck(APi[:, 2, :], 2), [2, 16], ["AP"], ["AP"])
        cmul(APr[:, 5:9, :], APi[:, 5:9, :], APr[:, 1:5, :], APi[:, 1:5, :], bck(APr[:, 4, :], 4), bck(APi[:, 4, :], 4), [4, 16], ["AP"], ["AP"])
        R8 = mk(ph, "R8", [128, 16]); nA8i = mk(ph, "nA8i", [128, 16])
        tt(R8[:], are[:], dt[:], ALU.mult, ["are", "dt"], ["R8"])
        P.op("act", lambda e: e.activation(R8[:], R8[:], AF.Exp, scale=8.0), reads=["R8"], writes=["R8"])
        P.op("dve", lambda e: e.tensor_scalar(TH8[:], th[:], 8.0, None, ALU.mult), reads=["th"], writes=["TH8"])
        P.op("dve", lambda e: e.tensor_scalar(nA8i[:], APi[:, 8, :], -1.0, None, ALU.mult), reads=["AP"], writes=["nA8i"])

        nr = mk(ph, "s5nr", [128, 16]); den = mk(ph, "s5den", [128, 16]); t16 = mk(ph, "s5t16", [128, 16])
        cr = mk(ph, "s5cr", [128, 16]); ci = mk(ph, "s5ci", [128, 16])
        P.op("dve", lambda e: e.tensor_scalar(nr[:], APr[:, 1, :], -1.0, None, ALU.add), reads=["AP"], writes=["nr"])
        tt(den[:], are[:], are[:], ALU.mult, ["are"], ["den"])
        tt(t16[:], aim[:], aim[:], ALU.mult, ["aim"], ["t16"])
        tt(den[:], den[:], t16[:], ALU.add, ["den", "t16"], ["den"])
        P.op("dve", lambda e: e.reciprocal(den[:], den[:]), reads=["den"], writes=["den"])
        tt(cr[:], nr[:], are[:], ALU.mult, ["nr", "are"], ["cr"])
        tt(t16[:], APi[:, 1, :], aim[:], ALU.mult, ["AP", "aim"], ["t16"])
        tt(cr[:], cr[:], t16[:], ALU.add, ["cr", "t16"], ["cr"])
        tt(cr[:], cr[:], den[:], ALU.mult, ["cr", "den"], ["cr"])
        tt(ci[:], APi[:, 1, :], are[:], ALU.mult, ["AP", "are"], ["ci"])
        tt(t16[:], nr[:], aim[:], ALU.mult, ["nr", "aim"], ["t16"])
        tt(ci[:], ci[:], t16[:], ALU.subtract, ["ci", "t16"], ["ci"])
        tt(ci[:], ci[:], den[:], ALU.mult, ["ci", "den"], ["ci"])
        bbr = mk(ph, "bbr", [128, 16, 16]); bbi = mk(ph, "bbi", [128, 16, 16])

        def bc16(ap16):
            return ap16.unsqueeze(2).to_broadcast([128, 16, 16])
        cmul(bbr[:], bbi[:], bc16(cr[:]), bc16(ci[:]), Bre[:], Bim[:], [16, 16], ["cr", "ci", "Bre", "Bim"], ["bb"])

        LA = mk(ph, "LA", [128, 4, 8, 2, 128], BF16)
        LC = mk(ph, "LC", [128, 8, 2, 16, 32], BF16)
        KC = mk(ph, "KC", [128, 4, 8, 128], BF16)
        P.op("pool", lambda e: e.memset(LC[:], 0.0), writes=["LC"])
        if True:
            ph2 = ph
            WP = mk(ph2, "WP", [128, 8, 2, 16, 2, 16])
            CP = mk(ph2, "CP", [128, 2, 16, 2, 16])
            Wr = mk(ph2, "s5Wr", [128, 16, 16]); Wi = mk(ph2, "s5Wi", [128, 16, 16])
            bm16 = mk(ph2, "bm16", [128, 128]); ti1 = mk(ph2, "s5ti1", [128, 128], I32); ti2 = mk(ph2, "s5ti2", [128, 128], I32)
            ktmp = mk(ph2, "ktmp", [128, 128])
            P.op("pool", lambda e: e.memset(WP[:], 0.0), writes=["WP"])
            P.op("pool", lambda e: e.memset(CP[:], 0.0), writes=["CP"])
            P.op("pool", lambda e: e.iota(ti1[:], [[1, 8], [0, 16]], base=0, channel_multiplier=0), writes=["s5ti1"])
            P.op("pool", lambda e: e.iota(ti2[:], [[0, 128]], base=0, channel_multiplier=1), writes=["s5ti2"])
            P.op("dve", lambda e: e.tensor_single_scalar(ti2[:], ti2[:], 4, ALU.arith_shift_right), reads=["s5ti2"], writes=["s5ti2"])
            P.op("dve", lambda e: e.tensor_tensor(bm16[:], ti1[:], ti2[:], ALU.is_equal), reads=["s5ti1", "s5ti2"], writes=["bm16"])
            for gi in range(2):
                sl = slice(gi * 64, (gi + 1) * 64)
                P.op("dve", lambda e, sl=sl, gi=gi: e.tensor_copy(CP[sl, 0, :, gi, :], CTr[sl]), reads=["CTr", "CP"], writes=["CP"])
                P.op("dve", lambda e, sl=sl, gi=gi: e.tensor_scalar(CP[sl, 1, :, gi, :], CTi[sl], -1.0, None, ALU.mult), reads=["CTi", "CP"], writes=["CP"])
            Wt = [[mk(ph2, "s5Wt%d_%d" % (i, j), [128, 16, 16]) for j in range(4)] for i in range(8)]

            def cmul_multi(insts, eng="dve"):
                for step in range(6):
                    for (o_re, o_im, a_re, a_im, b_re, b_im, v1, v2, rk, okey, vkey) in insts:
                        if step == 0:
                            tt(v1, a_re, b_re, ALU.mult, rk, [vkey + "a"], eng)
                        elif step == 1:
                            tt(v2, a_im, b_im, ALU.mult, rk, [vkey + "b"], eng)
                        elif step == 2:
                            tt(o_re, v1, v2, ALU.subtract, [vkey + "a", vkey + "b"], [okey + "r"], eng)
                        elif step == 3:
                            tt(v1, a_re, b_im, ALU.mult, rk, [vkey + "a"], eng)
                        elif step == 4:
                            tt(v2, a_im, b_re, ALU.mult, rk, [vkey + "b"], eng)
                        else:
                            tt(o_im, v1, v2, ALU.add, [vkey + "a", vkey + "b"], [okey + "i"], eng)
            insts = []
            for s in range(8):
                k = 7 - s
                w = Wt[s]
                insts.append((w[0][:], w[1][:], bc16(APr[:, k, :]), bc16(APi[:, k, :]), bbr[:], bbi[:], w[2][:], w[3][:], ["AP", "bb"], "Wt%d" % s, "Wv%d" % s))
            cmul_multi(insts)
            for s in range(8):
                w = Wt[s]
                for gi in range(2):
                    sl = slice(gi * 64, (gi + 1) * 64)
                    P.op("dve", lambda e, sl=sl, gi=gi, s=s, w=w: e.tensor_copy(WP[sl, s, 0, :, gi, :], w[0][sl]), reads=["Wt%dr" % s, "WP"], writes=["WP"])
                    P.op("dve", lambda e, sl=sl, gi=gi, s=s, w=w: e.tensor_copy(WP[sl, s, 1, :, gi, :], w[1][sl]), reads=["Wt%di" % s, "WP"], writes=["WP"])
            insts = []
            for j in range(8):
                w = Wt[j]
                insts.append((w[0][:], w[1][:], bc16(APr[:, j + 1, :]), bc16(APi[:, j + 1, :]), CTr[:], CTi[:], w[2][:], w[3][:], ["AP", "CTr", "CTi"], "Wt%d" % j, "Wv%d" % j))
            cmul_multi(insts, "pool")
            for j in range(8):
                w = Wt[j]
                for gi in range(2):
                    sl = slice(gi * 64, (gi + 1) * 64)
                    P.op("dve", lambda e, sl=sl, gi=gi, j=j, w=w: e.tensor_copy(LC[sl, j, 0, :, gi * 16:(gi + 1) * 16], w[0][sl]), reads=["Wt%dr" % j, "LC"], writes=["LC"])
                    P.op("dve", lambda e, sl=sl, gi=gi, j=j, w=w: e.tensor_scalar(LC[sl, j, 1, :, gi * 16:(gi + 1) * 16], w[1][sl], -1.0, None, ALU.mult), reads=["Wt%di" % j, "LC"], writes=["LC"])
            WPf = WP[:].rearrange("p s r a b c -> p s r (a b c)")
            CPf = CP[:].rearrange("p r a b c -> p r (a b c)")
            n = 0
            for q in range(4):
                qs = slice(q * 128, (q + 1) * 128)
                for s0 in range(0, 8, 2):
                    bank = 6 + n % 2
                    n += 1
                    for ds in range(2):
                        for ri in range(2):
                            col = (ds * 2 + ri) * 128
                            P.op("pe", lambda e, s_=s0 + ds, ri=ri, qs=qs, bank=bank, col=col: e.transpose(pb[bank][:, col:col + 128], WPf[:, s_, ri, qs], ident[:]),
                                 reads=["WP", "ident"], writes=["pb%d" % bank])
                    P.op("dve", lambda e, q=q, s0=s0, bank=bank: e.tensor_copy(LA[:, q, s0:s0 + 2, :, :].rearrange("p a b c -> p (a b c)"), pb[bank][:, 0:512]),
                         reads=["pb%d" % bank], writes=["LA"])
                for (t0, nt) in ((0, 1), (1, 4), (5, 3)):
                    bank = 4 + n % 2
                    n += 1
                    for dt_ in range(nt):
                        s_ = 7 - (t0 + dt_)
                        col = dt_ * 128
                        P.op("pe", lambda e, s_=s_, qs=qs, bank=bank, col=col: e.matmul(pb[bank][:, col:col + 128], WPf[:, s_, 0, qs], CPf[:, 0, qs], start=True, stop=False),
                             reads=["WP", "CP"], writes=["pb%d" % bank])
                        P.op("pe", lambda e, s_=s_, qs=qs, bank=bank, col=col: e.matmul(pb[bank][:, col:col + 128], WPf[:, s_, 1, qs], CPf[:, 1, qs], start=False, stop=True),
                             reads=["WP", "CP"], writes=["pb%d" % bank])
                    if t0 == 0:
                        tt(ktmp[:], pb[bank][:, 0:128], bm16[:], ALU.mult, ["pb%d" % bank, "bm16"], ["ktmp"])
                        P.op("dve", lambda e, q=q: e.scalar_tensor_tensor(KC[:, q, 0, :], ident[:], dcol[:, q:q + 1], ktmp[:], ALU.mult, ALU.add),
                             reads=["ident", "dcol", "ktmp"], writes=["KC"])
                    else:
                        P.op("dve", lambda e, q=q, t0=t0, nt=nt, bank=bank: e.tensor_tensor(KC[:, q, t0:t0 + nt, :], pb[bank][:, 0:nt * 128].rearrange("p (a b) -> p a b", b=128),
                                                                                             bm16[:].unsqueeze(1).to_broadcast([128, nt, 128]), ALU.mult),
                             reads=["pb%d" % bank, "bm16"], writes=["KC"])

        small = mk(ph, "s5small", [128, 4, 16])
        P.op("dve", lambda e: e.tensor_copy(small[:, 0, :], R8[:]), reads=["R8"], writes=["s5small"])
        P.op("dve", lambda e: e.tensor_copy(small[:, 1, :], nA8i[:]), reads=["nA8i"], writes=["s5small"])
        P.op("dve", lambda e: e.tensor_copy(small[:, 2, :], APr[:, 8, :]), reads=["AP"], writes=["s5small"])
        P.op("dve", lambda e: e.tensor_copy(small[:, 3, :], APi[:, 8, :]), reads=["AP"], writes=["s5small"])
        P.dma("sp", scr["small"], small[:], reads=["s5small"], writes=["scr_small"])
        P.dma("sp", scr["LA"], LA[:].rearrange("p a b c d -> p (a b c d)"), reads=["LA"], writes=["scr_LA"])
        P.dma("sp", scr["LC"], LC[:].rearrange("p a b c d -> p (a b c d)"), reads=["LC"], writes=["scr_LC"])
        P.dma("sp", scr["KC"], KC[:].rearrange("p a b c -> p (a b c)"), reads=["KC"], writes=["scr_KC"])


def s5_tables(nc, P, mk, ph, TH8, scr):
    TWO_PI = 2.0 * math.pi
    cio_i = mk(ph, "cio_i", [128, 256], I32); cio = mk(ph, "cio", [128, 256])
    nhalf = mk(ph, "nhalf", [128, 1])
    P.op("dve", lambda e: e.memset(nhalf[:], -3.14159), writes=["nhalf"])
    P.op("pool", lambda e: e.iota(cio_i[:], [[1, 256]], base=0, channel_multiplier=0), writes=["cio_i"])
    P.op("dve", lambda e: e.tensor_copy(cio[:], cio_i[:]), reads=["cio_i"], writes=["cio"])
    NQ = 4
    angq = mk(ph, "angq", [128, NQ * 256])
    tA = [mk(ph, "tAq%d" % i, [128, NQ * 256]) for i in range(2)]
    tB = [mk(ph, "tBq%d" % i, [128, NQ * 256]) for i in range(2)]
    tI = [mk(ph, "tIq%d" % i, [128, NQ * 256], I32) for i in range(2)]
    Eq = [[mk(ph, "Eq%d_%d" % (i, j), [128, NQ * 256]) for j in range(2)] for i in range(2)]
    for qq in range(16 // NQ):
        par = qq % 2
        P.op("dve", lambda e, qq=qq: e.tensor_tensor(angq[:].rearrange("p (a c) -> p a c", c=256),
                                                      cio[:].unsqueeze(1).to_broadcast([128, NQ, 256]),
                                                      TH8[:, qq * NQ:(qq + 1) * NQ].unsqueeze(2).to_broadcast([128, NQ, 256]), ALU.mult),
             reads=["cio", "TH8"], writes=["angq"])
        steps = []
        for ti, off in ((0, 0.0), (1, 0.25)):
            out = Eq[par][ti]; ok = "Eq%d_%d" % (par, ti)
            a = tA[ti][:]; b = tB[ti][:]; ii = tI[ti][:]
            ka, kb, ki = "tAq%d" % ti, "tBq%d" % ti, "tIq%d" % ti
            steps.append([
                ("dve", lambda e, a=a, off=off: e.tensor_scalar(a, angq[:], 1.0 / TWO_PI, off + 0.5, ALU.mult, ALU.add), ["angq"], [ka]),
                ("dve", lambda e, a=a, ii=ii: e.tensor_copy(ii, a), [ka], [ki]),
                ("dve", lambda e, b=b, ii=ii: e.tensor_copy(b, ii), [ki], [kb]),
                ("dve", lambda e, a=a, b=b: e.tensor_tensor(a, a, b, ALU.subtract), [ka, kb], [ka]),
                ("dve", lambda e, a=a, b=b: e.tensor_scalar(b, a, 0.0, 1.0, ALU.is_lt, ALU.mult), [ka], [kb]),
                ("dve", lambda e, a=a, b=b: e.tensor_tensor(a, a, b, ALU.add), [ka, kb], [ka]),
                ("act", lambda e, a=a, out=out: e.activation(out[:], a, AF.Sin, scale=6.28318, bias=nhalf[:, 0:1]), [ka, "nhalf"], [ok]),
            ])
        for k in range(7):
            for ti in range(2):
                eng, fn, rk, wk = steps[ti][k]
                P.op(eng, fn, reads=rk, writes=wk)
        for ti in range(2):
            out = Eq[par][ti]; ok = "Eq%d_%d" % (par, ti)
            P.dma("sp", scr["tab"][:, ti, qq * NQ:(qq + 1) * NQ, :], out[:].rearrange("p (a c) -> p a c", c=256), reads=[ok], writes=["scr_tab"])


def s5_phase(nc, P, mk, sb, pb, ident, uT, gy5T, fin_p, dr, scr, dump, dumps):
    with contextlib.ExitStack() as ph:
        LA = mk(ph, "LA_m", [128, 4, 8, 2, 128], BF16)
        LC = mk(ph, "LC_m", [128, 8, 2, 16, 32], BF16)
        KC = mk(ph, "KC_m", [128, 4, 8, 128], BF16)
        small = mk(ph, "s5small_m", [128, 4, 16])
        P.dma("sp", LA[:].rearrange("p a b c d -> p (a b c d)"), scr["LA"], reads=["scr_LA"], writes=["LA"])
        P.dma("sp", LC[:].rearrange("p a b c d -> p (a b c d)"), scr["LC"], reads=["scr_LC"], writes=["LC"])
        P.dma("sp", KC[:].rearrange("p a b c -> p (a b c)"), scr["KC"], reads=["scr_KC"], writes=["KC"])
        P.dma("sp", small[:], scr["small"], reads=["scr_small"], writes=["s5small"])
        sT = mk(ph, "sT", [128, 2, 16, 16])
        HS = mk(ph, "HS", [128, 16, 2, NCH], BF16)
        fin_s = mk(ph, "fin_s", [128, 2, 16, 16])
        st_one = mk(ph, "st_in0", [16, 2048])
        st_in = [st_one, st_one]
        for ri in range(2):
            P.dma("sp", st_in[ri][:], dr["sre" if ri == 0 else "sim"], writes=["st_in0"])
            for pr in range(16):
                P.op("pe", lambda e, ri=ri, pr=pr: e.transpose(pb[2 + ri][:, pr * 16:(pr + 1) * 16], st_in[ri][:, pr * 128:(pr + 1) * 128], ident[0:16, 0:16]),
                     reads=["st_in0", "ident"], writes=["pb%d" % (2 + ri)])
            P.op("act", lambda e, ri=ri: e.copy(sT[:, ri, :, :], pb[2 + ri][:, 0:256].rearrange("p (a b) -> p a b", b=16)), reads=["pb%d" % (2 + ri)], writes=["sT"])
            P.op("dve", lambda e, ri=ri: e.tensor_copy(HS[:, :, ri, 256:272], sT[:, ri, :, :]), reads=["sT"], writes=["HSs"])
            P.op("dve", lambda e, ri=ri: e.memset(HS[:, :, ri, 0:1], 0.0), writes=["HS0"])

        def rr(gens):
            gens = list(gens)
            while gens:
                for g_ in list(gens):
                    try:
                        next(g_)
                    except StopIteration:
                        gens.remove(g_)

        tmps = []
        for par in range(3):
            d = {}
            for nm in ("Ec", "Es", "Mr", "Mi", "m1", "m2", "Gr", "Gi"):
                d[nm] = mk(ph, "%s_%d" % (nm, par), [128, 256])
            d["Sr"] = mk(ph, "Sr_%d" % par, [128, NCH]); d["Si"] = mk(ph, "Si_%d" % par, [128, NCH])
            d["He"] = mk(ph, "He_%d" % par, [128, 2, 256])
            tmps.append(d)

        def pair_gen(pr, par):
            q, i = divmod(pr, 4)
            rows = slice(32 * i, 32 * i + 32)
            d = tmps[par]
            K = lambda nm: "%s_%d" % (nm, par)
            bnk = ((0, 1), (4, 5), (6, 7))[par]
            Ec, Es, Mr, Mi, m1, m2, Gr, Gi, Sr, Si, He = (d[x] for x in ("Ec", "Es", "Mr", "Mi", "m1", "m2", "Gr", "Gi", "Sr", "Si", "He"))

            def t2(out, a, b, op, reads, writes):
                P.op("dve", lambda e: e.tensor_tensor(out, a, b, op), reads=reads, writes=writes)
            for ri in range(2):
                bank = bnk[ri]
                for s_ in range(8):
                    P.op("pe", lambda e, s_=s_, ri=ri, bank=bank: e.matmul(pb[bank][:, 0:NCH], LA[rows, q, s_, ri, :], uT[rows, q, s_, :],
                                                                        start=(s_ == 0), stop=(s_ == 7), tile_position=(32 * i, 0)),
                         reads=["LA", "uT"], writes=["pb%d" % bank])
                yield
            P.op("act", lambda e: e.copy(Sr[:], pb[bnk[0]][:, 0:NCH]), reads=["pb%d" % bnk[0]], writes=[K("Sr")]); yield
            P.op("act", lambda e: e.copy(Si[:], pb[bnk[1]][:, 0:NCH]), reads=["pb%d" % bnk[1]], writes=[K("Si")]); yield
            P.dma("sp", Es[:], scr["tab"][:, 0, pr, :], reads=["scr_tab"], writes=[K("EcEs")]); yield
            P.dma("sp", Ec[:], scr["tab"][:, 1, pr, :], reads=["scr_tab"], writes=[K("EcEs")]); yield
            t2(m1[:], Sr[:, 0:256], Ec[:], ALU.mult, [K("Sr"), K("EcEs")], [K("m1")]); yield
            t2(m2[:], Si[:, 0:256], Es[:], ALU.mult, [K("Si"), K("EcEs")], [K("m2")]); yield
            t2(Mr[:], m1[:], m2[:], ALU.add, [K("m1"), K("m2")], [K("Mr"), K("m1"), K("m2")]); yield
            t2(m1[:], Si[:, 0:256], Ec[:], ALU.mult, [K("Si"), K("EcEs")], [K("m1")]); yield
            t2(m2[:], Sr[:, 0:256], Es[:], ALU.mult, [K("Sr"), K("EcEs")], [K("m2")]); yield
            t2(Mi[:], m1[:], m2[:], ALU.subtract, [K("m1"), K("m2")], [K("Mi"), K("m1"), K("m2")]); yield
            P.op("dve", lambda e: e.tensor_tensor_scan(Gr[:], small[:, 0, pr:pr + 1].to_broadcast([128, 256]), Mr[:], 0.0, ALU.mult, ALU.add),
                 reads=["s5small", K("Mr")], writes=[K("Gr")]); yield
            P.op("dve", lambda e: e.tensor_tensor_scan(Gi[:], small[:, 0, pr:pr + 1].to_broadcast([128, 256]), Mi[:], 0.0, ALU.mult, ALU.add),
                 reads=["s5small", K("Mi")], writes=[K("Gi")]); yield
            t2(m1[:], Gr[:], Ec[:], ALU.mult, [K("Gr"), K("EcEs")], [K("m1")]); yield
            t2(m2[:], Gi[:], Es[:], ALU.mult, [K("Gi"), K("EcEs")], [K("m2")]); yield
            t2(He[:, 0, :], m1[:], m2[:], ALU.subtract, [K("m1"), K("m2")], [K("He"), K("m1"), K("m2")]); yield
            t2(m1[:], Gi[:], Ec[:], ALU.mult, [K("Gi"), K("EcEs")], [K("m1")]); yield
            t2(m2[:], Gr[:], Es[:], ALU.mult, [K("Gr"), K("EcEs")], [K("m2")]); yield
            t2(He[:, 1, :], m1[:], m2[:], ALU.add, [K("m1"), K("m2")], [K("He"), K("m1"), K("m2")]); yield
            P.op("act", lambda e: e.copy(HS[:, pr, :, 1:256], He[:, :, 0:255]), reads=[K("He")], writes=["HSp%d" % q]); yield
            P.op("act", lambda e: e.copy(fin_p[:, :, pr:pr + 1], He[:, :, 255:256]), reads=[K("He")], writes=["fin_p"]); yield
            a8r = small[:, 2, pr:pr + 1]; a8i = small[:, 3, pr:pr + 1]; na8i = small[:, 1, pr:pr + 1]
            P.op("dve", lambda e: e.scalar_tensor_tensor(m1[:, 0:16], sT[:, 0, pr, :], a8r, Sr[:, 256:272], ALU.mult, ALU.add),
                 reads=["sT", "s5small", K("Sr")], writes=[K("m1")]); yield
            P.op("dve", lambda e: e.scalar_tensor_tensor(fin_s[:, 0, pr, :], sT[:, 1, pr, :], na8i, m1[:, 0:16], ALU.mult, ALU.add),
                 reads=["sT", "s5small", K("m1")], writes=["fin_s", K("m1")]); yield
            P.op("dve", lambda e: e.scalar_tensor_tensor(m2[:, 0:16], sT[:, 1, pr, :], a8r, Si[:, 256:272], ALU.mult, ALU.add),
                 reads=["sT", "s5small", K("Si")], writes=[K("m2")]); yield
            P.op("dve", lambda e: e.scalar_tensor_tensor(fin_s[:, 1, pr, :], sT[:, 0, pr, :], a8i, m2[:, 0:16], ALU.mult, ALU.add),
                 reads=["sT", "s5small", K("m2")], writes=["fin_s", K("m2")]); yield

        def c_gen(q):
            for j in range(8):
                bank = 2 + j % 2
                first = True
                for tau in range(j + 1):
                    P.op("pe", lambda e, q=q, j=j, tau=tau, bank=bank, first=first: e.matmul(pb[bank][:, 0:NCH], KC[:, q, tau, :], uT[:, q, j - tau, :], start=first, stop=False),
                         reads=["KC", "uT"], writes=["pb%d" % bank])
                    first = False
                yield
                for i in range(4):
                    pr = 4 * q + i
                    for ri in range(2):
                        last = (ri == 1)
                        P.op("pe", lambda e, j=j, ri=ri, pr=pr, i=i, bank=bank, last=last: e.matmul(pb[bank][32 * i:32 * i + 32, 0:NCH], LC[:, j, ri, pr, :], HS[:, pr, ri, :],
                                                                                                    start=False, stop=last, tile_position=(0, 32 * i)),
                             reads=["LC", "HSs", "HSp%d" % q, "HS0"], writes=["pb%d" % bank])
                P.op("act", lambda e, q=q, j=j, bank=bank: e.activation(gy5T[:, q, :, j], pb[bank][:, 0:NCH], AF.Gelu_apprx_tanh), reads=["pb%d" % bank], writes=["gy5T"]); yield


        done_q = 0
        pend_c = []
        for g_ in ([0, 1, 2], [3, 4, 5], [6, 7, 8], [9, 10, 11], [12, 13, 14], [15]):
            gens_ = [pair_gen(pr_, k_) for k_, pr_ in enumerate(g_)] + [c_gen(q_) for q_ in pend_c]
            pend_c = []
            rr(gens_)
            while done_q < 4 and 4 * done_q + 3 <= g_[-1]:
                pend_c.append(done_q)
                done_q += 1
        rr([c_gen(q_) for q_ in pend_c])

        st_out = st_in
        for ri, (dst_s, dst_p) in enumerate(((dr["sreo"], dr["pre"]), (dr["simo"], dr["pim"]))):
            so = st_out[ri]
            for pr in range(16):
                bank = 4 + pr // 4
                P.op("pe", lambda e, ri=ri, pr=pr, bank=bank: e.transpose(pb[bank][0:16, (pr % 4) * 128:(pr % 4 + 1) * 128], fin_s[:, ri, pr, :], ident[:]),
                     reads=["fin_s", "ident"], writes=["pb%d" % bank])
            for b4 in range(4):
                P.op("act", lambda e, so=so, b4=b4: e.copy(so[:, b4 * 512:(b4 + 1) * 512], pb[4 + b4][0:16, :]), reads=["pb%d" % (4 + b4)], writes=["st_in0"])
            P.dma("sp", dst_s, so[:], reads=["st_in0"], writes=["st_in0"])
            for gi in range(2):
                P.dma("sp", dst_p.rearrange("(pr gi) p -> gi p pr", gi=2)[gi], fin_p[gi * 64:(gi + 1) * 64, ri, :], reads=["fin_p"], allow_slow_non_contiguous=True)
        P.barrier()


def ffn_phase(nc, P, mk, sb, pb, ident, h2T, modT, tm_gate, fg_bc, pre, rs_all, junk, dr, cw, dump, stop_after=None):
    with contextlib.ExitStack() as ph:
        wdn = mk(ph, "wdn", [128, 22, D], BF16)

        def load_wdn(piece):
            h, kk = divmod(piece, 2)
            P.dma("pool", wdn[:, kk * 11:(kk + 1) * 11, h * 256:(h + 1) * 256], cw(dr["wdn"][kk * 1408:(kk + 1) * 1408, h * 256:(h + 1) * 256]), writes=["wdn"])

        NW = 2
        wug = [mk(ph, "wug%d" % i, [128, 8, 128], BF16) for i in range(NW)]
        wuv = [mk(ph, "wuv%d" % i, [128, 8, 128], BF16) for i in range(NW)]

        def load_up(i, slot):
            P.dma("pool", wug[slot][:], cw(dr["wup"][:, i * 128:(i + 1) * 128]), writes=["wug%d" % slot])
            P.dma("pool", wuv[slot][:], cw(dr["wup"][:, DFF + i * 128:DFF + (i + 1) * 128]), writes=["wuv%d" % slot])
        load_up(0, 0)
        wbase = 0
        preloaded = 1
        gf_p, gf_s, pastT, wc, bcv = pre
        cvo_s = mk(ph, "cvo_s", [128, 44, 32])
        cvo_p = mk(ph, "cvo_p", [128, 44, 2])
        carry = mk(ph, "carry", [128, 44, 2])
        carry2 = mk(ph, "carry2", [128, 44, 2])

        aT = mk(ph, "aT", [128, 22, 1152], BF16)
        Cg = [mk(ph, "Cg%d" % i, [128, 512]) for i in range(2)]
        Cv = [mk(ph, "Cv%d" % i, [128, 512]) for i in range(2)]
        Gg = [mk(ph, "Gg%d" % i, [128, 512]) for i in range(2)]
        Ux = [mk(ph, "Ux%d" % i, [128, 16, 10]) for i in range(2)]
        x1t = [mk(ph, "x1t%d" % i, [128, D]) for i in range(2)]
        yt = [mk(ph, "yt%d" % i, [128, D]) for i in range(2)]

        groups = [[0, 1], [2, 3, 4]]
        it = 0


        def up_s1(item):
            i, bi, slot, first_load, a0, n = item
            par = n % 2
            t0, tn = TBS[bi]
            if first_load is not None:
                load_up(*first_load)
                if n < 2 * 22 and 2 <= i < 10 and bi == 0:
                    load_wdn(i - 2)
            banks = (0, 1) if par == 0 else (2, 3)
            for half, wsl in ((0, wug[slot]), (1, wuv[slot])):
                wk = ("wug%d" if half == 0 else "wuv%d") % slot
                bank = banks[half]
                for k in range(8):
                    P.op("pe", lambda e, k=k, bank=bank, wsl=wsl: e.matmul(pb[bank][:, 0:tn], wsl[:, k, :], h2T[:, k, t0:t0 + tn], start=(k == 0), stop=(k == 7)),
                         reads=[wk, "B_dstT"], writes=["pb%d" % bank])
            info = []
            for half in range(2):
                c = i + 22 * half
                bank = banks[half]
                Ct = (Cg if half == 0 else Cv)[par]
                Ck = ("Cg%d" if half == 0 else "Cv%d") % par
                info.append((c, bank, "pb%d" % bank, Ct, Ck, wc[:, c, 0:1], wc[:, c, 1:2], wc[:, c, 2:3], bcv[:, c:c + 1]))
            if bi < 4:
                for (c, bank, bk, Ct, Ck, w0, w1, w2, bb) in info:
                    P.op("act", lambda e, bank=bank, Ct=Ct, w2=w2, bb=bb: e.activation(Ct[:, 0:tn], pb[bank][:, 0:tn], AF.Identity, bias=bb, scale=w2), reads=[bk, "wc", "bcv"], writes=[Ck])
                for (c, bank, bk, Ct, Ck, w0, w1, w2, bb) in info:
                    P.op("dve", lambda e, bank=bank, Ct=Ct, w1=w1: e.scalar_tensor_tensor(Ct[:, 1:tn], pb[bank][:, 0:tn - 1], w1, Ct[:, 1:tn], ALU.mult, ALU.add), reads=[bk, "wc", Ck], writes=[Ck])
                    P.op("dve", lambda e, bank=bank, Ct=Ct, w0=w0: e.scalar_tensor_tensor(Ct[:, 2:tn], pb[bank][:, 0:tn - 2], w0, Ct[:, 2:tn], ALU.mult, ALU.add), reads=[bk, "wc", Ck], writes=[Ck])
                    if bi < 3:
                        cw_ = carry if bi % 2 == 0 else carry2
                        P.op("dve", lambda e, c=c, bank=bank, cw_=cw_: e.tensor_copy(cw_[:, c, :], pb[bank][:, tn - 2:tn]), reads=[bk], writes=["carry%d_%d" % (bi % 2, c)])
                    else:
                        P.op("dve", lambda e, c=c, bank=bank: e.tensor_copy(cvo_p[:, c, :], pb[bank][:, tn - 2:tn]), reads=[bk], writes=["cvo_p"])
                if bi > 0:
                    cr_ = carry if (bi - 1) % 2 == 0 else carry2
                    for (c, bank, bk, Ct, Ck, w0, w1, w2, bb) in info:
                        ck = "carry%d_%d" % ((bi - 1) % 2, c)
                        P.op("dve", lambda e, c=c, Ct=Ct, w1=w1: e.scalar_tensor_tensor(Ct[:, 0:1], cr_[:, c, 1:2], w1, Ct[:, 0:1], ALU.mult, ALU.add), reads=[ck, "wc", Ck], writes=[Ck])
                    for (c, bank, bk, Ct, Ck, w0, w1, w2, bb) in info:
                        ck = "carry%d_%d" % ((bi - 1) % 2, c)
                        P.op("dve", lambda e, c=c, Ct=Ct, w0=w0: e.scalar_tensor_tensor(Ct[:, 0:2], cr_[:, c, 0:2], w0, Ct[:, 0:2], ALU.mult, ALU.add), reads=[ck, "wc", Ck], writes=[Ck])

            else:
                for hh, (c, bank, bk, Ct, Ck, w0, w1, w2, bb) in enumerate(info):
                    U = Ux[hh]; Uk = "Ux%d" % hh
                    C3 = Ct[:, 0:128].rearrange("p (n j) -> p n j", j=8)
                    P.op("act", lambda e, bank=bank, U=U: e.copy(U[:, :, 2:10], pb[bank][:, 0:128].rearrange("p (n j) -> p n j", j=8)), reads=[bk], writes=[Uk])
                    P.op("dve", lambda e, c=c, U=U: e.tensor_copy(U[:, :, 0:2], pastT[:, c, :].rearrange("p (n k) -> p n k", k=2)), reads=["pastT"], writes=[Uk])
                    P.op("dve", lambda e, U=U, C3=C3, w2=w2, bb=bb: e.tensor_scalar(C3, U[:, :, 2:10], w2, bb, ALU.mult, ALU.add), reads=[Uk, "wc", "bcv"], writes=[Ck])
                    P.op("dve", lambda e, U=U, C3=C3, w1=w1: e.scalar_tensor_tensor(C3, U[:, :, 1:9], w1, C3, ALU.mult, ALU.add), reads=[Uk, "wc", Ck], writes=[Ck])
                    P.op("dve", lambda e, U=U, C3=C3, w0=w0: e.scalar_tensor_tensor(C3, U[:, :, 0:8], w0, C3, ALU.mult, ALU.add), reads=[Uk, "wc", Ck], writes=[Ck])
                    P.op("act", lambda e, c=c, U=U: e.copy(cvo_s[:, c, :].rearrange("p (n k) -> p n k", k=2), U[:, :, 8:10]), reads=[Uk], writes=["cvo_s"])

        def up_s2(item):
            i, bi, slot, first_load, a0, n = item
            par = n % 2
            t0, tn = TBS[bi]
            G = Gg[par]; Gk = "Gg%d" % par
            Cgt = Cg[par]; Cvt = Cv[par]
            P.op("act", lambda e: e.activation(G[:, 0:tn], Cgt[:, 0:tn], AF.Gelu_apprx_tanh), reads=["Cg%d" % par], writes=[Gk])
            P.op("pool", lambda e: e.tensor_tensor(aT[:, i, t0 - a0:t0 - a0 + tn], G[:, 0:tn], Cvt[:, 0:tn], ALU.mult),
                 reads=[Gk, "Cv%d" % par], writes=["aT"])

        nblk = 0
        ntile = 0
        if stop_after == "F0":
            return
        for gidx, grp in enumerate(groups):
            a0 = TBS[grp[0]][0]
            items = []
            for i in range(22):
                slot = (wbase + i) % NW
                fl = None
                if i + 1 < 22 and i + 1 >= preloaded:
                    fl = (i + 1, (wbase + i + 1) % NW)
                for bi in grp:
                    items.append((i, bi, slot, fl, a0, nblk))
                    fl = None
                    nblk += 1
            LAG = 1
            for step in range(len(items) + LAG):
                if step < len(items):
                    up_s1(items[step])
                if step >= LAG:
                    up_s2(items[step - LAG])
            if stop_after == "F1":
                return
            if gidx + 1 < len(groups):
                wbase = (wbase + 22) % NW
                load_up(0, wbase)
                load_up(1, (wbase + 1) % NW)
                preloaded = 2
            else:
                for k in range(2):
                    P.dma("sp", dr["pcv"][k].rearrange("(c p) -> p c", p=128), cvo_p[:, :, k], reads=["cvo_p"])
            for bi in grp:
                t0, tn = TBS[bi]
                for tt_ in range(tn // 128):
                    t = t0 // 128 + tt_
                    loc = t0 - a0 + tt_ * 128
                    par = ntile % 2
                    ntile += 1
                    x1 = x1t[par]; x1k = "x1t%d" % par
                    P.dma("sp", x1[:], dr["x1"][t * 128:(t + 1) * 128, :], reads=["x1_d%d" % t], writes=[x1k])
                    dbanks = (4, 5) if par == 0 else (6, 7)
                    for h in range(2):
                        bank = dbanks[h]
                        for kc in range(22):
                            P.op("pe", lambda e, kc=kc, h=h, loc=loc, bank=bank: e.matmul(pb[bank][:], aT[:, kc, loc:loc + 128], wdn[:, kc, h * 512:(h + 1) * 512], start=(kc == 0), stop=(kc == 21)),
                                 reads=["aT", "wdn"], writes=["pb%d" % bank])
                    y = yt[par]; yk = "yt%d" % par
                    g = gf_p if t < 16 else gf_s
                    for h in range(2):
                        P.op("dve", lambda e, h=h, y=y, g=g, dbanks=dbanks: e.tensor_tensor(y[:, h * 512:(h + 1) * 512], pb[dbanks[h]][:], g[:, h * 512:(h + 1) * 512], ALU.mult),
                             reads=["pb%d" % dbanks[h], "gf_p", "gf_s"], writes=[yk])
                    P.op("pool", lambda e, y=y, x1=x1: e.tensor_tensor(y[:], y[:], x1[:], ALU.add), reads=[yk, x1k], writes=[yk])
                    ss = rs_all[:, 2 * NT + t: 2 * NT + t + 1]
                    ssk = "F_ss%d" % t
                    P.op("act", lambda e, y=y, ss=ss: e.activation(junk[:], y[:], AF.Square, accum_out=ss), reads=[yk], writes=["junk", ssk])
                    P.op("dve", lambda e, ss=ss: e.tensor_scalar(ss, ss, 1.0 / D, EPS, ALU.mult, ALU.add), reads=[ssk], writes=[ssk])
                    P.op("act", lambda e, ss=ss: e.activation(ss, ss, AF.Sqrt), reads=[ssk], writes=[ssk])
                    P.op("dve", lambda e, ss=ss: e.reciprocal(ss, ss), reads=[ssk], writes=[ssk])
                    P.op("dve", lambda e, y=y, ss=ss: e.scalar_tensor_tensor(y[:], y[:], ss, fg_bc[:], ALU.mult, ALU.mult), reads=[yk, ssk, "fg_bc"], writes=[yk])
                    P.dma("sp", dr["y"][t * 128:(t + 1) * 128, :], y[:], reads=[yk])

        so = [mk(ph, "cv_so%d" % i, [32, 512]) for i in range(2)]
        for c in range(44):
            bank = 4 + c // 4 % 4
            P.op("pe", lambda e, c=c, bank=bank: e.transpose(pb[bank][0:32, (c % 4) * 128:(c % 4 + 1) * 128], cvo_s[:, c, :], ident[:]),
                 reads=["cvo_s", "ident"], writes=["pb%d" % bank])
            if c % 4 == 3:
                c0 = c - 3
                sx = so[(c // 4) % 2]; sk = "cv_so%d" % ((c // 4) % 2)
                P.op("act", lambda e, sx=sx, bank=bank: e.copy(sx[:], pb[bank][0:32, :]), reads=["pb%d" % bank], writes=[sk])
                P.dma("sp", dr["scvo"][:, c0 * 128:(c0 + 4) * 128], sx[:], reads=[sk])


_NC_CACHE = {}


def _get_nc():
    if "nc" not in _NC_CACHE:
        _NC_CACHE["nc"] = build_program()
    return _NC_CACHE["nc"]


def make_in_maps(inputs):
    f = lambda a: np.ascontiguousarray(np.asarray(a, dtype=np.float32))
    shared = {}
    for k in ("norm1_g", "norm2_g", "w_ada", "b_ada", "w_in", "s5_a_re", "s5_a_im", "s5_log_dt", "s5_b_re", "s5_b_im",
              "s5_c_re", "s5_c_im", "s5_d", "w_s5_glu", "b_s5_glu", "gm_ln_g", "gm_ln_b", "gm_w_sp", "gm_b_sp",
              "w_gm_out", "w_out", "w_up", "w_conv", "b_conv", "w_down"):
        shared[k] = f(np.asarray(inputs[k])[0])
    shared["final_g"] = f(inputs["final_g"])
    xp = np.asarray(inputs["x_prompt"]); xs = np.asarray(inputs["x_sample"])
    cp = np.asarray(inputs["c_prompt"]); cs = np.asarray(inputs["c_sample"])
    sre = np.asarray(inputs["state_ssm_re"])[0]; sim = np.asarray(inputs["state_ssm_im"])[0]
    scv = np.asarray(inputs["state_ffn_conv"])[0]
    maps = []
    for i in range(NCORES):
        sl = slice(16 * i, 16 * i + 16)
        m = dict(shared)
        m["x"] = f(np.concatenate([xp[i], xs[sl].reshape(128, D)], axis=0))
        m["c"] = f(np.concatenate([cp[i:i + 1], cs[sl]], axis=0))
        m["sre"] = f(sre[sl].reshape(16, 2048))
        m["sim"] = f(sim[sl].reshape(16, 2048))
        m["scv"] = f(scv[sl].reshape(32, 2 * DFF))
        maps.append(m)
    return maps


def kernel(**inputs):
    nc = _get_nc()
    maps = make_in_maps(inputs)
    res = run_bass_kernel_spmd(nc, maps, core_ids=list(range(NCORES)))
    r = res.results
    y_p = np.stack([r[i]["y"][:2048] for i in range(NCORES)], axis=0)
    y_s = np.concatenate([r[i]["y"][2048:].reshape(16, 8, D) for i in range(NCORES)], axis=0)
    p_re = np.stack([r[i]["p_re"] for i in range(NCORES)], axis=0)[None]
    p_im = np.stack([r[i]["p_im"] for i in range(NCORES)], axis=0)[None]
    p_cv = np.stack([r[i]["p_conv"] for i in range(NCORES)], axis=0)[None]
    s_re = np.concatenate([r[i]["s_re"].reshape(16, 32, 64) for i in range(NCORES)], axis=0)[None]
    s_im = np.concatenate([r[i]["s_im"].reshape(16, 32, 64) for i in range(NCORES)], axis=0)[None]
    s_cv = np.concatenate([r[i]["s_conv"].reshape(16, 2, 2 * DFF) for i in range(NCORES)], axis=0)[None]
    s_v = np.concatenate([r[i]["s_v"].reshape(16, 8, 512) for i in range(NCORES)], axis=0)[None]
    outs = (y_p, y_s, p_re, p_im, p_cv, s_re, s_im, s_cv, s_v)
    return tuple(np.ascontiguousarray(o, dtype=np.float32) for o in outs)
```

```python
import contextlib
import math
import numpy as np
import concourse.bass as bass
import concourse.mybir as mybir
from concourse.bass_utils import run_bass_kernel_spmd

F32 = mybir.dt.float32
BF16 = mybir.dt.bfloat16
I32 = mybir.dt.int32
AF = mybir.ActivationFunctionType
ALU = mybir.AluOpType

NCORES = 8
D = 1024
T = 2176
NT = 17
NCH = 272
DFF = 2816
EPS = 1e-6
TBS = [(0, 512), (512, 512), (1024, 512), (1536, 512), (2048, 128)]

EPOCH = 6000
H1 = True
DMA_RING = 12
SAME_ENGINE_SYNC = True


class Prog:
    ENG = ("pe", "act", "dve", "pool", "sp")

    def __init__(self, nc, stack):
        self.nc = nc
        self.stack = stack
        self.streams = {e: [] for e in self.ENG}
        self.count = {e: 0 for e in self.ENG}
        self.sems = {}
        self.waited = {e: {} for e in self.ENG}
        self.last_w = {}
        self.readers = {}
        self.dma_n = {e: 0 for e in self.ENG}
        self.dma_val = {}

    def _sem(self, key):
        if key not in self.sems:
            name = "s_" + "_".join(str(k) for k in key)
            self.sems[key] = self.stack.enter_context(self.nc.semaphore(name))
        return self.sems[key]

    def _add_wait(self, eng, tok, waits):
        key, val = tok
        if key[0] == "eng" and key[1] == eng:
            if eng == "pe" or not SAME_ENGINE_SYNC:
                return
        if self.waited[eng].get(key, 0) >= val:
            return
        self.waited[eng][key] = val
        waits.append((key, val))

    def _deps(self, eng, reads, writes, waits):
        best = {}

        def need(t):
            if t is not None and best.get(t[0], 0) < t[1]:
                best[t[0]] = t[1]
        for k in reads:
            need(self.last_w.get(k))
        for k in writes:
            need(self.last_w.get(k))
            for t in self.readers.get(k, ()):
                need(t)
        for key, val in best.items():
            self._add_wait(eng, (key, val), waits)

    def _record(self, tok, reads, writes):
        for k in reads:
            self.readers.setdefault(k, []).append(tok)
        for k in writes:
            self.last_w[k] = tok
            self.readers[k] = []

    def op(self, eng, fn, reads=(), writes=()):
        waits = []
        self._deps(eng, reads, writes, waits)
        self.count[eng] += 1
        n = self.count[eng]
        key = ("eng", eng, (n - 1) // EPOCH)
        val = (n - 1) % EPOCH + 1
        self._sem(key)
        self.streams[eng].append((waits, fn, key, 1))
        self._record((key, val), reads, writes)

    def dma(self, eng, out, in_, reads=(), writes=(), **kw):
        waits = []
        self._deps(eng, reads, writes, waits)
        slot = self.dma_n[eng] % DMA_RING
        self.dma_n[eng] += 1
        key = ("dma", eng, slot)
        prev = self.dma_val.get(key, 0)
        if prev:
            self._add_wait(eng, (key, prev), waits)
        val = prev + 16
        self.dma_val[key] = val
        self._sem(key)

        kw.setdefault("allow_slow_non_contiguous", True)

        def fn(e, out=out, in_=in_, kw=kw):
            return e.dma_start(out=out, in_=in_, **kw)
        self.streams[eng].append((waits, fn, key, 16))
        self._record((key, val), reads, writes)

    def _all_tokens(self):
        toks = [(k, v) for k, v in self.dma_val.items()]
        for e in self.ENG:
            n = self.count[e]
            if n:
                toks.append((("eng", e, (n - 1) // EPOCH), (n - 1) % EPOCH + 1))
        return toks

    def barrier(self):
        toks = self._all_tokens()
        for e in self.ENG:
            waits = []
            for t in toks:
                if t[0][0] == "eng" and t[0][1] == e:
                    continue
                self._add_wait(e, t, waits)
            if waits:
                self.streams[e].append((waits, None, None, 0))

    def finish(self):
        waits = []
        for t in self._all_tokens():
            self._add_wait("sp", t, waits)
        self.streams["sp"].append((waits, None, None, 0))

    def replay(self):
        nc = self.nc
        sems = self.sems
        streams = self.streams

        def run(e, items):
            for waits, fn, key, inc in items:
                for wkey, wval in waits:
                    e.wait_ge(sems[wkey], wval)
                if fn is not None:
                    fn(e).then_inc(sems[key], inc)

        with nc.Block() as block:
            @block.tensor
            def _(e):
                run(e, streams["pe"])

            @block.scalar
            def _(e):
                run(e, streams["act"])

            @block.vector
            def _(e):
                run(e, streams["dve"])

            @block.gpsimd
            def _(e):
                run(e, streams["pool"])

            @block.sync
            def _(e):
                run(e, streams["sp"])


class Rec:
    def __init__(self):
        self.items = []

    def op(self, eng, fn, reads=(), writes=()):
        self.items.append(("op", eng, fn, list(reads), list(writes)))

    def dma(self, eng, out, in_, reads=(), writes=(), **kw):
        self.items.append(("dma", eng, out, in_, list(reads), list(writes), kw))

    def barrier(self):
        pass

    def gen(self, P, every=1):
        for n, it in enumerate(self.items):
            if it[0] == "op":
                P.op(it[1], it[2], reads=it[3], writes=it[4])
            else:
                P.dma(it[1], it[2], it[3], reads=it[4], writes=it[5], **it[6])
            if n % every == every - 1:
                yield


def build_program(stop_after=None, dumps=()):
    nc = bass.Bass("TRN2", target_bir_lowering=False)

    def din(name, shape):
        return nc.dram_tensor(name, list(shape), F32, kind="ExternalInput").ap()

    def dout(name, shape):
        return nc.dram_tensor(name, list(shape), F32, kind="ExternalOutput").ap()

    x_d = din("x", [T, D])
    c_d = din("c", [17, D])
    sre_d = din("sre", [16, 2048])
    sim_d = din("sim", [16, 2048])
    scv_d = din("scv", [32, 2 * DFF])
    g1_d = din("norm1_g", [D])
    g2_d = din("norm2_g", [D])
    wada_d = din("w_ada", [D, 6 * D])
    bada_d = din("b_ada", [6 * D])
    win_d = din("w_in", [D, 3584])
    are_d = din("s5_a_re", [32, 64])
    aim_d = din("s5_a_im", [32, 64])
    ldt_d = din("s5_log_dt", [32])
    bre_d = din("s5_b_re", [32, 64, 16])
    bim_d = din("s5_b_im", [32, 64, 16])
    cre_d = din("s5_c_re", [32, 16, 64])
    cim_d = din("s5_c_im", [32, 16, 64])
    sd_d = din("s5_d", [32, 16])
    wglu_d = din("w_s5_glu", [512, 2048])
    bglu_d = din("b_s5_glu", [2048])
    lng_d = din("gm_ln_g", [512])
    lnb_d = din("gm_ln_b", [512])
    wsp_d = din("gm_w_sp", [8, 128, 128])
    bsp_d = din("gm_b_sp", [8, 128])
    wgo_d = din("w_gm_out", [512, D])
    wout_d = din("w_out", [D, D])
    wup_d = din("w_up", [D, 2 * DFF])
    wcv_d = din("w_conv", [3, 2 * DFF])
    bcv_d = din("b_conv", [2 * DFF])
    wdn_d = din("w_down", [DFF, D])
    fg_d = din("final_g", [D])

    y_d = dout("y", [T, D])
    pre_d = dout("p_re", [32, 64])
    pim_d = dout("p_im", [32, 64])
    pcv_d = dout("p_conv", [2, 2 * DFF])
    sreo_d = dout("s_re", [16, 2048])
    simo_d = dout("s_im", [16, 2048])
    scvo_d = dout("s_conv", [32, 2 * DFF])
    sv_d = dout("s_v", [128, 512])
    x1_d = nc.dram_tensor("x1_scratch", [T, D], F32, kind="Internal").ap()

    dump_aps = {}

    with contextlib.ExitStack() as st:
        P = Prog(nc, st)

        def mk(stack, name, shape, dt=F32):
            return stack.enter_context(nc.sbuf_tensor(name, list(shape), dt))

        def sb(name, shape, dt=F32):
            return mk(st, name, shape, dt)

        def dump(name, ap, shape, reads, dt=F32):
            if name in dumps:
                d = nc.dram_tensor("dbg_" + name, list(shape), dt, kind="ExternalOutput").ap()
                P.dma("sp", d, ap, reads=reads)

        pb = [st.enter_context(nc.psum_tensor("pb%d" % i, [128, 512], F32)) for i in range(8)]

        def cw(wdr):
            return wdr.rearrange("(k p) n -> p k n", p=128)

        ident = sb("ident", [128, 128])
        io_i = sb("io_i", [128, 128], I32)
        P.op("pool", lambda e: e.iota(io_i[:], [[1, 128]], base=0, channel_multiplier=-1), writes=["io_i"])
        P.op("dve", lambda e: e.tensor_single_scalar(ident[:], io_i[:], 0, ALU.is_equal), reads=["io_i"], writes=["ident"])

        scr = dict(
            LA=nc.dram_tensor("scr_LA", [128, 8192], BF16, kind="Internal").ap(),
            LC=nc.dram_tensor("scr_LC", [128, 8192], BF16, kind="Internal").ap(),
            KC=nc.dram_tensor("scr_KC", [128, 4096], BF16, kind="Internal").ap(),
            tab=nc.dram_tensor("scr_tab", [128, 2, 16, 256], F32, kind="Internal").ap(),
            small=nc.dram_tensor("scr_small", [128, 4, 16], F32, kind="Internal").ap())
        s5dr = dict(are=are_d, aim=aim_d, ldt=ldt_d, bre=bre_d, bim=bim_d, cre=cre_d, cim=cim_d, sd=sd_d,
                    sre=sre_d, sim=sim_d, sreo=sreo_d, simo=simo_d, pre=pre_d, pim=pim_d)
        TH8 = sb("TH8", [128, 16])
        pastT = sb("pastT", [128, 44, 32])
        gf_p = sb("gf_p", [128, D])
        gf_s = sb("gf_s", [128, D])
        wc = sb("wc", [128, 44, 3])
        bcv = sb("bcv", [128, 44])
        modT = sb("modT", [128, 48, 17])
        gs1 = sb("gs1", [128, 8, 17])
        gs2 = sb("gs2", [128, 8, 17])
        g1T = sb("g1T", [128, 8])
        g2T = sb("g2T", [128, 8])
        bT = sb("bT", [128, 48])
        fg_bc = sb("fg_bc", [128, D])
        with contextlib.ExitStack() as ph:
            P_real = P
            P = Rec()
            ct = mk(ph, "ct", [17, D])
            cT = mk(ph, "cT", [128, 8, 17], BF16)
            wad = [mk(ph, "wad%d" % i, [128, 8, 128], BF16) for i in range(4)]
            P.dma("sp", ct[:], c_d, writes=["ct"])
            P.dma("act", bT[:], bada_d.rearrange("(b p) -> p b", p=128), writes=["bT"], allow_slow_non_contiguous=True)
            P.dma("act", g1T[:], g1_d.rearrange("(k p) -> p k", p=128), writes=["g1T"], allow_slow_non_contiguous=True)
            P.dma("act", g2T[:], g2_d.rearrange("(k p) -> p k", p=128), writes=["g2T"], allow_slow_non_contiguous=True)
            P.dma("sp", fg_bc[:], fg_d.rearrange("(o n) -> o n", o=1).to_broadcast([128, D]), writes=["fg_bc"])
            P.op("act", lambda e: e.activation(ct[:], ct[:], AF.Silu), reads=["ct"], writes=["ct"])
            for k in range(8):
                P.op("pe", lambda e, k=k: e.transpose(pb[0][:, k * 32:k * 32 + 17], ct[:, k * 128:(k + 1) * 128], ident[0:17, 0:17]),
                     reads=["ct", "ident"], writes=["pb0"])
            P.op("act", lambda e: e.copy(cT[:], pb[0][:, 0:256].rearrange("p (k c) -> p k c", c=32)[:, :, 0:17]),
                 reads=["pb0"], writes=["cT"])
            for blk in range(48):
                w = wad[blk % 4]
                wk = "wad%d" % (blk % 4)
                P.dma("pool", w[:], cw(wada_d[:, blk * 128:(blk + 1) * 128]), writes=[wk])
                bank = 1 + blk % 2
                for k in range(8):
                    P.op("pe", lambda e, k=k, w=w, bank=bank: e.matmul(pb[bank][:, 0:17], w[:, k, :], cT[:, k, :], start=(k == 0), stop=(k == 7)),
                         reads=[wk, "cT"], writes=["pb%d" % bank])
                P.op("act", lambda e, blk=blk, bank=bank: e.activation(modT[:, blk, :], pb[bank][:, 0:17], AF.Identity, bias=bT[:, blk:blk + 1]),
                     reads=["pb%d" % bank, "bT"], writes=["modT"])
            for k in range(8):
                P.op("dve", lambda e, k=k: e.tensor_scalar(gs1[:, k, :], modT[:, 8 + k, :], 1.0, g1T[:, k:k + 1], ALU.add, ALU.mult),
                     reads=["modT", "g1T"], writes=["gs1"])
                P.op("dve", lambda e, k=k: e.tensor_scalar(gs2[:, k, :], modT[:, 32 + k, :], 1.0, g2T[:, k:k + 1], ALU.add, ALU.mult),
                     reads=["modT", "g2T"], writes=["gs2"])
            rec_mod = P
            P = Rec()
            s5_prep(nc, P, mk, ph, pb, ident, s5dr, scr, TH8)
            rec_prep = P
            P = P_real
            gens = [rec_mod.gen(P), rec_prep.gen(P)]
            while gens:
                for g_ in list(gens):
                    try:
                        next(g_)
                    except StopIteration:
                        gens.remove(g_)
            past_in = mk(ph, "past_in", [32, 2 * DFF])
            P.dma("sp", past_in[:], scv_d, writes=["past_in"])
            for c in range(44):
                bank = c // 16
                P.op("pe", lambda e, c=c, bank=bank: e.transpose(pb[bank][:, (c % 16) * 32:(c % 16 + 1) * 32], past_in[:, c * 128:(c + 1) * 128], ident[0:32, 0:32]),
                     reads=["past_in", "ident"], writes=["pb%d" % bank])
            for bank in range(3):
                n_ = 16 if bank < 2 else 12
                P.op("dve", lambda e, bank=bank, n_=n_: e.tensor_copy(pastT[:, bank * 16:bank * 16 + n_, :], pb[bank][:, 0:n_ * 32].rearrange("p (c m) -> p c m", m=32)),
                     reads=["pb%d" % bank], writes=["pastT"])
            P.barrier()
        dump("modT", modT[:], [128, 48, 17], ["modT"])

        def tm_gate(dst_p, dst_s, blk0, tag):
            for k in range(8):
                P.op("dve", lambda e, k=k: e.tensor_copy(bc_p[:], modT[:, blk0 + k, 0:1].to_broadcast([128, 128])),
                     reads=["modT"], writes=["bc_p"])
                P.op("dve", lambda e, k=k: e.tensor_copy(bc_s[:].rearrange("p (n j) -> p n j", j=8),
                                                         modT[:, blk0 + k, 1:17].unsqueeze(2).to_broadcast([128, 16, 8])),
                     reads=["modT"], writes=["bc_s"])
                P.op("pe", lambda e, k=k: e.matmul(pb[0][:, k * 128:(k + 1) * 128] if k < 4 else pb[1][:, (k - 4) * 128:(k - 3) * 128],
                                                   bc_p[:], ident[:], start=True, stop=True),
                     reads=["bc_p", "ident"], writes=["pb0" if k < 4 else "pb1"])
                P.op("pe", lambda e, k=k: e.matmul(pb[2][:, k * 128:(k + 1) * 128] if k < 4 else pb[3][:, (k - 4) * 128:(k - 3) * 128],
                                                   bc_s[:], ident[:], start=True, stop=True),
                     reads=["bc_s", "ident"], writes=["pb2" if k < 4 else "pb3"])
            for h in range(2):
                P.op("act", lambda e, h=h: e.copy(dst_p[:, h * 512:(h + 1) * 512], pb[h][:]), reads=["pb%d" % h], writes=[tag + "_p"])
                P.op("act", lambda e, h=h: e.copy(dst_s[:, h * 512:(h + 1) * 512], pb[2 + h][:]), reads=["pb%d" % (2 + h)], writes=[tag + "_s"])

        bc_p = sb("bc_p", [128, 128])
        bc_s = sb("bc_s", [128, 128])
        tm_gate(gf_p, gf_s, 40, "gf")

        hT = sb("hT", [128, 8, T], BF16)
        rs_all = sb("rs_all", [128, 4 * NT])

        def run_pipeline(stages, n):
            for _ in pipeline_gen(stages, n):
                pass

        def pipeline_gen(stages, n):
            nst = len(stages)
            for step in range(n + nst - 1):
                gens = []
                for s_, f in enumerate(stages):
                    idx = step - s_
                    if 0 <= idx < n:
                        gens.append(f(idx))
                while gens:
                    for g_ in list(gens):
                        try:
                            next(g_)
                        except StopIteration:
                            gens.remove(g_)
                    yield

        def norm_stages(srcf, dstT, gs, shblk, ssk, tagp):
            def n1(t):
                src, srck = srcf(t)
                ss = rs_all[:, ssk * NT + t: ssk * NT + t + 1]
                sk = tagp + "ss%d" % t
                P.op("act", lambda e: e.activation(junk[:], src, AF.Square, accum_out=ss), reads=[srck], writes=["junk", sk]); yield
                P.op("dve", lambda e: e.tensor_scalar(ss, ss, 1.0 / D, EPS, ALU.mult, ALU.add), reads=[sk], writes=[sk]); yield
                P.op("act", lambda e: e.activation(ss, ss, AF.Sqrt), reads=[sk], writes=[sk]); yield
                P.op("dve", lambda e: e.reciprocal(ss, ss), reads=[sk], writes=[sk]); yield

            def n2(t):
                src, srck = srcf(t)
                ss = rs_all[:, ssk * NT + t: ssk * NT + t + 1]
                sk = tagp + "ss%d" % t
                xn = rings["xn"][t % 2]
                xnk = "xn%d" % (t % 2)
                P.op("pool", lambda e: e.tensor_scalar(xn[:], src, ss, 0.0, ALU.mult, ALU.add), reads=[srck, sk], writes=[xnk]); yield
                b0 = 4 + 2 * (t % 2)
                for k in range(8):
                    bank = b0 + k // 4
                    P.op("pe", lambda e, k=k, bank=bank: e.transpose(pb[bank][:, (k % 4) * 128:(k % 4 + 1) * 128], xn[:, k * 128:(k + 1) * 128], ident[:]),
                         reads=[xnk, "ident"], writes=["pb%d" % bank])
                    if k % 4 == 3:
                        yield

            def n3(t):
                b0 = 4 + 2 * (t % 2)
                for k in range(8):
                    bank = b0 + k // 4
                    src_ps = pb[bank][:, (k % 4) * 128:(k % 4 + 1) * 128]
                    dst = dstT[:, k, t * 128:(t + 1) * 128]
                    if t < 16:
                        if k % 2 == 0:
                            P.op("act", lambda e, k=k, src_ps=src_ps, dst=dst: e.activation(dst, src_ps, AF.Identity, bias=modT[:, shblk + k, 0:1], scale=gs[:, k, 0:1]),
                                 reads=["pb%d" % bank, "modT", "gs1", "gs2"], writes=[tagp + "dstT"])
                        else:
                            P.op("dve", lambda e, k=k, src_ps=src_ps, dst=dst: e.tensor_scalar(dst, src_ps, gs[:, k, 0:1], modT[:, shblk + k, 0:1], ALU.mult, ALU.add),
                                 reads=["pb%d" % bank, "modT", "gs1", "gs2"], writes=[tagp + "dstT"])
                    else:
                        tm = tmp128[k % 2]; tk = "tmp128_%d" % (k % 2)
                        P.op("dve", lambda e, k=k, src_ps=src_ps, tm=tm: e.tensor_tensor(tm[:].rearrange("p (n j) -> p n j", j=8),
                                                                                          src_ps.rearrange("p (n j) -> p n j", j=8),
                                                                                          gs[:, k, 1:17].unsqueeze(2).to_broadcast([128, 16, 8]), ALU.mult),
                             reads=["pb%d" % bank, "gs1", "gs2"], writes=[tk])
                        P.op("dve", lambda e, k=k, dst=dst, tm=tm: e.tensor_tensor(dst.rearrange("p (n j) -> p n j", j=8),
                                                                                    tm[:].rearrange("p (n j) -> p n j", j=8),
                                                                                    modT[:, shblk + k, 1:17].unsqueeze(2).to_broadcast([128, 16, 8]), ALU.add),
                             reads=[tk, "modT"], writes=[tagp + "dstT"])
                    yield
            return [n1, n2, n3]

        junk = sb("junk", [128, D], BF16)
        tmp128 = [sb("tmp128_%d" % i, [128, 128]) for i in range(2)]
        rings = {}

        with contextlib.ExitStack() as phT, contextlib.ExitStack() as ph:
            rec_tab = Rec()
            s5_tables(nc, rec_tab, mk, phT, TH8, scr)
            rings["xn"] = [mk(ph, "xn%d" % i, [128, D]) for i in range(2)]
            xt_ring = [mk(ph, "xt%d" % i, [128, D]) for i in range(4)]
            def a0(t):
                P.dma("sp", xt_ring[t % 4][:], x_d[t * 128:(t + 1) * 128, :], writes=["xt%d" % (t % 4)]); yield
            gens = [(pipeline_gen([a0] + norm_stages(lambda t: (xt_ring[t % 4][:], "xt%d" % (t % 4)), hT, gs1, 0, 0, "A_"), NT), 2), (rec_tab.gen(P), 1)]
            while gens:
                for g_ in list(gens):
                    try:
                        for _ in range(g_[1]):
                            next(g_[0])
                    except StopIteration:
                        gens.remove(g_)
            P.barrier()
        dump("hT", hT[:], [128, 8, T], ["A_dstT"], BF16)
        if stop_after == "A":
            P.finish(); P.replay(); return nc

        mix = contextlib.ExitStack()
        uT = mk(mix, "uT", [128, 4, 8, NCH], BF16)
        gy5T = mk(mix, "gy5T", [128, 4, NCH, 8], BF16)
        gy5Tf = gy5T[:].rearrange("p q c s -> p q (c s)")
        with contextlib.ExitStack() as ph:
            wU = mk(ph, "wU", [128, 8, 512], BF16)
            P.dma("pool", wU[:], cw(win_d[:, 0:512]), writes=["wU"])
            n = 0
            for q in range(4):
                for (t0, tn) in TBS:
                    bank = n % 4
                    n += 1
                    for k in range(8):
                        P.op("pe", lambda e, k=k, q=q, t0=t0, tn=tn, bank=bank: e.matmul(pb[bank][:, 0:tn], wU[:, k, q * 128:(q + 1) * 128], hT[:, k, t0:t0 + tn], start=(k == 0), stop=(k == 7)),
                             reads=["wU", "A_dstT"], writes=["pb%d" % bank])
                    eng = "act" if n % 2 else "dve"
                    if eng == "act":
                        P.op("act", lambda e, q=q, t0=t0, tn=tn, bank=bank: e.copy(uT[:, q, :, t0 // 8:(t0 + tn) // 8], pb[bank][:, 0:tn].rearrange("p (c s) -> p s c", s=8)), reads=["pb%d" % bank], writes=["uT"])
                    else:
                        P.op("dve", lambda e, q=q, t0=t0, tn=tn, bank=bank: e.tensor_copy(uT[:, q, :, t0 // 8:(t0 + tn) // 8], pb[bank][:, 0:tn].rearrange("p (c s) -> p s c", s=8)), reads=["pb%d" % bank], writes=["uT"])
            P.barrier()

        fin_p = mk(mix, "fin_p", [128, 2, 16])
        s5_phase(nc, P, mk, sb, pb, ident, uT, gy5T, fin_p, s5dr, scr, dump, dumps)
        dump("gy5T", gy5Tf, [128, 4, T], ["gy5T"], BF16)
        if stop_after == "S5":
            P.finish(); P.replay(); mix.close(); return nc

        ygT = mk(mix, "ygT", [128, 4, T], BF16)
        with contextlib.ExitStack() as ph:
            wGM = mk(ph, "wGM", [128, 8, 1024], BF16)
            P.dma("pool", wGM[:, :, 0:512], cw(win_d[:, 512:1024]), writes=["wGMu"])
            P.dma("pool", wGM[:, :, 512:1024], cw(win_d[:, 1024:1536]), writes=["wGMv"])
            lng_bc = mk(ph, "lng_bc", [128, 512])
            lnb_bc = mk(ph, "lnb_bc", [128, 512])
            P.dma("sp", lng_bc[:], lng_d.rearrange("(o n) -> o n", o=1).to_broadcast([128, 512]), writes=["lng"])
            P.dma("sp", lnb_bc[:], lnb_d.rearrange("(o n) -> o n", o=1).to_broadcast([128, 512]), writes=["lnb"])
            wsp_n = mk(ph, "wsp_n", [128, 8, 128])
            P.dma("sp", wsp_n[:], wsp_d.rearrange("h t s -> t h s"), writes=["wsp_n"])
            WmT = mk(ph, "WmT", [128, 8, 128], BF16)
            WmT32 = mk(ph, "WmT32", [128, 8, 128])
            for h in range(8):
                bank = h // 4
                P.op("pe", lambda e, h=h, bank=bank: e.transpose(pb[bank][:, (h % 4) * 128:(h % 4 + 1) * 128], wsp_n[:, h, :], ident[:]),
                     reads=["wsp_n", "ident"], writes=["pb%d" % bank])
            for bank in range(2):
                P.op("act", lambda e, bank=bank: e.copy(WmT32[:, bank * 4:(bank + 1) * 4, :], pb[bank][:].rearrange("p (h t) -> p h t", t=128)),
                     reads=["pb%d" % bank], writes=["WmT32"])
            P.op("pool", lambda e: e.affine_select(WmT32[:], WmT32[:], [[0, 8], [1, 128]], ALU.is_ge, 0.0, base=0, channel_multiplier=-1),
                 reads=["WmT32"], writes=["WmT32"])
            P.op("act", lambda e: e.copy(WmT[:], WmT32[:]), reads=["WmT32"], writes=["WmT"])
            WmS = mk(ph, "WmS", [128, 8, 128], BF16)
            bm8 = mk(ph, "bm8", [128, 128])
            ti1 = mk(ph, "ti1", [128, 128], I32)
            ti2 = mk(ph, "ti2", [128, 128], I32)
            E8 = mk(ph, "E8", [8, 128])
            A1 = mk(ph, "A1", [8, 8, 128])
            P.op("pool", lambda e: e.iota(ti1[:], [[1, 16], [0, 8]], base=0, channel_multiplier=0), writes=["ti1"])
            P.op("pool", lambda e: e.iota(ti2[:], [[0, 128]], base=0, channel_multiplier=1), writes=["ti2"])
            P.op("dve", lambda e: e.tensor_single_scalar(ti2[:], ti2[:], 3, ALU.arith_shift_right), reads=["ti2"], writes=["ti2"])
            P.op("dve", lambda e: e.tensor_tensor(bm8[:], ti1[:], ti2[:], ALU.is_equal), reads=["ti1", "ti2"], writes=["bm8"])
            P.op("pool", lambda e: e.iota(ti1[0:8, :], [[0, 16], [1, 8]], base=0, channel_multiplier=-1), reads=["bm8"], writes=["ti1"])
            P.op("dve", lambda e: e.tensor_single_scalar(E8[:], ti1[0:8, :], 0, ALU.is_equal), reads=["ti1"], writes=["E8"])
            for h in range(8):
                P.op("dve", lambda e, h=h: e.tensor_copy(A1[:, h, :].rearrange("p (n j) -> p n j", j=8),
                                                         WmT32[0:8, h, 0:8].unsqueeze(1).to_broadcast([8, 16, 8])),
                     reads=["WmT32"], writes=["A1"])
            for h in range(8):
                bank = h // 4
                P.op("pe", lambda e, h=h, bank=bank: e.matmul(pb[bank][:, (h % 4) * 128:(h % 4 + 1) * 128], E8[:], A1[:, h, :], start=True, stop=True),
                     reads=["E8", "A1"], writes=["pb%d" % bank])
            for h in range(8):
                bank = h // 4
                P.op("dve", lambda e, h=h, bank=bank: e.tensor_tensor(WmS[:, h, :], pb[bank][:, (h % 4) * 128:(h % 4 + 1) * 128], bm8[:], ALU.mult),
                     reads=["pb%d" % bank, "bm8"], writes=["WmS"])
            bsp = mk(ph, "bsp", [128, 4, 128])
            for h in range(8):
                P.dma("sp", bsp[(h % 2) * 64:(h % 2 + 1) * 64, h // 2, :], bsp_d[h:h + 1, :].to_broadcast([64, 128]), writes=["bsp"])

            gu_blk = [mk(ph, "gu_blk%d" % i, [128, 4, 512]) for i in range(3)]
            vg = [mk(ph, "vg%d" % i, [128, 512]) for i in range(4)]
            vl = [mk(ph, "vl%d" % i, [128, 512]) for i in range(2)]
            vb = [mk(ph, "vb%d" % i, [128, 512], BF16) for i in range(2)]
            st6 = [mk(ph, "st6_%d" % i, [128, 6]) for i in range(3)]
            mv = [mk(ph, "mv%d" % i, [128, 2]) for i in range(3)]
            stmp = [mk(ph, "stmp%d" % i, [128, 4, 128]) for i in range(2)]
            neghalf = mk(ph, "neghalf", [128, 1])
            P.op("dve", lambda e: e.memset(neghalf[:], -0.5), writes=["neghalf"])

            def g0(t):
                bi = t // 4
                t0, tn = TBS[bi]
                todo = [(0, q) for q in range(4)] if t == 0 else []
                if bi + 1 < len(TBS) and t // 4 == bi and t < 16:
                    todo.append((bi + 1, t % 4))
                for (b2, q) in todo:
                    t0b, tnb = TBS[b2]
                    gb2 = gu_blk[b2 % 3]; gk_ = "gu_blk%d" % (b2 % 3)
                    bank = q % 2
                    for k in range(8):
                        P.op("pe", lambda e, k=k, q=q, bank=bank, t0b=t0b, tnb=tnb: e.matmul(pb[bank][:, 0:tnb], wGM[:, k, q * 128:(q + 1) * 128], hT[:, k, t0b:t0b + tnb], start=(k == 0), stop=(k == 7)),
                             reads=["wGMu", "A_dstT"], writes=["pb%d" % bank])
                    yield
                    P.op("act", lambda e, q=q, bank=bank, gb2=gb2, tnb=tnb: e.activation(gb2[:, q, 0:tnb], pb[bank][:, 0:tnb], AF.Gelu_apprx_tanh),
                         reads=["pb%d" % bank], writes=[gk_]); yield
                vbank = 2 if t % 2 == 0 else 4
                vk = "pb%d" % vbank
                for k in range(8):
                    P.op("pe", lambda e, k=k: e.matmul(pb[vbank][:], hT[:, k, t * 128:(t + 1) * 128], wGM[:, k, 512:1024], start=(k == 0), stop=(k == 7)),
                         reads=["wGMv", "A_dstT"], writes=[vk])
                yield
                g = vg[t % 4]; gk2 = "vg%d" % (t % 4)
                P.op("act", lambda e: e.activation(g[:], pb[vbank][:], AF.Gelu_apprx_tanh), reads=[vk], writes=[gk2]); yield

            def g0b(t):
                g = vg[t % 4]; gk2 = "vg%d" % (t % 4); s6 = st6[t % 3]; m = mv[t % 3]; mk_ = "mv%d" % (t % 3)
                P.op("dve", lambda e: e.bn_stats(s6[:], g[:]), reads=[gk2], writes=[mk_ + "s"]); yield
                P.op("dve", lambda e: e.bn_aggr(m[:], s6[:]), reads=[mk_ + "s"], writes=[mk_]); yield
                P.op("dve", lambda e: e.tensor_scalar(m[:, 1:2], m[:, 1:2], EPS, None, ALU.add), reads=[mk_], writes=[mk_]); yield
                P.op("pool", lambda e: e.tensor_tensor(m[:, 1:2], m[:, 1:2], neghalf[:, 0:1], ALU.pow), reads=[mk_, "neghalf"], writes=[mk_]); yield

            def g1(t):
                g = vg[t % 4]; gk2 = "vg%d" % (t % 4); m = mv[t % 3]; mk_ = "mv%d" % (t % 3)
                l = vl[t % 2]; lk = "vl%d" % (t % 2); b = vb[t % 2]; bk = "vb%d" % (t % 2)
                P.op("dve", lambda e: e.tensor_scalar(g[:], g[:], m[:, 0:1], m[:, 1:2], ALU.subtract, ALU.mult), reads=[gk2, mk_], writes=[gk2]); yield
                P.op("pool", lambda e: e.tensor_tensor(l[:], g[:], lng_bc[:], ALU.mult), reads=[gk2, "lng"], writes=[lk]); yield
                P.op("pool", lambda e: e.tensor_tensor(l[:], l[:], lnb_bc[:], ALU.add), reads=[lk, "lnb"], writes=[lk]); yield
                if t == 16:
                    P.dma("sp", sv_d, l[:], reads=[lk])
                P.op("act", lambda e: e.copy(b[:], l[:]), reads=[lk], writes=[bk]); yield

            def g2(t):
                bi = t // 4
                tt = t % 4
                b = vb[t % 2]; bk = "vb%d" % (t % 2)
                sbank = 3 if t % 2 == 0 else 5
                sk = "pb%d" % sbank
                stm = stmp[t % 2]; stk = "stmp%d" % (t % 2)
                gb = gu_blk[bi % 3]; gk = "gu_blk%d" % (bi % 3)
                Wm = WmT if t < 16 else WmS
                for h in range(8):
                    P.op("pe", lambda e, h=h: e.matmul(pb[sbank][(h % 2) * 64:(h % 2 + 1) * 64, (h // 2) * 128:(h // 2 + 1) * 128],
                                                       b[:, h * 64:(h + 1) * 64], Wm[:, h, :], start=True, stop=True),
                         reads=[bk, "WmT", "WmS"], writes=[sk])
                yield
                if t < 16:
                    P.op("dve", lambda e: e.tensor_tensor(stm[:], pb[sbank][:].rearrange("p (a t) -> p a t", t=128), bsp[:], ALU.add),
                         reads=[sk, "bsp"], writes=[stk])
                else:
                    P.op("dve", lambda e: e.tensor_tensor(stm[:].rearrange("p a (n j) -> p a n j", j=8),
                                                           pb[sbank][:].rearrange("p (a n j) -> p a n j", n=16, j=8),
                                                           bsp[:, :, 0:8].unsqueeze(2).to_broadcast([128, 4, 16, 8]), ALU.add),
                         reads=[sk, "bsp"], writes=[stk])
                yield
                P.op("dve", lambda e: e.tensor_tensor(ygT[:, :, t * 128:(t + 1) * 128], stm[:], gb[:, :, tt * 128:(tt + 1) * 128], ALU.mult),
                     reads=[stk, gk], writes=["ygT"]); yield
            run_pipeline([g0, g0b, g1, g2], NT)
            P.barrier()
        dump("ygT", ygT[:], [128, 4, T], ["ygT"], BF16)
        if stop_after == "GM":
            P.finish(); P.replay(); mix.close(); return nc

        for k in range(3):
            P.dma("act", wc[:, :, k], wcv_d[k].rearrange("(c p) -> p c", p=128), writes=["wc"])
        P.dma("act", bcv[:], bcv_d.rearrange("(c p) -> p c", p=128), writes=["bcv"])
        mergedT = mk(mix, "mergedT", [128, 8, T], BF16)
        bgl = mk(mix, "bgl", [128, 16])
        P.dma("sp", bgl[:], bglu_d.rearrange("(b p) -> p b", p=128), writes=["bgl"], allow_slow_non_contiguous=True)
        with contextlib.ExitStack() as ph:
            NR = 2
            wga = [mk(ph, "wga%d" % i, [128, 8, 128], BF16) for i in range(NR)]
            wgb = [mk(ph, "wgb%d" % i, [128, 8, 128], BF16) for i in range(NR)]
            wz1 = [mk(ph, "wz1%d" % i, [128, 4, 128], BF16) for i in range(NR)]
            wz2 = [mk(ph, "wz2%d" % i, [128, 4, 128], BF16) for i in range(NR)]
            wyb = [mk(ph, "wyb%d" % i, [128, 4, 128], BF16) for i in range(NR)]
            tA = [mk(ph, "tA%d" % i, [128, 512]) for i in range(2)]
            tB = [mk(ph, "tB%d" % i, [128, 512]) for i in range(2)]
            tC = [mk(ph, "tC%d" % i, [128, 512]) for i in range(2)]

            def load_m(m):
                r = m % NR
                P.dma("pool", wga[r][:], cw(win_d[:, 1536 + m * 128:1536 + (m + 1) * 128]), writes=["wga%d" % r])
                P.dma("pool", wgb[r][:], cw(win_d[:, 2560 + m * 128:2560 + (m + 1) * 128]), writes=["wgb%d" % r])
                P.dma("pool", wz1[r][:], cw(wglu_d[:, m * 128:(m + 1) * 128]), writes=["wz1%d" % r])
                P.dma("pool", wz2[r][:], cw(wglu_d[:, 1024 + m * 128:1024 + (m + 1) * 128]), writes=["wz2%d" % r])
                P.dma("pool", wyb[r][:], cw(wgo_d[:, m * 128:(m + 1) * 128]), writes=["wyb%d" % r])
            load_m(0)
            it = 0
            for m in range(8):
                if m + 1 < 8:
                    load_m(m + 1)
                r = m % NR
                for (t0, tn) in TBS:
                    par = it % 2
                    it += 1
                    bga, bgb, bz1, bz2, byb = 0, 1, 2, 3, 4
                    for k in range(8):
                        P.op("pe", lambda e, k=k, r=r, t0=t0, tn=tn: e.matmul(pb[0][:, 0:tn], wga[r][:, k, :], hT[:, k, t0:t0 + tn], start=(k == 0), stop=(k == 7)),
                             reads=["wga%d" % r, "A_dstT"], writes=["pb0"])
                    for k in range(4):
                        P.op("pe", lambda e, k=k, r=r, t0=t0, tn=tn: e.matmul(pb[3][:, 0:tn], wz2[r][:, k, :], gy5Tf[:, k, t0:t0 + tn], start=(k == 0), stop=(k == 3)),
                             reads=["wz2%d" % r, "gy5T"], writes=["pb3"])
                    for k in range(4):
                        P.op("pe", lambda e, k=k, r=r, t0=t0, tn=tn: e.matmul(pb[2][:, 0:tn], wz1[r][:, k, :], gy5Tf[:, k, t0:t0 + tn], start=(k == 0), stop=(k == 3)),
                             reads=["wz1%d" % r, "gy5T"], writes=["pb2"])
                    for k in range(8):
                        P.op("pe", lambda e, k=k, r=r, t0=t0, tn=tn: e.matmul(pb[1][:, 0:tn], wgb[r][:, k, :], hT[:, k, t0:t0 + tn], start=(k == 0), stop=(k == 7)),
                             reads=["wgb%d" % r, "A_dstT"], writes=["pb1"])
                    for k in range(4):
                        P.op("pe", lambda e, k=k, r=r, t0=t0, tn=tn: e.matmul(pb[4][:, 0:tn], wyb[r][:, k, :], ygT[:, k, t0:t0 + tn], start=(k == 0), stop=(k == 3)),
                             reads=["wyb%d" % r, "ygT"], writes=["pb4"])
                    a, b, c3 = tA[par], tB[par], tC[par]
                    ak, bk, ck = "tA%d" % par, "tB%d" % par, "tC%d" % par
                    P.op("act", lambda e, a=a, tn=tn: e.activation(a[:, 0:tn], pb[0][:, 0:tn], AF.Sigmoid), reads=["pb0"], writes=[ak])
                    P.op("act", lambda e, b=b, tn=tn, m=m: e.activation(b[:, 0:tn], pb[3][:, 0:tn], AF.Sigmoid, bias=bgl[:, 8 + m:9 + m]), reads=["pb3", "bgl"], writes=[bk])
                    P.op("act", lambda e, c3=c3, tn=tn: e.activation(c3[:, 0:tn], pb[1][:, 0:tn], AF.Sigmoid), reads=["pb1"], writes=[ck])
                    P.op("dve", lambda e, b=b, tn=tn, m=m: e.scalar_tensor_tensor(b[:, 0:tn], pb[2][:, 0:tn], bgl[:, m:m + 1], b[:, 0:tn], ALU.add, ALU.mult),
                         reads=["pb2", "bgl", bk], writes=[bk])
                    P.op("dve", lambda e, a=a, b=b, tn=tn: e.tensor_tensor(a[:, 0:tn], a[:, 0:tn], b[:, 0:tn], ALU.mult), reads=[ak, bk], writes=[ak])
                    P.op("dve", lambda e, c3=c3, tn=tn: e.tensor_tensor(c3[:, 0:tn], pb[4][:, 0:tn], c3[:, 0:tn], ALU.mult), reads=["pb4", ck], writes=[ck])
                    P.op("dve", lambda e, a=a, c3=c3, m=m, t0=t0, tn=tn: e.tensor_tensor(mergedT[:, m, t0:t0 + tn], a[:, 0:tn], c3[:, 0:tn], ALU.add),
                         reads=[ak, ck], writes=["mergedT"])
            P.barrier()
        dump("mergedT", mergedT[:], [128, 8, T], ["mergedT"], BF16)
        if stop_after == "MERGE":
            P.finish(); P.replay(); mix.close(); return nc

        h2T = hT
        with contextlib.ExitStack() as ph:
            wo = mk(ph, "wo", [128, 8, D], BF16)
            P.dma("pool", wo[:, :, 0:512], cw(wout_d[:, 0:512]), writes=["wo0"])
            P.dma("pool", wo[:, :, 512:1024], cw(wout_d[:, 512:1024]), writes=["wo1"])
            gm_p = mk(ph, "gm_p", [128, D])
            gm_s = mk(ph, "gm_s", [128, D])
            tm_gate(gm_p, gm_s, 16, "gm")
            x1r = [mk(ph, "x1r%d" % i, [128, D]) for i in range(3)]
            rings["xn"] = [mk(ph, "xnB%d" % i, [128, D]) for i in range(2)]
            xt_ring = [mk(ph, "xtB%d" % i, [128, D]) for i in range(3)]

            def p0(t):
                xt = xt_ring[t % 3]
                xk = "xt%d" % (t % 3)
                P.dma("sp", xt[:], x_d[t * 128:(t + 1) * 128, :], writes=[xk]); yield
                banks = (0, 1) if t % 2 == 0 else (2, 3)
                for h in range(2):
                    for k in range(8):
                        P.op("pe", lambda e, k=k, h=h: e.matmul(pb[banks[h]][:], mergedT[:, k, t * 128:(t + 1) * 128], wo[:, k, h * 512:(h + 1) * 512], start=(k == 0), stop=(k == 7)),
                             reads=["mergedT", "wo%d" % h], writes=["pb%d" % banks[h]])
                    yield

            def p0b(t):
                xt = xt_ring[t % 3]
                xk = "xt%d" % (t % 3)
                banks = (0, 1) if t % 2 == 0 else (2, 3)
                x1 = x1r[t % 3]
                x1k = "x1r%d" % (t % 3)
                g = gm_p if t < 16 else gm_s
                for h in range(2):
                    P.op("dve", lambda e, h=h: e.tensor_tensor(x1[:, h * 512:(h + 1) * 512], pb[banks[h]][:], g[:, h * 512:(h + 1) * 512], ALU.mult),
                         reads=["pb%d" % banks[h], "gm_p", "gm_s"], writes=[x1k]); yield
                P.op("pool", lambda e: e.tensor_tensor(x1[:], x1[:], xt[:], ALU.add), reads=[x1k, xk], writes=[x1k]); yield
                P.dma("sp", x1_d[t * 128:(t + 1) * 128, :], x1[:], reads=[x1k], writes=["x1_d%d" % t]); yield
            run_pipeline([p0, p0b] + norm_stages(lambda t: (x1r[t % 3][:], "x1r%d" % (t % 3)), h2T, gs2, 24, 1, "B_"), NT)
            P.barrier()
        dump("h2T", h2T[:], [128, 8, T], ["B_dstT"], BF16)
        if "h2T" in dumps or "mergedT" in dumps:
            P.barrier()
        if stop_after == "P3":
            P.finish(); P.replay(); mix.close(); return nc

        mix.close()
        ffn_phase(nc, P, mk, sb, pb, ident, h2T, modT, tm_gate, fg_bc, (gf_p, gf_s, pastT, wc, bcv), rs_all, junk, dict(
            wup=wup_d, wcv=wcv_d, bcv=bcv_d, wdn=wdn_d, scv=scv_d, x1=x1_d, y=y_d, pcv=pcv_d, scvo=scvo_d), cw, dump, stop_after)

        P.finish()
        P.replay()
    return nc


def s5_prep(nc, P, mk, ph, pb, ident, dr, scr, TH8):
    TWO_PI = 2.0 * math.pi
    if True:
        are = mk(ph, "are", [128, 16]); aim = mk(ph, "aim", [128, 16]); ldt = mk(ph, "ldt", [128, 16])
        Bre = mk(ph, "Bre", [128, 16, 16]); Bim = mk(ph, "Bim", [128, 16, 16])
        CTr = mk(ph, "CTr", [128, 16, 16]); CTi = mk(ph, "CTi", [128, 16, 16])
        dcol = mk(ph, "dcol", [128, 4])
        ph0 = ph
        Cn_re = mk(ph0, "Cn_re", [16, 32, 64]); Cn_im = mk(ph0, "Cn_im", [16, 32, 64])
        for gi in range(2):
            sl = slice(gi * 64, (gi + 1) * 64)
            P.dma("sp", are[sl, :], dr["are"].rearrange("(pr gi) p -> gi p pr", gi=2)[gi], writes=["are"], allow_slow_non_contiguous=True)
            P.dma("sp", aim[sl, :], dr["aim"].rearrange("(pr gi) p -> gi p pr", gi=2)[gi], writes=["aim"], allow_slow_non_contiguous=True)
            P.dma("sp", ldt[sl, :], dr["ldt"].rearrange("(pr gi) -> gi pr", gi=2)[gi:gi + 1, :].to_broadcast([64, 16]), writes=["ldt"])
            P.dma("sp", Bre[sl], dr["bre"].rearrange("(pr gi) p h -> gi p pr h", gi=2)[gi], writes=["Bre"])
            P.dma("sp", Bim[sl], dr["bim"].rearrange("(pr gi) p h -> gi p pr h", gi=2)[gi], writes=["Bim"])
        P.dma("sp", Cn_re[:], dr["cre"].rearrange("g h p -> h g p"), writes=["Cn_re"])
        P.dma("sp", Cn_im[:], dr["cim"].rearrange("g h p -> h g p"), writes=["Cn_im"])
        P.dma("sp", dcol[:], dr["sd"].rearrange("(q g) h -> (g h) q", q=4), writes=["dcol"], allow_slow_non_contiguous=True)
        for ri, (Cn, CT, nm) in enumerate(((Cn_re, CTr, "CTr"), (Cn_im, CTi, "CTi"))):
            for pr in range(16):
                P.op("pe", lambda e, pr=pr, Cn=Cn, ri=ri: e.transpose(pb[4 + ri][:, pr * 16:(pr + 1) * 16],
                                                                      Cn[:, 2 * pr:2 * pr + 2, :].rearrange("h g p -> h (g p)"), ident[0:16, 0:16]),
                     reads=["Cn_re", "Cn_im", "ident"], writes=["pb%d" % (4 + ri)])
            P.op("act", lambda e, CT=CT, ri=ri: e.copy(CT[:], pb[4 + ri][:, 0:256].rearrange("p (a b) -> p a b", b=16)), reads=["pb%d" % (4 + ri)], writes=[nm])

        tA = mk(ph, "s5tA", [128, 256]); tB = mk(ph, "s5tB", [128, 256]); tI = mk(ph, "s5tI", [128, 256], I32)

        def sincos(ang, n, sn_out, cs_out, rk, wk):
            a = tA[:, 0:n]; b = tB[:, 0:n]; ii = tI[:, 0:n]
            for off, out in ((0.0, sn_out), (0.25, cs_out)):
                P.op("dve", lambda e, off=off: e.tensor_scalar(a, ang, 1.0 / TWO_PI, off, ALU.mult, ALU.add), reads=rk, writes=["s5tA"])
                P.op("dve", lambda e: e.tensor_copy(ii, a), reads=["s5tA"], writes=["s5tI"])
                P.op("dve", lambda e: e.tensor_copy(b, ii), reads=["s5tI"], writes=["s5tB"])
                P.op("dve", lambda e: e.tensor_tensor(a, a, b, ALU.subtract), reads=["s5tA", "s5tB"], writes=["s5tA"])
                P.op("dve", lambda e: e.tensor_scalar(b, a, 0.5, -1.0, ALU.is_gt, ALU.mult), reads=["s5tA"], writes=["s5tB"])
                P.op("dve", lambda e: e.tensor_tensor(a, a, b, ALU.add), reads=["s5tA", "s5tB"], writes=["s5tA"])
                P.op("dve", lambda e: e.tensor_scalar(b, a, -0.5, 1.0, ALU.is_lt, ALU.mult), reads=["s5tA"], writes=["s5tB"])
                P.op("dve", lambda e: e.tensor_tensor(a, a, b, ALU.add), reads=["s5tA", "s5tB"], writes=["s5tA"])
                P.op("act", lambda e, out=out: e.activation(out, a, AF.Sin, scale=6.28318), reads=["s5tA"], writes=wk)

        def tt(out, a, b, op, reads, writes, eng="dve"):
            P.op(eng, lambda e: e.tensor_tensor(out, a, b, op), reads=reads, writes=writes)

        c1 = mk(ph, "s5c1", [128, 256]); c2 = mk(ph, "s5c2", [128, 256])

        def cmul(o_re, o_im, a_re, a_im, b_re, b_im, shape, reads, writes):
            n = int(np.prod(shape))
            v1 = c1[:, 0:n]; v2 = c2[:, 0:n]
            if len(shape) == 2:
                v1 = v1.rearrange("p (a b) -> p a b", b=shape[1]); v2 = v2.rearrange("p (a b) -> p a b", b=shape[1])
            tt(v1, a_re, b_re, ALU.mult, reads, ["s5c1"])
            tt(v2, a_im, b_im, ALU.mult, reads, ["s5c2"])
            tt(o_re, v1, v2, ALU.subtract, ["s5c1", "s5c2"], writes + ["s5c1", "s5c2"])
            tt(v1, a_re, b_im, ALU.mult, reads, ["s5c1"])
            tt(v2, a_im, b_re, ALU.mult, reads, ["s5c2"])
            tt(o_im, v1, v2, ALU.add, ["s5c1", "s5c2"], writes + ["s5c1", "s5c2"])

        dt = mk(ph, "s5dt", [128, 16]); mag = mk(ph, "s5mag", [128, 16]); th = mk(ph, "s5th", [128, 16])
        sn = mk(ph, "s5sn", [128, 16]); cs = mk(ph, "s5cs", [128, 16])
        APr = mk(ph, "APr", [128, 9, 16]); APi = mk(ph, "APi", [128, 9, 16])
        P.op("act", lambda e: e.activation(dt[:], ldt[:], AF.Exp), reads=["ldt"], writes=["dt"])
        tt(mag[:], are[:], dt[:], ALU.mult, ["are", "dt"], ["mag"])
        P.op("act", lambda e: e.activation(mag[:], mag[:], AF.Exp), reads=["mag"], writes=["mag"])
        tt(th[:], aim[:], dt[:], ALU.mult, ["aim", "dt"], ["th"])
        sincos(th[:], 16, sn[:], cs[:], ["th"], ["sncs"])
        P.op("dve", lambda e: e.memset(APr[:, 0, :], 1.0), writes=["AP"])
        P.op("dve", lambda e: e.memset(APi[:, 0, :], 0.0), writes=["AP"])
        tt(APr[:, 1, :], mag[:], cs[:], ALU.mult, ["mag", "sncs"], ["AP"])
        tt(APi[:, 1, :], mag[:], sn[:], ALU.mult, ["mag", "sncs"], ["AP"])
        def bck(ap16, n):
            return ap16.unsqueeze(1).to_broadcast([128, n, 16])
        cmul(APr[:, 2, :], APi[:, 2, :], APr[:, 1, :], APi[:, 1, :], APr[:, 1, :], APi[:, 1, :], [16], ["AP"], ["AP"])
        cmul(APr[:, 3:5, :], APi[:, 3:5, :], APr[:, 1:3, :], APi[:, 1:3, :], bck(APr[:, 2, :], 2), bck(APi[:, 2, :], 2), [2, 16], ["AP"], ["AP"])
        cmul(APr[:, 5:9, :], APi[:, 5:9, :], APr[:, 1:5, :], APi[:, 1:5, :], bck(APr[:, 4, :], 4), bck(APi[:, 4, :], 4), [4, 16], ["AP"], ["AP"])
        R8 = mk(ph, "R8", [128, 16]); nA8i = mk(ph, "nA8i", [128, 16])
        tt(R8[:], are[:], dt[:], ALU.mult, ["are", "dt"], ["R8"])
        P.op("act", lambda e: e.activation(R8[:], R8[:], AF.Exp, scale=8.0), reads=["R8"], writes=["R8"])
        P.op("dve", lambda e: e.tensor_scalar(TH8[:], th[:], 8.0, None, ALU.mult), reads=["th"], writes=["TH8"])
        P.op("dve", lambda e: e.tensor_scalar(nA8i[:], APi[:, 8, :], -1.0, None, ALU.mult), reads=["AP"], writes=["nA8i"])

        nr = mk(ph, "s5nr", [128, 16]); den = mk(ph, "s5den", [128, 16]); t16 = mk(ph, "s5t16", [128, 16])
        cr = mk(ph, "s5cr", [128, 16]); ci = mk(ph, "s5ci", [128, 16])
        P.op("dve", lambda e: e.tensor_scalar(nr[:], APr[:, 1, :], -1.0, None, ALU.add), reads=["AP"], writes=["nr"])
        tt(den[:], are[:], are[:], ALU.mult, ["are"], ["den"])
        tt(t16[:], aim[:], aim[:], ALU.mult, ["aim"], ["t16"])
        tt(den[:], den[:], t16[:], ALU.add, ["den", "t16"], ["den"])
        P.op("dve", lambda e: e.reciprocal(den[:], den[:]), reads=["den"], writes=["den"])
        tt(cr[:], nr[:], are[:], ALU.mult, ["nr", "are"], ["cr"])
        tt(t16[:], APi[:, 1, :], aim[:], ALU.mult, ["AP", "aim"], ["t16"])
        tt(cr[:], cr[:], t16[:], ALU.add, ["cr", "t16"], ["cr"])
        tt(cr[:], cr[:], den[:], ALU.mult, ["cr", "den"], ["cr"])
        tt(ci[:], APi[:, 1, :], are[:], ALU.mult, ["AP", "are"], ["ci"])
        tt(t16[:], nr[:], aim[:], ALU.mult, ["nr", "aim"], ["t16"])
        tt(ci[:], ci[:], t16[:], ALU.subtract, ["ci", "t16"], ["ci"])
        tt(ci[:], ci[:], den[:], ALU.mult, ["ci", "den"], ["ci"])
        bbr = mk(ph, "bbr", [128, 16, 16]); bbi = mk(ph, "bbi", [128, 16, 16])

        def bc16(ap16):
            return ap16.unsqueeze(2).to_broadcast([128, 16, 16])
        cmul(bbr[:], bbi[:], bc16(cr[:]), bc16(ci[:]), Bre[:], Bim[:], [16, 16], ["cr", "ci", "Bre", "Bim"], ["bb"])

        LA = mk(ph, "LA", [128, 4, 8, 2, 128], BF16)
        LC = mk(ph, "LC", [128, 8, 2, 16, 32], BF16)
        KC = mk(ph, "KC", [128, 4, 8, 128], BF16)
        P.op("pool", lambda e: e.memset(LC[:], 0.0), writes=["LC"])
        if True:
            ph2 = ph
            WP = mk(ph2, "WP", [128, 8, 2, 16, 2, 16])
            CP = mk(ph2, "CP", [128, 2, 16, 2, 16])
            Wr = mk(ph2, "s5Wr", [128, 16, 16]); Wi = mk(ph2, "s5Wi", [128, 16, 16])
            bm16 = mk(ph2, "bm16", [128, 128]); ti1 = mk(ph2, "s5ti1", [128, 128], I32); ti2 = mk(ph2, "s5ti2", [128, 128], I32)
            ktmp = mk(ph2, "ktmp", [128, 128])
            P.op("pool", lambda e: e.memset(WP[:], 0.0), writes=["WP"])
            P.op("pool", lambda e: e.memset(CP[:], 0.0), writes=["CP"])
            P.op("pool", lambda e: e.iota(ti1[:], [[1, 8], [0, 16]], base=0, channel_multiplier=0), writes=["s5ti1"])
            P.op("pool", lambda e: e.iota(ti2[:], [[0, 128]], base=0, channel_multiplier=1), writes=["s5ti2"])
            P.op("dve", lambda e: e.tensor_single_scalar(ti2[:], ti2[:], 4, ALU.arith_shift_right), reads=["s5ti2"], writes=["s5ti2"])
            P.op("dve", lambda e: e.tensor_tensor(bm16[:], ti1[:], ti2[:], ALU.is_equal), reads=["s5ti1", "s5ti2"], writes=["bm16"])
            for gi in range(2):
                sl = slice(gi * 64, (gi + 1) * 64)
                P.op("dve", lambda e, sl=sl, gi=gi: e.tensor_copy(CP[sl, 0, :, gi, :], CTr[sl]), reads=["CTr", "CP"], writes=["CP"])
                P.op("dve", lambda e, sl=sl, gi=gi: e.tensor_scalar(CP[sl, 1, :, gi, :], CTi[sl], -1.0, None, ALU.mult), reads=["CTi", "CP"], writes=["CP"])
            Wt = [[mk(ph2, "s5Wt%d_%d" % (i, j), [128, 16, 16]) for j in range(4)] for i in range(8)]

            def cmul_multi(insts, eng="dve"):
                for step in range(6):
                    for (o_re, o_im, a_re, a_im, b_re, b_im, v1, v2, rk, okey, vkey) in insts:
                        if step == 0:
                            tt(v1, a_re, b_re, ALU.mult, rk, [vkey + "a"], eng)
                        elif step == 1:
                            tt(v2, a_im, b_im, ALU.mult, rk, [vkey + "b"], eng)
                        elif step == 2:
                            tt(o_re, v1, v2, ALU.subtract, [vkey + "a", vkey + "b"], [okey + "r"], eng)
                        elif step == 3:
                            tt(v1, a_re, b_im, ALU.mult, rk, [vkey + "a"], eng)
                        elif step == 4:
                            tt(v2, a_im, b_re, ALU.mult, rk, [vkey + "b"], eng)
                        else:
                            tt(o_im, v1, v2, ALU.add, [vkey + "a", vkey + "b"], [okey + "i"], eng)
            insts = []
            for s in range(8):
                k = 7 - s
                w = Wt[s]
                insts.append((w[0][:], w[1][:], bc16(APr[:, k, :]), bc16(APi[:, k, :]), bbr[:], bbi[:], w[2][:], w[3][:], ["AP", "bb"], "Wt%d" % s, "Wv%d" % s))
            cmul_multi(insts)
            for s in range(8):
                w = Wt[s]
                for gi in range(2):
                    sl = slice(gi * 64, (gi + 1) * 64)
                    P.op("dve", lambda e, sl=sl, gi=gi, s=s, w=w: e.tensor_copy(WP[sl, s, 0, :, gi, :], w[0][sl]), reads=["Wt%dr" % s, "WP"], writes=["WP"])
                    P.op("dve", lambda e, sl=sl, gi=gi, s=s, w=w: e.tensor_copy(WP[sl, s, 1, :, gi, :], w[1][sl]), reads=["Wt%di" % s, "WP"], writes=["WP"])
            insts = []
            for j in range(8):
                w = Wt[j]
                insts.append((w[0][:], w[1][:], bc16(APr[:, j + 1, :]), bc16(APi[:, j + 1, :]), CTr[:], CTi[:], w[2][:], w[3][:], ["AP", "CTr", "CTi"], "Wt%d" % j, "Wv%d" % j))
            cmul_multi(insts, "pool")
            for j in range(8):
                w = Wt[j]
                for gi in range(2):
                    sl = slice(gi * 64, (gi + 1) * 64)
                    P.op("dve", lambda e, sl=sl, gi=gi, j=j, w=w: e.tensor_copy(LC[sl, j, 0, :, gi * 16:(gi + 1) * 16], w[0][sl]), reads=["Wt%dr" % j, "LC"], writes=["LC"])
                    P.op("dve", lambda e, sl=sl, gi=gi, j=j, w=w: e.tensor_scalar(LC[sl, j, 1, :, gi * 16:(gi + 1) * 16], w[1][sl], -1.0, None, ALU.mult), reads=["Wt%di" % j, "LC"], writes=["LC"])
            WPf = WP[:].rearrange("p s r a b c -> p s r (a b c)")
            CPf = CP[:].rearrange("p r a b c -> p r (a b c)")
            n = 0
            for q in range(4):
                qs = slice(q * 128, (q + 1) * 128)
                for s0 in range(0, 8, 2):
                    bank = 6 + n % 2
                    n += 1
                    for ds in range(2):
                        for ri in range(2):
                            col = (ds * 2 + ri) * 128
                            P.op("pe", lambda e, s_=s0 + ds, ri=ri, qs=qs, bank=bank, col=col: e.transpose(pb[bank][:, col:col + 128], WPf[:, s_, ri, qs], ident[:]),
                                 reads=["WP", "ident"], writes=["pb%d" % bank])
                    P.op("dve", lambda e, q=q, s0=s0, bank=bank: e.tensor_copy(LA[:, q, s0:s0 + 2, :, :].rearrange("p a b c -> p (a b c)"), pb[bank][:, 0:512]),
                         reads=["pb%d" % bank], writes=["LA"])
                for (t0, nt) in ((0, 1), (1, 4), (5, 3)):
                    bank = 4 + n % 2
                    n += 1
                    for dt_ in range(nt):
                        s_ = 7 - (t0 + dt_)
                        col = dt_ * 128
                        P.op("pe", lambda e, s_=s_, qs=qs, bank=bank, col=col: e.matmul(pb[bank][:, col:col + 128], WPf[:, s_, 0, qs], CPf[:, 0, qs], start=True, stop=False),
                             reads=["WP", "CP"], writes=["pb%d" % bank])
                        P.op("pe", lambda e, s_=s_, qs=qs, bank=bank, col=col: e.matmul(pb[bank][:, col:col + 128], WPf[:, s_, 1, qs], CPf[:, 1, qs], start=False, stop=True),
                             reads=["WP", "CP"], writes=["pb%d" % bank])
                    if t0 == 0:
                        tt(ktmp[:], pb[bank][:, 0:128], bm16[:], ALU.mult, ["pb%d" % bank, "bm16"], ["ktmp"])
                        P.op("dve", lambda e, q=q: e.scalar_tensor_tensor(KC[:, q, 0, :], ident[:], dcol[:, q:q + 1], ktmp[:], ALU.mult, ALU.add),
                             reads=["ident", "dcol", "ktmp"], writes=["KC"])
                    else:
                        P.op("dve", lambda e, q=q, t0=t0, nt=nt, bank=bank: e.tensor_tensor(KC[:, q, t0:t0 + nt, :], pb[bank][:, 0:nt * 128].rearrange("p (a b) -> p a b", b=128),
                                                                                             bm16[:].unsqueeze(1).to_broadcast([128, nt, 128]), ALU.mult),
                             reads=["pb%d" % bank, "bm16"], writes=["KC"])

        small = mk(ph, "s5small", [128, 4, 16])
        P.op("dve", lambda e: e.tensor_copy(small[:, 0, :], R8[:]), reads=["R8"], writes=["s5small"])
        P.op("dve", lambda e: e.tensor_copy(small[:, 1, :], nA8i[:]), reads=["nA8i"], writes=["s5small"])
        P.op("dve", lambda e: e.tensor_copy(small[:, 2, :], APr[:, 8, :]), reads=["AP"], writes=["s5small"])
        P.op("dve", lambda e: e.tensor_copy(small[:, 3, :], APi[:, 8, :]), reads=["AP"], writes=["s5small"])
        P.dma("sp", scr["small"], small[:], reads=["s5small"], writes=["scr_small"])
        P.dma("sp", scr["LA"], LA[:].rearrange("p a b c d -> p (a b c d)"), reads=["LA"], writes=["scr_LA"])
        P.dma("sp", scr["LC"], LC[:].rearrange("p a b c d -> p (a b c d)"), reads=["LC"], writes=["scr_LC"])
        P.dma("sp", scr["KC"], KC[:].rearrange("p a b c -> p (a b c)"), reads=["KC"], writes=["scr_KC"])


def s5_tables(nc, P, mk, ph, TH8, scr):
    TWO_PI = 2.0 * math.pi
    cio_i = mk(ph, "cio_i", [128, 256], I32); cio = mk(ph, "cio", [128, 256])
    nhalf = mk(ph, "nhalf", [128, 1])
    P.op("dve", lambda e: e.memset(nhalf[:], -3.14159), writes=["nhalf"])
    P.op("pool", lambda e: e.iota(cio_i[:], [[1, 256]], base=0, channel_multiplier=0), writes=["cio_i"])
    P.op("dve", lambda e: e.tensor_copy(cio[:], cio_i[:]), reads=["cio_i"], writes=["cio"])
    NQ = 4
    angq = mk(ph, "angq", [128, NQ * 256])
    tA = [mk(ph, "tAq%d" % i, [128, NQ * 256]) for i in range(2)]
    tB = [mk(ph, "tBq%d" % i, [128, NQ * 256]) for i in range(2)]
    tI = [mk(ph, "tIq%d" % i, [128, NQ * 256], I32) for i in range(2)]
    Eq = [[mk(ph, "Eq%d_%d" % (i, j), [128, NQ * 256]) for j in range(2)] for i in range(2)]
    for qq in range(16 // NQ):
        par = qq % 2
        P.op("dve", lambda e, qq=qq: e.tensor_tensor(angq[:].rearrange("p (a c) -> p a c", c=256),
                                                      cio[:].unsqueeze(1).to_broadcast([128, NQ, 256]),
                                                      TH8[:, qq * NQ:(qq + 1) * NQ].unsqueeze(2).to_broadcast([128, NQ, 256]), ALU.mult),
             reads=["cio", "TH8"], writes=["angq"])
        steps = []
        for ti, off in ((0, 0.0), (1, 0.25)):
            out = Eq[par][ti]; ok = "Eq%d_%d" % (par, ti)
            a = tA[ti][:]; b = tB[ti][:]; ii = tI[ti][:]
            ka, kb, ki = "tAq%d" % ti, "tBq%d" % ti, "tIq%d" % ti
            steps.append([
                ("dve", lambda e, a=a, off=off: e.tensor_scalar(a, angq[:], 1.0 / TWO_PI, off + 0.5, ALU.mult, ALU.add), ["angq"], [ka]),
                ("dve", lambda e, a=a, ii=ii: e.tensor_copy(ii, a), [ka], [ki]),
                ("dve", lambda e, b=b, ii=ii: e.tensor_copy(b, ii), [ki], [kb]),
                ("dve", lambda e, a=a, b=b: e.tensor_tensor(a, a, b, ALU.subtract), [ka, kb], [ka]),
                ("dve", lambda e, a=a, b=b: e.tensor_scalar(b, a, 0.0, 1.0, ALU.is_lt, ALU.mult), [ka], [kb]),
                ("dve", lambda e, a=a, b=b: e.tensor_tensor(a, a, b, ALU.add), [ka, kb], [ka]),
                ("act", lambda e, a=a, out=out: e.activation(out[:], a, AF.Sin, scale=6.28318, bias=nhalf[:, 0:1]), [ka, "nhalf"], [ok]),
            ])
        for k in range(7):
            for ti in range(2):
                eng, fn, rk, wk = steps[ti][k]
                P.op(eng, fn, reads=rk, writes=wk)
        for ti in range(2):
            out = Eq[par][ti]; ok = "Eq%d_%d" % (par, ti)
            P.dma("sp", scr["tab"][:, ti, qq * NQ:(qq + 1) * NQ, :], out[:].rearrange("p (a c) -> p a c", c=256), reads=[ok], writes=["scr_tab"])


def s5_phase(nc, P, mk, sb, pb, ident, uT, gy5T, fin_p, dr, scr, dump, dumps):
    with contextlib.ExitStack() as ph:
        LA = mk(ph, "LA_m", [128, 4, 8, 2, 128], BF16)
        LC = mk(ph, "LC_m", [128, 8, 2, 16, 32], BF16)
        KC = mk(ph, "KC_m", [128, 4, 8, 128], BF16)
        small = mk(ph, "s5small_m", [128, 4, 16])
        P.dma("sp", LA[:].rearrange("p a b c d -> p (a b c d)"), scr["LA"], reads=["scr_LA"], writes=["LA"])
        P.dma("sp", LC[:].rearrange("p a b c d -> p (a b c d)"), scr["LC"], reads=["scr_LC"], writes=["LC"])
        P.dma("sp", KC[:].rearrange("p a b c -> p (a b c)"), scr["KC"], reads=["scr_KC"], writes=["KC"])
        P.dma("sp", small[:], scr["small"], reads=["scr_small"], writes=["s5small"])
        sT = mk(ph, "sT", [128, 2, 16, 16])
        HS = mk(ph, "HS", [128, 16, 2, NCH], BF16)
        fin_s = mk(ph, "fin_s", [128, 2, 16, 16])
        st_one = mk(ph, "st_in0", [16, 2048])
        st_in = [st_one, st_one]
        for ri in range(2):
            P.dma("sp", st_in[ri][:], dr["sre" if ri == 0 else "sim"], writes=["st_in0"])
            for pr in range(16):
                P.op("pe", lambda e, ri=ri, pr=pr: e.transpose(pb[2 + ri][:, pr * 16:(pr + 1) * 16], st_in[ri][:, pr * 128:(pr + 1) * 128], ident[0:16, 0:16]),
                     reads=["st_in0", "ident"], writes=["pb%d" % (2 + ri)])
            P.op("act", lambda e, ri=ri: e.copy(sT[:, ri, :, :], pb[2 + ri][:, 0:256].rearrange("p (a b) -> p a b", b=16)), reads=["pb%d" % (2 + ri)], writes=["sT"])
            P.op("dve", lambda e, ri=ri: e.tensor_copy(HS[:, :, ri, 256:272], sT[:, ri, :, :]), reads=["sT"], writes=["HSs"])
            P.op("dve", lambda e, ri=ri: e.memset(HS[:, :, ri, 0:1], 0.0), writes=["HS0"])

        def rr(gens):
            gens = list(gens)
            while gens:
                for g_ in list(gens):
                    try:
                        next(g_)
                    except StopIteration:
                        gens.remove(g_)

        tmps = []
        for par in range(3):
            d = {}
            for nm in ("Ec", "Es", "Mr", "Mi", "m1", "m2", "Gr", "Gi"):
                d[nm] = mk(ph, "%s_%d" % (nm, par), [128, 256])
            d["Sr"] = mk(ph, "Sr_%d" % par, [128, NCH]); d["Si"] = mk(ph, "Si_%d" % par, [128, NCH])
            d["He"] = mk(ph, "He_%d" % par, [128, 2, 256])
            tmps.append(d)

        def pair_gen(pr, par):
            q, i = divmod(pr, 4)
            rows = slice(32 * i, 32 * i + 32)
            d = tmps[par]
            K = lambda nm: "%s_%d" % (nm, par)
            bnk = ((0, 1), (4, 5), (6, 7))[par]
            Ec, Es, Mr, Mi, m1, m2, Gr, Gi, Sr, Si, He = (d[x] for x in ("Ec", "Es", "Mr", "Mi", "m1", "m2", "Gr", "Gi", "Sr", "Si", "He"))

            def t2(out, a, b, op, reads, writes):
                P.op("dve", lambda e: e.tensor_tensor(out, a, b, op), reads=reads, writes=writes)
            for ri in range(2):
                bank = bnk[ri]
                for s_ in range(8):
                    P.op("pe", lambda e, s_=s_, ri=ri, bank=bank: e.matmul(pb[bank][:, 0:NCH], LA[rows, q, s_, ri, :], uT[rows, q, s_, :],
                                                                        start=(s_ == 0), stop=(s_ == 7), tile_position=(32 * i, 0)),
                         reads=["LA", "uT"], writes=["pb%d" % bank])
                yield
            P.op("act", lambda e: e.copy(Sr[:], pb[bnk[0]][:, 0:NCH]), reads=["pb%d" % bnk[0]], writes=[K("Sr")]); yield
            P.op("act", lambda e: e.copy(Si[:], pb[bnk[1]][:, 0:NCH]), reads=["pb%d" % bnk[1]], writes=[K("Si")]); yield
            P.dma("sp", Es[:], scr["tab"][:, 0, pr, :], reads=["scr_tab"], writes=[K("EcEs")]); yield
            P.dma("sp", Ec[:], scr["tab"][:, 1, pr, :], reads=["scr_tab"], writes=[K("EcEs")]); yield
            t2(m1[:], Sr[:, 0:256], Ec[:], ALU.mult, [K("Sr"), K("EcEs")], [K("m1")]); yield
            t2(m2[:], Si[:, 0:256], Es[:], ALU.mult, [K("Si"), K("EcEs")], [K("m2")]); yield
            t2(Mr[:], m1[:], m2[:], ALU.add, [K("m1"), K("m2")], [K("Mr"), K("m1"), K("m2")]); yield
            t2(m1[:], Si[:, 0:256], Ec[:], ALU.mult, [K("Si"), K("EcEs")], [K("m1")]); yield
            t2(m2[:], Sr[:, 0:256], Es[:], ALU.mult, [K("Sr"), K("EcEs")], [K("m2")]); yield
            t2(Mi[:], m1[:], m2[:], ALU.subtract, [K("m1"), K("m2")], [K("Mi"), K("m1"), K("m2")]); yield
            P.op("dve", lambda e: e.tensor_tensor_scan(Gr[:], small[:, 0, pr:pr + 1].to_broadcast([128, 256]), Mr[:], 0.0, ALU.mult, ALU.add),
                 reads=["s5small", K("Mr")], writes=[K("Gr")]); yield
            P.op("dve", lambda e: e.tensor_tensor_scan(Gi[:], small[:, 0, pr:pr + 1].to_broadcast([128, 256]), Mi[:], 0.0, ALU.mult, ALU.add),
                 reads=["s5small", K("Mi")], writes=[K("Gi")]); yield
            t2(m1[:], Gr[:], Ec[:], ALU.mult, [K("Gr"), K("EcEs")], [K("m1")]); yield
            t2(m2[:], Gi[:], Es[:], ALU.mult, [K("Gi"), K("EcEs")], [K("m2")]); yield
            t2(He[:, 0, :], m1[:], m2[:], ALU.subtract, [K("m1"), K("m2")], [K("He"), K("m1"), K("m2")]); yield
            t2(m1[:], Gi[:], Ec[:], ALU.mult, [K("Gi"), K("EcEs")], [K("m1")]); yield
            t2(m2[:], Gr[:], Es[:], ALU.mult, [K("Gr"), K("EcEs")], [K("m2")]); yield
            t2(He[:, 1, :], m1[:], m2[:], ALU.add, [K("m1"), K("m2")], [K("He"), K("m1"), K("m2")]); yield
            P.op("act", lambda e: e.copy(HS[:, pr, :, 1:256], He[:, :, 0:255]), reads=[K("He")], writes=["HSp%d" % q]); yield
            P.op("act", lambda e: e.copy(fin_p[:, :, pr:pr + 1], He[:, :, 255:256]), reads=[K("He")], writes=["fin_p"]); yield
            a8r = small[:, 2, pr:pr + 1]; a8i = small[:, 3, pr:pr + 1]; na8i = small[:, 1, pr:pr + 1]
            P.op("dve", lambda e: e.scalar_tensor_tensor(m1[:, 0:16], sT[:, 0, pr, :], a8r, Sr[:, 256:272], ALU.mult, ALU.add),
                 reads=["sT", "s5small", K("Sr")], writes=[K("m1")]); yield
            P.op("dve", lambda e: e.scalar_tensor_tensor(fin_s[:, 0, pr, :], sT[:, 1, pr, :], na8i, m1[:, 0:16], ALU.mult, ALU.add),
                 reads=["sT", "s5small", K("m1")], writes=["fin_s", K("m1")]); yield
            P.op("dve", lambda e: e.scalar_tensor_tensor(m2[:, 0:16], sT[:, 1, pr, :], a8r, Si[:, 256:272], ALU.mult, ALU.add),
                 reads=["sT", "s5small", K("Si")], writes=[K("m2")]); yield
            P.op("dve", lambda e: e.scalar_tensor_tensor(fin_s[:, 1, pr, :], sT[:, 0, pr, :], a8i, m2[:, 0:16], ALU.mult, ALU.add),
                 reads=["sT", "s5small", K("m2")], writes=["fin_s", K("m2")]); yield

        def c_gen(q):
            for j in range(8):
                bank = 2 + j % 2
                first = True
                for tau in range(j + 1):
                    P.op("pe", lambda e, q=q, j=j, tau=tau, bank=bank, first=first: e.matmul(pb[bank][:, 0:NCH], KC[:, q, tau, :], uT[:, q, j - tau, :], start=first, stop=False),
                         reads=["KC", "uT"], writes=["pb%d" % bank])
                    first = False
                yield
                for i in range(4):
                    pr = 4 * q + i
                    for ri in range(2):
                        last = (ri == 1)
                        P.op("pe", lambda e, j=j, ri=ri, pr=pr, i=i, bank=bank, last=last: e.matmul(pb[bank][32 * i:32 * i + 32, 0:NCH], LC[:, j, ri, pr, :], HS[:, pr, ri, :],
                                                                                                    start=False, stop=last, tile_position=(0, 32 * i)),
                             reads=["LC", "HSs", "HSp%d" % q, "HS0"], writes=["pb%d" % bank])
                P.op("act", lambda e, q=q, j=j, bank=bank: e.activation(gy5T[:, q, :, j], pb[bank][:, 0:NCH], AF.Gelu_apprx_tanh), reads=["pb%d" % bank], writes=["gy5T"]); yield


        done_q = 0
        pend_c = []
        for g_ in ([0, 1, 2], [3, 4, 5], [6, 7, 8], [9, 10, 11], [12, 13, 14], [15]):
            gens_ = [pair_gen(pr_, k_) for k_, pr_ in enumerate(g_)] + [c_gen(q_) for q_ in pend_c]
            pend_c = []
            rr(gens_)
            while done_q < 4 and 4 * done_q + 3 <= g_[-1]:
                pend_c.append(done_q)
                done_q += 1
        rr([c_gen(q_) for q_ in pend_c])

        st_out = st_in
        for ri, (dst_s, dst_p) in enumerate(((dr["sreo"], dr["pre"]), (dr["simo"], dr["pim"]))):
            so = st_out[ri]
            for pr in range(16):
                bank = 4 + pr // 4
                P.op("pe", lambda e, ri=ri, pr=pr, bank=bank: e.transpose(pb[bank][0:16, (pr % 4) * 128:(pr % 4 + 1) * 128], fin_s[:, ri, pr, :], ident[:]),
                     reads=["fin_s", "ident"], writes=["pb%d" % bank])
            for b4 in range(4):
                P.op("act", lambda e, so=so, b4=b4: e.copy(so[:, b4 * 512:(b4 + 1) * 512], pb[4 + b4][0:16, :]), reads=["pb%d" % (4 + b4)], writes=["st_in0"])
            P.dma("sp", dst_s, so[:], reads=["st_in0"], writes=["st_in0"])
            for gi in range(2):
                P.dma("sp", dst_p.rearrange("(pr gi) p -> gi p pr", gi=2)[gi], fin_p[gi * 64:(gi + 1) * 64, ri, :], reads=["fin_p"], allow_slow_non_contiguous=True)
        P.barrier()


def ffn_phase(nc, P, mk, sb, pb, ident, h2T, modT, tm_gate, fg_bc, pre, rs_all, junk, dr, cw, dump, stop_after=None):
    with contextlib.ExitStack() as ph:
        wdn = mk(ph, "wdn", [128, 22, D], BF16)

        def load_wdn(piece):
            h, kk = divmod(piece, 2)
            P.dma("pool", wdn[:, kk * 11:(kk + 1) * 11, h * 256:(h + 1) * 256], cw(dr["wdn"][kk * 1408:(kk + 1) * 1408, h * 256:(h + 1) * 256]), writes=["wdn"])

        NW = 2
        wug = [mk(ph, "wug%d" % i, [128, 8, 128], BF16) for i in range(NW)]
        wuv = [mk(ph, "wuv%d" % i, [128, 8, 128], BF16) for i in range(NW)]

        def load_up(i, slot):
            P.dma("pool", wug[slot][:], cw(dr["wup"][:, i * 128:(i + 1) * 128]), writes=["wug%d" % slot])
            P.dma("pool", wuv[slot][:], cw(dr["wup"][:, DFF + i * 128:DFF + (i + 1) * 128]), writes=["wuv%d" % slot])
        load_up(0, 0)
        wbase = 0
        preloaded = 1
        gf_p, gf_s, pastT, wc, bcv = pre
        cvo_s = mk(ph, "cvo_s", [128, 44, 32])
        cvo_p = mk(ph, "cvo_p", [128, 44, 2])
        carry = mk(ph, "carry", [128, 44, 2])
        carry2 = mk(ph, "carry2", [128, 44, 2])

        aT = mk(ph, "aT", [128, 22, 1152], BF16)
        Cg = [mk(ph, "Cg%d" % i, [128, 512]) for i in range(2)]
        Cv = [mk(ph, "Cv%d" % i, [128, 512]) for i in range(2)]
        Gg = [mk(ph, "Gg%d" % i, [128, 512]) for i in range(2)]
        Ux = [mk(ph, "Ux%d" % i, [128, 16, 10]) for i in range(2)]
        x1t = [mk(ph, "x1t%d" % i, [128, D]) for i in range(2)]
        yt = [mk(ph, "yt%d" % i, [128, D]) for i in range(2)]

        groups = [[0, 1], [2, 3, 4]]
        it = 0


        def up_s1(item):
            i, bi, slot, first_load, a0, n = item
            par = n % 2
            t0, tn = TBS[bi]
            if first_load is not None:
                load_up(*first_load)
                if n < 2 * 22 and 2 <= i < 10 and bi == 0:
                    load_wdn(i - 2)
            banks = (0, 1) if par == 0 else (2, 3)
            for half, wsl in ((0, wug[slot]), (1, wuv[slot])):
                wk = ("wug%d" if half == 0 else "wuv%d") % slot
                bank = banks[half]
                for k in range(8):
                    P.op("pe", lambda e, k=k, bank=bank, wsl=wsl: e.matmul(pb[bank][:, 0:tn], wsl[:, k, :], h2T[:, k, t0:t0 + tn], start=(k == 0), stop=(k == 7)),
                         reads=[wk, "B_dstT"], writes=["pb%d" % bank])
            info = []
            for half in range(2):
                c = i + 22 * half
                bank = banks[half]
                Ct = (Cg if half == 0 else Cv)[par]
                Ck = ("Cg%d" if half == 0 else "Cv%d") % par
                info.append((c, bank, "pb%d" % bank, Ct, Ck, wc[:, c, 0:1], wc[:, c, 1:2], wc[:, c, 2:3], bcv[:, c:c + 1]))
            if bi < 4:
                for (c, bank, bk, Ct, Ck, w0, w1, w2, bb) in info:
                    P.op("act", lambda e, bank=bank, Ct=Ct, w2=w2, bb=bb: e.activation(Ct[:, 0:tn], pb[bank][:, 0:tn], AF.Identity, bias=bb, scale=w2), reads=[bk, "wc", "bcv"], writes=[Ck])
                for (c, bank, bk, Ct, Ck, w0, w1, w2, bb) in info:
                    P.op("dve", lambda e, bank=bank, Ct=Ct, w1=w1: e.scalar_tensor_tensor(Ct[:, 1:tn], pb[bank][:, 0:tn - 1], w1, Ct[:, 1:tn], ALU.mult, ALU.add), reads=[bk, "wc", Ck], writes=[Ck])
                    P.op("dve", lambda e, bank=bank, Ct=Ct, w0=w0: e.scalar_tensor_tensor(Ct[:, 2:tn], pb[bank][:, 0:tn - 2], w0, Ct[:, 2:tn], ALU.mult, ALU.add), reads=[bk, "wc", Ck], writes=[Ck])
                    if bi < 3:
                        cw_ = carry if bi % 2 == 0 else carry2
                        P.op("dve", lambda e, c=c, bank=bank, cw_=cw_: e.tensor_copy(cw_[:, c, :], pb[bank][:, tn - 2:tn]), reads=[bk], writes=["carry%d_%d" % (bi % 2, c)])
                    else:
                        P.op("dve", lambda e, c=c, bank=bank: e.tensor_copy(cvo_p[:, c, :], pb[bank][:, tn - 2:tn]), reads=[bk], writes=["cvo_p"])
                if bi > 0:
                    cr_ = carry if (bi - 1) % 2 == 0 else carry2
                    for (c, bank, bk, Ct, Ck, w0, w1, w2, bb) in info:
                        ck = "carry%d_%d" % ((bi - 1) % 2, c)
                        P.op("dve", lambda e, c=c, Ct=Ct, w1=w1: e.scalar_tensor_tensor(Ct[:, 0:1], cr_[:, c, 1:2], w1, Ct[:, 0:1], ALU.mult, ALU.add), reads=[ck, "wc", Ck], writes=[Ck])
                    for (c, bank, bk, Ct, Ck, w0, w1, w2, bb) in info:
                        ck = "carry%d_%d" % ((bi - 1) % 2, c)
                        P.op("dve", lambda e, c=c, Ct=Ct, w0=w0: e.scalar_tensor_tensor(Ct[:, 0:2], cr_[:, c, 0:2], w0, Ct[:, 0:2], ALU.mult, ALU.add), reads=[ck, "wc", Ck], writes=[Ck])

            else:
                for hh, (c, bank, bk, Ct, Ck, w0, w1, w2, bb) in enumerate(info):
                    U = Ux[hh]; Uk = "Ux%d" % hh
                    C3 = Ct[:, 0:128].rearrange("p (n j) -> p n j", j=8)
                    P.op("act", lambda e, bank=bank, U=U: e.copy(U[:, :, 2:10], pb[bank][:, 0:128].rearrange("p (n j) -> p n j", j=8)), reads=[bk], writes=[Uk])
                    P.op("dve", lambda e, c=c, U=U: e.tensor_copy(U[:, :, 0:2], pastT[:, c, :].rearrange("p (n k) -> p n k", k=2)), reads=["pastT"], writes=[Uk])
                    P.op("dve", lambda e, U=U, C3=C3, w2=w2, bb=bb: e.tensor_scalar(C3, U[:, :, 2:10], w2, bb, ALU.mult, ALU.add), reads=[Uk, "wc", "bcv"], writes=[Ck])
                    P.op("dve", lambda e, U=U, C3=C3, w1=w1: e.scalar_tensor_tensor(C3, U[:, :, 1:9], w1, C3, ALU.mult, ALU.add), reads=[Uk, "wc", Ck], writes=[Ck])
                    P.op("dve", lambda e, U=U, C3=C3, w0=w0: e.scalar_tensor_tensor(C3, U[:, :, 0:8], w0, C3, ALU.mult, ALU.add), reads=[Uk, "wc", Ck], writes=[Ck])
                    P.op("act", lambda e, c=c, U=U: e.copy(cvo_s[:, c, :].rearrange("p (n k) -> p n k", k=2), U[:, :, 8:10]), reads=[Uk], writes=["cvo_s"])

        def up_s2(item):
            i, bi, slot, first_load, a0, n = item
            par = n % 2
            t0, tn = TBS[bi]
            G = Gg[par]; Gk = "Gg%d" % par
            Cgt = Cg[par]; Cvt = Cv[par]
            P.op("act", lambda e: e.activation(G[:, 0:tn], Cgt[:, 0:tn], AF.Gelu_apprx_tanh), reads=["Cg%d" % par], writes=[Gk])
            P.op("pool", lambda e: e.tensor_tensor(aT[:, i, t0 - a0:t0 - a0 + tn], G[:, 0:tn], Cvt[:, 0:tn], ALU.mult),
                 reads=[Gk, "Cv%d" % par], writes=["aT"])

        nblk = 0
        ntile = 0
        if stop_after == "F0":
            return
        for gidx, grp in enumerate(groups):
            a0 = TBS[grp[0]][0]
            items = []
            for i in range(22):
                slot = (wbase + i) % NW
                fl = None
                if i + 1 < 22 and i + 1 >= preloaded:
                    fl = (i + 1, (wbase + i + 1) % NW)
                for bi in grp:
                    items.append((i, bi, slot, fl, a0, nblk))
                    fl = None
                    nblk += 1
            LAG = 1
            for step in range(len(items) + LAG):
                if step < len(items):
                    up_s1(items[step])
                if step >= LAG:
                    up_s2(items[step - LAG])
            if stop_after == "F1":
                return
            if gidx + 1 < len(groups):
                wbase = (wbase + 22) % NW
                load_up(0, wbase)
                load_up(1, (wbase + 1) % NW)
                preloaded = 2
            else:
                for k in range(2):
                    P.dma("sp", dr["pcv"][k].rearrange("(c p) -> p c", p=128), cvo_p[:, :, k], reads=["cvo_p"])
            for bi in grp:
                t0, tn = TBS[bi]
                for tt_ in range(tn // 128):
                    t = t0 // 128 + tt_
                    loc = t0 - a0 + tt_ * 128
                    par = ntile % 2
                    ntile += 1
                    x1 = x1t[par]; x1k = "x1t%d" % par
                    P.dma("sp", x1[:], dr["x1"][t * 128:(t + 1) * 128, :], reads=["x1_d%d" % t], writes=[x1k])
                    dbanks = (4, 5) if par == 0 else (6, 7)
                    for h in range(2):
                        bank = dbanks[h]
                        for kc in range(22):
                            P.op("pe", lambda e, kc=kc, h=h, loc=loc, bank=bank: e.matmul(pb[bank][:], aT[:, kc, loc:loc + 128], wdn[:, kc, h * 512:(h + 1) * 512], start=(kc == 0), stop=(kc == 21)),
                                 reads=["aT", "wdn"], writes=["pb%d" % bank])
                    y = yt[par]; yk = "yt%d" % par
                    g = gf_p if t < 16 else gf_s
                    for h in range(2):
                        P.op("dve", lambda e, h=h, y=y, g=g, dbanks=dbanks: e.tensor_tensor(y[:, h * 512:(h + 1) * 512], pb[dbanks[h]][:], g[:, h * 512:(h + 1) * 512], ALU.mult),
                             reads=["pb%d" % dbanks[h], "gf_p", "gf_s"], writes=[yk])
                    P.op("pool", lambda e, y=y, x1=x1: e.tensor_tensor(y[:], y[:], x1[:], ALU.add), reads=[yk, x1k], writes=[yk])
                    ss = rs_all[:, 2 * NT + t: 2 * NT + t + 1]
                    ssk = "F_ss%d" % t
                    P.op("act", lambda e, y=y, ss=ss: e.activation(junk[:], y[:], AF.Square, accum_out=ss), reads=[yk], writes=["junk", ssk])
                    P.op("dve", lambda e, ss=ss: e.tensor_scalar(ss, ss, 1.0 / D, EPS, ALU.mult, ALU.add), reads=[ssk], writes=[ssk])
                    P.op("act", lambda e, ss=ss: e.activation(ss, ss, AF.Sqrt), reads=[ssk], writes=[ssk])
                    P.op("dve", lambda e, ss=ss: e.reciprocal(ss, ss), reads=[ssk], writes=[ssk])
                    P.op("dve", lambda e, y=y, ss=ss: e.scalar_tensor_tensor(y[:], y[:], ss, fg_bc[:], ALU.mult, ALU.mult), reads=[yk, ssk, "fg_bc"], writes=[yk])
                    P.dma("sp", dr["y"][t * 128:(t + 1) * 128, :], y[:], reads=[yk])

        so = [mk(ph, "cv_so%d" % i, [32, 512]) for i in range(2)]
        for c in range(44):
            bank = 4 + c // 4 % 4
            P.op("pe", lambda e, c=c, bank=bank: e.transpose(pb[bank][0:32, (c % 4) * 128:(c % 4 + 1) * 128], cvo_s[:, c, :], ident[:]),
                 reads=["cvo_s", "ident"], writes=["pb%d" % bank])
            if c % 4 == 3:
                c0 = c - 3
                sx = so[(c // 4) % 2]; sk = "cv_so%d" % ((c // 4) % 2)
                P.op("act", lambda e, sx=sx, bank=bank: e.copy(sx[:], pb[bank][0:32, :]), reads=["pb%d" % bank], writes=[sk])
                P.dma("sp", dr["scvo"][:, c0 * 128:(c0 + 4) * 128], sx[:], reads=[sk])


_NC_CACHE = {}


def _get_nc():
    if "nc" not in _NC_CACHE:
        _NC_CACHE["nc"] = build_program()
    return _NC_CACHE["nc"]


def make_in_maps(inputs):
    f = lambda a: np.ascontiguousarray(np.asarray(a, dtype=np.float32))
    shared = {}
    for k in ("norm1_g", "norm2_g", "w_ada", "b_ada", "w_in", "s5_a_re", "s5_a_im", "s5_log_dt", "s5_b_re", "s5_b_im",
              "s5_c_re", "s5_c_im", "s5_d", "w_s5_glu", "b_s5_glu", "gm_ln_g", "gm_ln_b", "gm_w_sp", "gm_b_sp",
              "w_gm_out", "w_out", "w_up", "w_conv", "b_conv", "w_down"):
        shared[k] = f(np.asarray(inputs[k])[0])
    shared["final_g"] = f(inputs["final_g"])
    xp = np.asarray(inputs["x_prompt"]); xs = np.asarray(inputs["x_sample"])
    cp = np.asarray(inputs["c_prompt"]); cs = np.asarray(inputs["c_sample"])
    sre = np.asarray(inputs["state_ssm_re"])[0]; sim = np.asarray(inputs["state_ssm_im"])[0]
    scv = np.asarray(inputs["state_ffn_conv"])[0]
    maps = []
    for i in range(NCORES):
        sl = slice(16 * i, 16 * i + 16)
        m = dict(shared)
        m["x"] = f(np.concatenate([xp[i], xs[sl].reshape(128, D)], axis=0))
        m["c"] = f(np.concatenate([cp[i:i + 1], cs[sl]], axis=0))
        m["sre"] = f(sre[sl].reshape(16, 2048))
        m["sim"] = f(sim[sl].reshape(16, 2048))
        m["scv"] = f(scv[sl].reshape(32, 2 * DFF))
        maps.append(m)
    return maps


def kernel(**inputs):
    nc = _get_nc()
    maps = make_in_maps(inputs)
    res = run_bass_kernel_spmd(nc, maps, core_ids=list(range(NCORES)))
    r = res.results
    y_p = np.stack([r[i]["y"][:2048] for i in range(NCORES)], axis=0)
    y_s = np.concatenate([r[i]["y"][2048:].reshape(16, 8, D) for i in range(NCORES)], axis=0)
    p_re = np.stack([r[i]["p_re"] for i in range(NCORES)], axis=0)[None]
    p_im = np.stack([r[i]["p_im"] for i in range(NCORES)], axis=0)[None]
    p_cv = np.stack([r[i]["p_conv"] for i in range(NCORES)], axis=0)[None]
    s_re = np.concatenate([r[i]["s_re"].reshape(16, 32, 64) for i in range(NCORES)], axis=0)[None]
    s_im = np.concatenate([r[i]["s_im"].reshape(16, 32, 64) for i in range(NCORES)], axis=0)[None]
    s_cv = np.concatenate([r[i]["s_conv"].reshape(16, 2, 2 * DFF) for i in range(NCORES)], axis=0)[None]
    s_v = np.concatenate([r[i]["s_v"].reshape(16, 8, 512) for i in range(NCORES)], axis=0)[None]
    outs = (y_p, y_s, p_re, p_im, p_cv, s_re, s_im, s_cv, s_v)
    return tuple(np.ascontiguousarray(o, dtype=np.float32) for o in outs)
```
